# Optimizing a Trainium2 kernel written in Bass

```python
import math
import jax, jax.numpy as jnp
from jax import lax
import numpy as np


D_MODEL = 1024
BATCH = 4
SEQ = 4096
DEPTH = 4
DEC_BATCH = 128
DEC_SEQ = 1
PAST_LEN = 2048
PAGE_SIZE = 128

N_A_LAYERS = DEPTH // 2
N_B_LAYERS = DEPTH - N_A_LAYERS
CHUNK = 128
GATE_DIM = 2 * D_MODEL
N_GROUPS = 8
GROUP_DIM = GATE_DIM // N_GROUPS
N_HEADS = 8
HEAD_DIM = D_MODEL // (2 * N_HEADS)
V_DIM = 2 * HEAD_DIM
ATTN_BLOCK = 128
MASK_VALUE = -1e9
PEER_HEADS = 8
N_KEYS = 128
N_EXPERTS = N_KEYS * N_KEYS
PEER_QDIM = 256
PEER_TOPK = 16
PEER_BLOCK = 128
EPS = 1e-6

kernel_name = 'yoco_gmlp_diffattn_peer_step'


def rmsnorm(x, g):
    xf = x.astype(jnp.float32)
    y = xf * lax.rsqrt(jnp.mean(xf * xf, axis=-1, keepdims=True) + EPS)
    return (y * g.astype(jnp.float32)).astype(x.dtype)


def alibi_slopes():
    return jnp.asarray(2.0 ** (-8.0 * np.arange(1, N_HEADS + 1, dtype=np.float32) / N_HEADS), jnp.float32)


def gmlp_project(h, w_in, vnorm_g):
    z = jax.nn.gelu(h @ w_in)
    u, v = jnp.split(z, 2, axis=-1)
    return u, rmsnorm(v, vnorm_g)


def gmlp_mix(u, v, w_s, b_s, w_out):
    B, S, _ = v.shape
    L = min(S, CHUNK)
    nc = S // L
    causal = jnp.tril(jnp.ones((L, L), dtype=bool))
    w = jnp.where(causal, w_s[:, :L, :L], 0.0)
    vc = v.reshape(B, nc, L, N_GROUPS, GROUP_DIM)
    mixed = jnp.einsum('gts,bcsgd->bctgd', w, vc) + b_s[:, :L].T[:, :, None]
    gated = u * mixed.reshape(B, S, GATE_DIM)
    return gated @ w_out


def shared_kv(x, kv_norm_g, w_k, w_v, k_norm_g):
    B, T, _ = x.shape
    h = rmsnorm(x, kv_norm_g)
    k = rmsnorm((h @ w_k).reshape(B, T, 2 * N_HEADS, HEAD_DIM), k_norm_g)
    v = (h @ w_v).reshape(B, T, N_HEADS, V_DIM)
    return k, v


def diff_attn_core(q, k, v, q_pos, k_pos, lam, slopes):
    s = jnp.einsum('bqmhd,bkmhd->bmhqk', q, k, preferred_element_type=jnp.float32) * (HEAD_DIM ** -0.5)
    dist = (q_pos[:, None] - k_pos[None, :]).astype(jnp.float32)
    bias = jnp.where(dist >= 0, -slopes[:, None, None] * dist, MASK_VALUE)
    p = jax.nn.softmax(s + bias, axis=-1)
    a = p[:, 0] - lam * p[:, 1]
    return jnp.einsum('bhqk,bkhd->bqhd', a.astype(v.dtype), v)


def diff_attn_layer(x, k, v, q_pos, k_pos, norm_g, w_q, q_norm_g, lam, lam_init, subln_g, w_o):
    B, T, _ = x.shape
    h = rmsnorm(x, norm_g)
    q = rmsnorm((h @ w_q).reshape(B, T, 2 * N_HEADS, HEAD_DIM), q_norm_g).reshape(B, T, 2, N_HEADS, HEAD_DIM)
    kk = k.reshape(k.shape[0], k.shape[1], 2, N_HEADS, HEAD_DIM)
    slopes = alibi_slopes()
    if T > ATTN_BLOCK and T % ATTN_BLOCK == 0:
        nb = T // ATTN_BLOCK
        qb = q.reshape(B, nb, ATTN_BLOCK, 2, N_HEADS, HEAD_DIM).swapaxes(0, 1)
        pb = q_pos.reshape(nb, ATTN_BLOCK)
        o = lax.map(lambda a: diff_attn_core(a[0], kk, v, a[1], k_pos, lam, slopes), (qb, pb))
        o = o.swapaxes(0, 1).reshape(B, T, N_HEADS, V_DIM)
    else:
        o = diff_attn_core(q, kk, v, q_pos, k_pos, lam, slopes)
    o = rmsnorm(o, subln_g) * (1.0 - lam_init)
    return o.reshape(B, T, N_HEADS * V_DIM) @ w_o


def peer_block(h, w_q, keys, u_tab, v_tab):
    T = h.shape[0]
    q = (h @ w_q).reshape(T, PEER_HEADS, 2, PEER_QDIM // 2)
    s = jnp.einsum('thcd,hcnd->thcn', q, keys, preferred_element_type=jnp.float32)
    sv, si = lax.top_k(s, PEER_TOPK)
    cand = sv[:, :, 0, :, None] + sv[:, :, 1, None, :]
    cidx = si[:, :, 0, :, None] * N_KEYS + si[:, :, 1, None, :]
    top_s, pos = lax.top_k(cand.reshape(T, PEER_HEADS, PEER_TOPK * PEER_TOPK), PEER_TOPK)
    eidx = jnp.take_along_axis(cidx.reshape(T, PEER_HEADS, PEER_TOPK * PEER_TOPK), pos, axis=-1)
    g = jax.nn.softmax(top_s, axis=-1)
    u = u_tab[eidx]
    vv = v_tab[eidx]
    act = jax.nn.gelu(jnp.einsum('thkd,td->thk', u, h, preferred_element_type=jnp.float32))
    return jnp.einsum('thk,thkd->td', (g * act).astype(h.dtype), vv)


def peer_ffn(h, w_q, keys, u_tab, v_tab):
    B, T, D = h.shape
    n = B * T
    nb = -(-n // PEER_BLOCK)
    flat = jnp.pad(h.reshape(n, D), ((0, nb * PEER_BLOCK - n), (0, 0)))
    out = lax.map(lambda blk: peer_block(blk, w_q, keys, u_tab, v_tab), flat.reshape(nb, PEER_BLOCK, D))
    return out.reshape(nb * PEER_BLOCK, D)[:n].reshape(B, T, D)


def setup_inputs(seed: int = 0) -> dict:
    key = jax.random.key(seed)
    ks = jax.random.split(key, 32)

    def nrm(k, shape, scale):
        return jax.random.normal(k, shape, jnp.float32) * scale

    def gain(k, shape):
        return 1.0 + 0.02 * jax.random.normal(k, shape, jnp.float32)

    n_pages = PAST_LEN // PAGE_SIZE
    n_used = DEC_BATCH * n_pages
    n_pool = n_used + max(1, n_used // 4)
    page_table = jax.random.permutation(ks[0], n_pool)[:n_used].reshape(DEC_BATCH, n_pages).astype(jnp.int32)
    return {
        'x_prompt': nrm(ks[1], (BATCH, SEQ, D_MODEL), 1.0),
        'x_sample': nrm(ks[2], (DEC_BATCH, DEC_SEQ, D_MODEL), 1.0),
        'cache_k': nrm(ks[3], (n_pool, PAGE_SIZE, 2 * N_HEADS, HEAD_DIM), 1.0),
        'cache_v': nrm(ks[4], (n_pool, PAGE_SIZE, N_HEADS, V_DIM), 1.0),
        'page_table': page_table,
        'a_norm_g': gain(ks[5], (N_A_LAYERS, D_MODEL)),
        'a_w_in': nrm(ks[6], (N_A_LAYERS, D_MODEL, 2 * GATE_DIM), D_MODEL ** -0.5),
        'a_vnorm_g': gain(ks[7], (N_A_LAYERS, GATE_DIM)),
        'a_w_s': nrm(ks[8], (N_A_LAYERS, N_GROUPS, CHUNK, CHUNK), CHUNK ** -0.5),
        'a_b_s': gain(ks[9], (N_A_LAYERS, N_GROUPS, CHUNK)),
        'a_w_out': nrm(ks[10], (N_A_LAYERS, GATE_DIM, D_MODEL), GATE_DIM ** -0.5),
        'kv_norm_g': gain(ks[11], (D_MODEL,)),
        'w_k': nrm(ks[12], (D_MODEL, 2 * N_HEADS * HEAD_DIM), D_MODEL ** -0.5),
        'w_v': nrm(ks[13], (D_MODEL, N_HEADS * V_DIM), D_MODEL ** -0.5),
        'k_norm_g': gain(ks[14], (HEAD_DIM,)),
        'b_norm_g': gain(ks[15], (N_B_LAYERS, D_MODEL)),
        'w_q': nrm(ks[16], (N_B_LAYERS, D_MODEL, 2 * N_HEADS * HEAD_DIM), D_MODEL ** -0.5),
        'q_norm_g': gain(ks[17], (N_B_LAYERS, HEAD_DIM)),
        'lambda_q1': nrm(ks[18], (N_B_LAYERS, HEAD_DIM), 0.1),
        'lambda_k1': nrm(ks[19], (N_B_LAYERS, HEAD_DIM), 0.1),
        'lambda_q2': nrm(ks[20], (N_B_LAYERS, HEAD_DIM), 0.1),
        'lambda_k2': nrm(ks[21], (N_B_LAYERS, HEAD_DIM), 0.1),
        'subln_g': gain(ks[22], (N_B_LAYERS, V_DIM)),
        'w_o': nrm(ks[23], (N_B_LAYERS, N_HEADS * V_DIM, D_MODEL), (N_HEADS * V_DIM) ** -0.5),
        'f_norm_g': gain(ks[24], (DEPTH, D_MODEL)),
        'peer_w_q': nrm(ks[25], (DEPTH, D_MODEL, PEER_HEADS * PEER_QDIM), D_MODEL ** -0.5),
        'peer_keys': nrm(ks[26], (DEPTH, PEER_HEADS, 2, N_KEYS, PEER_QDIM // 2), (PEER_QDIM // 2) ** -0.5),
        'peer_u': nrm(ks[27], (DEPTH, N_EXPERTS, D_MODEL), D_MODEL ** -0.5),
        'peer_v': nrm(ks[28], (DEPTH, N_EXPERTS, D_MODEL), (PEER_HEADS * PEER_TOPK) ** -0.5),
    }


def reference(x_prompt, x_sample, cache_k, cache_v, page_table,
              a_norm_g, a_w_in, a_vnorm_g, a_w_s, a_b_s, a_w_out,
              kv_norm_g, w_k, w_v, k_norm_g,
              b_norm_g, w_q, q_norm_g, lambda_q1, lambda_k1, lambda_q2, lambda_k2, subln_g, w_o,
              f_norm_g, peer_w_q, peer_keys, peer_u, peer_v):
    xp, xs = x_prompt, x_sample
    seq = xp.shape[1]
    dec_b, n_pages = page_table.shape
    past_len = n_pages * cache_k.shape[1]
    dec_seq = xs.shape[1]
    gmlp_v_rows = []
    for l in range(DEPTH):
        if l < N_A_LAYERS:
            i = l
            up, vp = gmlp_project(rmsnorm(xp, a_norm_g[i]), a_w_in[i], a_vnorm_g[i])
            xp = xp + gmlp_mix(up, vp, a_w_s[i], a_b_s[i], a_w_out[i])
            us, vs = gmlp_project(rmsnorm(xs, a_norm_g[i]), a_w_in[i], a_vnorm_g[i])
            xs = xs + gmlp_mix(us, vs, a_w_s[i], a_b_s[i], a_w_out[i])
            gmlp_v_rows.append(vs)
        else:
            if l == N_A_LAYERS:
                new_k_prompt, new_v_prompt = shared_kv(xp, kv_norm_g, w_k, w_v, k_norm_g)
                new_k_sample, new_v_sample = shared_kv(xs, kv_norm_g, w_k, w_v, k_norm_g)
                past_k = cache_k[page_table].reshape(dec_b, past_len, 2 * N_HEADS, HEAD_DIM)
                past_v = cache_v[page_table].reshape(dec_b, past_len, N_HEADS, V_DIM)
                k_s = jnp.concatenate([past_k, new_k_sample.astype(past_k.dtype)], axis=1)
                v_s = jnp.concatenate([past_v, new_v_sample.astype(past_v.dtype)], axis=1)
                pos_p = jnp.arange(seq, dtype=jnp.int32)
                qpos_s = past_len + jnp.arange(dec_seq, dtype=jnp.int32)
                kpos_s = jnp.arange(past_len + dec_seq, dtype=jnp.int32)
            i = l - N_A_LAYERS
            lam_init = 0.8 - 0.6 * math.exp(-0.3 * l)
            lam = (jnp.exp(jnp.sum(lambda_q1[i].astype(jnp.float32) * lambda_k1[i].astype(jnp.float32)))
                   - jnp.exp(jnp.sum(lambda_q2[i].astype(jnp.float32) * lambda_k2[i].astype(jnp.float32)))
                   + lam_init)
            xp = xp + diff_attn_layer(xp, new_k_prompt, new_v_prompt, pos_p, pos_p, b_norm_g[i], w_q[i],
                                      q_norm_g[i], lam, lam_init, subln_g[i], w_o[i])
            xs = xs + diff_attn_layer(xs, k_s, v_s, qpos_s, kpos_s, b_norm_g[i], w_q[i],
                                      q_norm_g[i], lam, lam_init, subln_g[i], w_o[i])
        xp = xp + peer_ffn(rmsnorm(xp, f_norm_g[l]), peer_w_q[l], peer_keys[l], peer_u[l], peer_v[l])
        xs = xs + peer_ffn(rmsnorm(xs, f_norm_g[l]), peer_w_q[l], peer_keys[l], peer_u[l], peer_v[l])
    state_gmlp_v = jnp.stack(gmlp_v_rows, axis=0)
    y_prompt = xp
    y_sample = xs
    return (y_prompt, y_sample, new_k_prompt, new_v_prompt, new_k_sample, new_v_sample, state_gmlp_v)
```

```python
import math
import numpy as np
import concourse.bass as bass
import concourse.mybir as mybir
from concourse.bass_utils import run_bass_kernel_spmd

F32 = mybir.dt.float32
BF16 = mybir.dt.bfloat16
I32 = mybir.dt.int32
U32 = mybir.dt.uint32
ALU = mybir.AluOpType
AF = mybir.ActivationFunctionType
AX = mybir.AxisListType

SEM_ROT = 30000
import os as _os
FORCE_SPECIAL = bool(int(_os.environ.get('FORCE_SPECIAL', '0')))
N_DSEM = 72
EPS = 1e-6
NTILE = 17
NALL = 33
NPT = 16
DEPTH = 4
N_EXP = 16384


class Res:
    __slots__ = ("name", "t", "w", "r", "dslot")

    def __init__(self, name, t=None):
        self.name = name
        self.t = t
        self.w = None
        self.r = []
        self.dslot = None


class Prog:
    def __init__(self, nc):
        self.nc = nc
        self.eng = {"pe": nc.tensor, "dve": nc.vector, "act": nc.scalar, "pool": nc.gpsimd, "sync": nc.sync}
        self.sem = {}
        self.cnt = {}
        self.nsem = 0
        self._keep = []
        self._scopes = []
        for k in ("pe", "dve", "act", "pool"):
            self._new_eng_sem(k)
        self.dfree = [[self._alloc_sem(f"d{i}"), 0] for i in range(N_DSEM)]
        self.dall = list(self.dfree)
        self.seen = {k: {} for k in self.eng}
        self.out_events = []
        self.n_inst = 0

    def _alloc_sem(self, name):
        g = self.nc.semaphore(name)
        s = g.__enter__()
        self.nsem += 1
        return s

    def _new_eng_sem(self, k):
        self.sem[k] = self._alloc_sem(f"s_{k}_{self.nsem}")
        self.cnt[k] = 0

    def _reg(self, res, g):
        if self._scopes:
            self._scopes[-1].append((res, g))
        else:
            self._keep.append((res, g))
        return res

    def sb(self, name, shape, dtype):
        self._uid = getattr(self, "_uid", 0) + 1
        name = f"{name}_{self._uid}"
        g = self.nc.sbuf_tensor(name, list(shape), dtype)
        return self._reg(Res(name, g.__enter__()), g)

    def ps(self, name, shape, dtype=F32):
        g = self.nc.psum_tensor(name, list(shape), dtype)
        return self._reg(Res(name, g.__enter__()), g)

    def dram(self, name, shape, dtype):
        t = self.nc.dram_tensor(name, list(shape), dtype, kind="Internal")
        return Res(name, t.ap())

    def view(self, name, ap):
        return Res(name, ap)

    def scope(self):
        prog = self

        class _S:
            def __enter__(s):
                prog._scopes.append([])

            def __exit__(s, *a):
                if a[0] is not None:
                    return False
                prog.barrier()
                items = prog._scopes.pop()
                for res, g in reversed(items):
                    if res.dslot is not None:
                        prog.dfree.append(res.dslot)
                        res.dslot = None
                    g.__exit__(None, None, None)
                return False
        return _S()

    def barrier(self):
        evs = []
        for k in ("pe", "dve", "act", "pool"):
            if self.cnt[k] > 0:
                evs.append((self.sem[k], self.cnt[k]))
        for sl in self.dall:
            if sl[1] > 0:
                evs.append((sl[0], 16 * sl[1]))
        for q in ("pe", "dve", "act", "pool", "sync"):
            for ev in evs:
                self._wait(q, ev)

    def _wait(self, k, ev):
        if ev is None:
            return
        sem, val = ev
        sid = id(sem)
        if self.seen[k].get(sid, 0) >= val:
            return
        self.eng[k].wait_ge(sem, val)
        self.seen[k][sid] = val
        self.n_inst += 1

    @staticmethod
    def _compact(evs):
        best = {}
        for sem, val in evs:
            sid = id(sem)
            if sid not in best or best[sid][1] < val:
                best[sid] = (sem, val)
        return list(best.values())

    def _commit(self, ev, reads, writes):
        for r in reads:
            r.r.append(ev)
            if len(r.r) > 16:
                r.r = self._compact(r.r)
        for w in writes:
            w.w = ev
            w.r = []

    def op(self, k, fn, reads=(), writes=()):
        if self.cnt[k] >= SEM_ROT:
            self._new_eng_sem(k)
        if k == "pe":
            self.seen[k][id(self.sem[k])] = 1 << 60
        for r in reads:
            self._wait(k, r.w)
        for w in writes:
            self._wait(k, w.w)
            for ev in w.r:
                self._wait(k, ev)
        inst = fn(self.eng[k])
        self.cnt[k] += 1
        inst.then_inc(self.sem[k], 1)
        ev = (self.sem[k], self.cnt[k])
        self._commit(ev, reads, writes)
        self.n_inst += 1
        return ev

    def _dma_event(self, prim):
        if prim.dslot is None:
            prim.dslot = self.dfree.pop(0)
        prim.dslot[1] += 1
        return (prim.dslot[0], 16 * prim.dslot[1])

    def _dma_deps(self, q, reads, writes):
        for r in reads:
            self._wait(q, r.w)
        for w in writes:
            if w.w is not None and not (w.dslot is not None and w.w[0] is w.dslot[0]):
                self._wait(q, w.w)
            for ev in w.r:
                self._wait(q, ev)

    def dma(self, q, out_ap, in_ap, reads=(), writes=(), out=False, prim=None, **kw):
        self._dma_deps(q, reads, writes)
        if prim is None:
            prim = (list(writes) + list(reads))[0]
        ev = self._dma_event(prim)
        kw.setdefault("allow_slow_non_contiguous", True)
        self.eng[q].dma_start(out=out_ap, in_=in_ap, **kw).then_inc(ev[0], 16)
        for r in reads:
            r.r.append(ev)
        for w in writes:
            w.w = ev
            w.r = []
        if out:
            self.out_events.append(ev)
        self.n_inst += 1
        return ev

    def gather(self, out_ap, table_ap, idx_ap, reads=(), writes=(), prim=None):
        q = "pool"
        self._dma_deps(q, reads, writes)
        if prim is None:
            prim = list(writes)[0]
        ev = self._dma_event(prim)
        self.nc.gpsimd.indirect_dma_start(
            out=out_ap, out_offset=None, in_=table_ap,
            in_offset=bass.IndirectOffsetOnAxis(ap=idx_ap, axis=0),
        ).then_inc(ev[0], 16)
        for r in reads:
            r.r.append(ev)
        for w in writes:
            w.w = ev
            w.r = []
        self.n_inst += 1
        return ev

    def collective(self, fn, reads, writes):
        q = "pool"
        self._dma_deps(q, reads, writes)
        ev = self._dma_event(list(writes)[0])
        fn(self.nc.gpsimd).then_inc(ev[0], 16)
        for r in reads:
            r.r.append(ev)
        for w in writes:
            w.w = ev
            w.r = []
        return ev

    def finish(self):
        for ev in self._compact(self.out_events):
            self._wait("sync", ev)
        self.barrier()


def _lam_init(l):
    return 0.8 - 0.6 * math.exp(-0.3 * l)


def build_program(stages=("A", "KV", "B"), n_tiles_peer=NALL, cache_rows=2560 * 128, dbg_skip_sample=False, dbg_qtiles=NPT, dbg_att=9):
    nc = bass.Bass("TRN2", target_bir_lowering=False)
    P = Prog(nc)
    stages = list(stages)

    def DI(name, shape, dt=F32):
        return nc.dram_tensor(name, list(shape), dt, kind="ExternalInput").ap()

    def DO(name, shape, dt=F32):
        return nc.dram_tensor(name, list(shape), dt, kind="ExternalOutput").ap()

    xin = DI("xin", [NALL * 128, 1024])
    cache_k = DI("cache_k", [cache_rows, 1024])
    cache_v = DI("cache_v", [cache_rows, 1024])
    pt = DI("pt", [1, 256], I32)
    a_norm_g = DI("a_norm_g", [2, 1024])
    a_w_in = DI("a_w_in", [2, 1024, 4096])
    a_vnorm_g = DI("a_vnorm_g", [2, 2048])
    a_w_s = DI("a_w_s", [2, 8, 128, 128])
    a_b_s = DI("a_b_s", [2, 8, 128])
    a_w_out = DI("a_w_out", [2, 2048, 1024])
    kv_norm_g = DI("kv_norm_g", [1, 1024])
    w_k = DI("w_k", [1024, 1024])
    w_v = DI("w_v", [1024, 1024])
    k_norm_g = DI("k_norm_g", [1, 64])
    b_norm_g = DI("b_norm_g", [2, 1024])
    w_q = DI("w_q", [2, 1024, 1024])
    q_norm_g = DI("q_norm_g", [2, 64])
    lam_in = DI("lam_in", [2, 256])
    subln_g = DI("subln_g", [2, 128])
    w_o = DI("w_o", [2, 1024, 1024])
    f_norm_g = DI("f_norm_g", [4, 1024])
    peer_w_q = DI("peer_w_q", [4, 1024, 2048])
    peer_keys = DI("peer_keys", [4, 16, 128, 128])
    peer_u = DI("peer_u", [4 * N_EXP, 1024])
    peer_v = DI("peer_v", [4 * N_EXP, 1024])
    maskadd = DI("maskadd", [128, 2, 128])
    hfcol_in = DI("hfcol", [128, 1])

    y = DO("y", [NTILE * 128, 1024])
    nkp = DO("nkp", [NPT * 128, 1024])
    nvp = DO("nvp", [NPT * 128, 1024])
    nks = DO("nks", [16, 1024])
    nvs = DO("nvs", [16, 1024])
    gvo = DO("gv", [2, 16, 2048])

    kvx_all = P.dram("kvx_all", [2 * NPT, 128, 2056], BF16)
    xp = P.dram("xp", [NPT * 128, 1024], F32)
    XPR = [P.view(f"xp{k}", xp.t[k * 128:(k + 1) * 128, :]) for k in range(NPT)]
    dq = P.dram("dq", [16, 1024], F32)
    dD = P.dram("dD", [16, 16, 129], F32)

    X = [P.sb(f"x{j}", [128, 1024], F32) for j in range(NTILE)]
    ident_f = P.sb("ident_f", [128, 128], F32)
    ident_b = P.sb("ident_b", [128, 128], BF16)
    ss = P.sb("ss", [128, 64], F32)
    iota16 = P.sb("iota16", [128, 16], F32)
    pcol = P.sb("pcol", [128, 1], F32)
    BK = [P.ps(f"bank{b}", [128, 512], F32) for b in range(8)]
    bank_rr = [0]

    def nextbank(lo=0, hi=6):
        b = lo + bank_rr[0] % (hi - lo)
        bank_rr[0] += 1
        return BK[b]

    XS = [P.sb("xs0", [128, 1024], F32)]
    XSB = [XS]
    for j in range(NTILE):
        P.dma("sync", X[j].t[:], xin[j * 128:(j + 1) * 128, :], writes=[X[j]])
    for k in range(NPT):
        P.dma("sync", XPR[k].t, xin[(NTILE + k) * 128:(NTILE + k + 1) * 128, :], writes=[XPR[k]])

    def x_begin(j, buf=0):
        if j < NTILE:
            return X[j]
        r = XSB[0][buf]
        P.dma("sync", r.t[:], XPR[j - NTILE].t, reads=[XPR[j - NTILE]], writes=[r])
        return r

    def x_end(j, r):
        if j >= NTILE:
            P.dma("sync", XPR[j - NTILE].t, r.t[:], reads=[r], writes=[XPR[j - NTILE]])
    P.op("pool", lambda e: e.memset(ident_f.t[:], 1.0), writes=[ident_f])
    P.op("pool", lambda e: e.affine_select(out=ident_f.t[:], in_=ident_f.t[:], pattern=[[-1, 128]],
                                            compare_op=ALU.is_equal, fill=0.0, base=0, channel_multiplier=1),
         reads=[ident_f], writes=[ident_f])
    P.op("dve", lambda e: e.tensor_copy(ident_b.t[:], ident_f.t[:]), reads=[ident_f], writes=[ident_b])
    P.op("pool", lambda e: e.iota(iota16.t[:], [[1, 16]], base=0, channel_multiplier=0,
                                   allow_small_or_imprecise_dtypes=True), writes=[iota16])
    P.op("pool", lambda e: e.iota(pcol.t[:], [[0, 1]], base=0, channel_multiplier=1,
                                   allow_small_or_imprecise_dtypes=True), writes=[pcol])

    def rows(j):
        return 16 if j == 16 else 128

    def rstd_of(src_res, src_ap, junk_res, junk_ap, D, col):
        P.op("dve", lambda e: e.scalar_tensor_tensor(out=junk_ap, in0=src_ap, scalar=1.0, in1=src_ap,
                                                     op0=ALU.mult, op1=ALU.mult, accum_out=ss.t[:, col:col + 1]),
             reads=[src_res], writes=[junk_res, ss])
        P.op("dve", lambda e: e.tensor_scalar(out=ss.t[:, col + 1:col + 2], in0=ss.t[:, col:col + 1],
                                              scalar1=1.0 / D, scalar2=EPS, op0=ALU.mult, op1=ALU.add),
             reads=[ss], writes=[ss])
        P.op("act", lambda e: e.activation(ss.t[:, col + 2:col + 3], ss.t[:, col + 1:col + 2], AF.Sqrt),
             reads=[ss], writes=[ss])
        P.op("dve", lambda e: e.reciprocal(ss.t[:, col + 3:col + 4], ss.t[:, col + 2:col + 3]),
             reads=[ss], writes=[ss])
        return ss.t[:, col + 3:col + 4]

    def group_rstd(src_res, src3, sq_res, sq3, n, gd, st_res, base):
        P.op("dve", lambda e: e.tensor_tensor(out=sq3, in0=src3, in1=src3, op=ALU.mult),
             reads=[src_res], writes=[sq_res])
        a = st_res.t[:, base:base + n]
        b = st_res.t[:, base + n:base + 2 * n]
        c = st_res.t[:, base + 2 * n:base + 3 * n]
        P.op("dve", lambda e: e.tensor_reduce(out=a, in_=sq3, axis=AX.X, op=ALU.add), reads=[sq_res], writes=[st_res])
        P.op("dve", lambda e: e.tensor_scalar(out=a, in0=a, scalar1=1.0 / gd, scalar2=EPS, op0=ALU.mult, op1=ALU.add),
             reads=[st_res], writes=[st_res])
        P.op("act", lambda e: e.activation(b, a, AF.Sqrt), reads=[st_res], writes=[st_res])
        P.op("dve", lambda e: e.reciprocal(c, b), reads=[st_res], writes=[st_res])
        return c

    def transposes(dst_res, dst3, src_res, src_fn, n, ident=None, evac="act"):
        ident = ident or ident_b
        for g0 in range(0, n, 8):
            bk = nextbank()
            bv = bk.t[:].bitcast(BF16)
            cnt = min(8, n - g0)
            for i in range(cnt):
                P.op("pe", lambda e, i=i: e.transpose(bv[:, i * 128:(i + 1) * 128], src_fn(g0 + i), ident.t[:]),
                     reads=[src_res, ident], writes=[bk])
            P.op(evac, (lambda e: e.activation(dst3[:, g0:g0 + cnt, :], bv[:, 0:cnt * 128].rearrange("p (a b) -> p a b", b=128), AF.Copy))
                 if evac == "act" else
                 (lambda e: e.tensor_copy(dst3[:, g0:g0 + cnt, :], bv[:, 0:cnt * 128].rearrange("p (a b) -> p a b", b=128))),
                 reads=[bk], writes=[dst_res])

    def load_w_bf16(dst_res, src2d, kc_n):
        v = src2d.rearrange("(kc p) n -> p kc n", p=128)
        for kc in range(kc_n):
            P.dma("pool", dst_res.t[:, kc, :], v[:, kc, :], writes=[dst_res])

    def bcast_load(dst_res, row_ap):
        P.dma("sync", dst_res.t[:], row_ap.partition_broadcast(128), writes=[dst_res])

    def proj(lhsT_res, lhsT3, w_res, w3, kc_n, n_out, evac_fn, bank_lo=0, bank_hi=6):
        for n in range(n_out // 512):
            bk = nextbank(bank_lo, bank_hi)
            for kc in range(kc_n):
                P.op("pe", lambda e, kc=kc: e.matmul(bk.t[:, 0:512], lhsT3[:, kc, :], w3[:, kc, n * 512:(n + 1) * 512],
                                                     start=(kc == 0), stop=(kc == kc_n - 1)),
                     reads=[lhsT_res, w_res], writes=[bk])
            evac_fn(n, bk)

    def x_add(xr, j, n, bk):
        r = rows(j)
        P.op("dve", lambda e: e.tensor_tensor(out=xr.t[0:r, n * 512:(n + 1) * 512], in0=xr.t[0:r, n * 512:(n + 1) * 512],
                                              in1=bk.t[0:r, 0:512], op=ALU.add),
             reads=[xr, bk], writes=[xr])

    def gmlp_layer(l):
        with P.scope():
            w_in = P.sb("w_in", [128, 8, 4096], BF16)
            w_out = P.sb("w_out", [128, 16, 1024], BF16)
            g_v = P.sb("g_v", [128, 2048], F32)
            ga = P.sb("ga", [128, 8], F32)
            wsT = P.sb("wsT", [128, 8, 128], BF16)
            wsS = P.sb("wsS", [128, 8, 128], BF16)
            w00 = P.sb("w00", [128, 8, 1], F32)
            bsP = P.sb("bsP", [128, 8], F32)
            bsS = P.sb("bsS", [128, 8, 1], F32)
            u = P.sb("u", [128, 2048], BF16)
            v = P.sb("v", [128, 2048], F32)
            vg = P.sb("vg", [128, 2048], BF16)
            hnT = P.view("hnT", vg.t[:, 0:1024].rearrange("p (a b) -> p a b", b=128))
            gT = P.sb("gT", [128, 16, 128], BF16)

            load_w_bf16(w_in, a_w_in[l], 8)
            load_w_bf16(w_out, a_w_out[l], 16)
            bcast_load(g_v, a_vnorm_g[l])
            P.dma("sync", ga.t[:], a_norm_g[l].rearrange("(kc p) -> p kc", p=128), writes=[ga], allow_slow_non_contiguous=True)
            for kc in range(8):
                P.op("dve", lambda e, kc=kc: e.tensor_scalar(out=w_in.t[:, kc, :], in0=w_in.t[:, kc, :], scalar1=ga.t[:, kc:kc + 1],
                                                             scalar2=None, op0=ALU.mult), reads=[w_in, ga], writes=[w_in])
            wsf3 = v.t[:, 0:1024].rearrange("p (g s) -> p g s", s=128)
            wsb3 = vg.t[:, 0:1024].rearrange("p (g s) -> p g s", s=128)
            P.dma("sync", wsf3, a_w_s[l].rearrange("g t s -> t g s"), writes=[v])
            P.op("pool", lambda e: e.affine_select(out=wsf3, in_=wsf3, pattern=[[0, 8], [-1, 128]], compare_op=ALU.is_ge,
                                                    fill=0.0, base=0, channel_multiplier=1), reads=[v], writes=[v])
            P.op("dve", lambda e: e.tensor_copy(wsb3, wsf3), reads=[v], writes=[vg])
            transposes(wsT, wsT.t, vg, lambda i: wsb3[:, i, :], 8)
            P.dma("sync", w00.t[:], a_w_s[l, :, 0, 0:1].partition_broadcast(128), writes=[w00])
            for g in range(8):
                P.op("dve", lambda e, g=g: e.tensor_scalar(out=wsS.t[:, g, :], in0=ident_f.t[:], scalar1=w00.t[:, g, 0:1], scalar2=None,
                                                           op0=ALU.mult), reads=[ident_f, w00], writes=[wsS])
            P.dma("sync", bsP.t[:], a_b_s[l].rearrange("g t -> t g"), writes=[bsP], allow_slow_non_contiguous=True)
            P.dma("sync", bsS.t[:], a_b_s[l, :, 0:1].partition_broadcast(128), writes=[bsS])

            for j in range(NALL):
                xt = x_begin(j)
                hn_ap = gT.t[:, 0:8, :].rearrange("p a b -> p (a b)")
                rs = rstd_of(xt, xt.t[:], v, v.t[:, 0:1024], 1024, 0)
                P.op("dve", lambda e: e.tensor_scalar(out=hn_ap, in0=xt.t[:], scalar1=rs, scalar2=None, op0=ALU.mult),
                     reads=[xt, ss], writes=[gT])
                transposes(vg, hnT.t, gT, lambda i: hn_ap[:, i * 128:(i + 1) * 128], 8)

                def evac_z(n, bk):
                    if n < 4:
                        P.op("act", lambda e: e.activation(u.t[:, n * 512:(n + 1) * 512], bk.t[:, 0:512], AF.Gelu), reads=[bk], writes=[u])
                    else:
                        P.op("act", lambda e: e.activation(v.t[:, (n - 4) * 512:(n - 3) * 512], bk.t[:, 0:512], AF.Gelu), reads=[bk], writes=[v])
                proj(vg, hnT.t, w_in, w_in.t, 8, 4096, evac_z)
                rs2 = rstd_of(v, v.t[:], gT, gT.t[:].rearrange("p a b -> p (a b)"), 2048, 4)
                P.op("dve", lambda e: e.scalar_tensor_tensor(out=v.t[:], in0=v.t[:], scalar=rs2, in1=g_v.t[:], op0=ALU.mult, op1=ALU.mult),
                     reads=[v, ss, g_v], writes=[v])
                if j == 16:
                    P.dma("sync", gvo[l], v.t[0:16, :], reads=[v], out=True)
                P.op("act", lambda e: e.activation(vg.t[:], v.t[:], AF.Copy), reads=[v], writes=[vg])
                wmix = wsS if j == 16 else wsT
                bmix = (lambda g: bsS.t[:, g, 0:1]) if j == 16 else (lambda g: bsP.t[:, g:g + 1])
                for g in range(8):
                    bk = nextbank()
                    P.op("pe", lambda e, g=g: e.matmul(bk.t[:, 0:256], wmix.t[:, g, :], vg.t[:, g * 256:(g + 1) * 256], start=True, stop=True),
                         reads=[wmix, vg], writes=[bk])
                    P.op("dve", lambda e, g=g: e.scalar_tensor_tensor(out=vg.t[:, g * 256:(g + 1) * 256], in0=bk.t[:, 0:256], scalar=bmix(g),
                                                                      in1=u.t[:, g * 256:(g + 1) * 256], op0=ALU.add, op1=ALU.mult),
                         reads=[bk, u, bsP, bsS], writes=[vg])
                transposes(gT, gT.t, vg, lambda i: vg.t[:, i * 128:(i + 1) * 128], 16)
                proj(gT, gT.t, w_out, w_out.t, 16, 1024, lambda n, bk: x_add(xt, j, n, bk))
                x_end(j, xt)

    NB = 5

    def peer_layer(l):
        with P.scope():
            pwq = P.sb("pwq", [128, 8, 2048], BF16)
            kf = P.sb("kf", [128, 16, 128], F32)
            kb16 = P.sb("kb16", [128, 16, 128], BF16)
            keysT = P.sb("keysT", [128, 16, 128], BF16)
            g_f = P.sb("g_f", [128, 1024], F32)
            HN = [P.sb(f"hn{i}", [128, 1024], F32) for i in range(2)]
            EIDX = [P.sb(f"eidx{i}", [128, 128], I32) for i in range(2)]
            GW = [P.sb(f"gw{i}", [128, 128], F32) for i in range(2)]
            hnb = P.sb("hnb", [128, 1024], BF16)
            hnT = P.sb("hnT", [128, 8, 128], BF16)
            qb = P.sb("qb", [128, 2048], BF16)
            qT = P.sb("qT", [128, 16, 128], BF16)
            sc = P.sb("sc", [128, 16, 128], F32)
            wk = P.sb("wk", [128, 256], F32)
            sv = P.sb("sv", [128, 16, 16], F32)
            si = P.sb("si", [128, 16, 16], U32)
            sif = P.sb("sif", [128, 16, 16], F32)
            cand = P.sb("cand", [128, 8, 256], F32)
            tops = P.sb("tops", [128, 8, 16], F32)
            pos = P.sb("pos", [128, 8, 16], U32)
            posf = P.sb("posf", [128, 8, 16], F32)
            af = P.sb("af", [128, 8, 16], F32)
            bf = P.sb("bf", [128, 8, 16], F32)
            i0 = P.sb("i0", [128, 8, 16], F32)
            i1 = P.sb("i1", [128, 8, 16], F32)
            ex = P.sb("ex", [128, 8, 16], F32)
            zz = P.sb("zz", [128, 16], F32)
            actp = P.sb("actp", [128, 128], F32)
            wgt = P.sb("wgt", [128, 128], F32)
            junk = P.sb("junk", [128, 1024], F32)
            GB = [P.sb(f"gb{i}", [128, 1024], F32) for i in range(NB)]
            DG = [P.sb(f"dg{i}", [128, 128], F32) for i in range(3)]

            load_w_bf16(pwq, peer_w_q[l], 8)
            bcast_load(g_f, f_norm_g[l])
            P.dma("sync", kf.t[:], peer_keys[l].rearrange("c n d -> n c d"), writes=[kf])
            P.op("dve", lambda e: e.tensor_copy(kb16.t[:], kf.t[:]), reads=[kf], writes=[kb16])
            transposes(keysT, keysT.t, kb16, lambda i: kb16.t[:, i, :], 16)

            if l < 2:
                XSB[0] = [XS[0], P.sb("xs1", [128, 1024], F32)]
            XT = {}

            def routing(j):
                xt = XT[j] = x_begin(j, j % 2) if j >= NTILE else X[j]
                hn, eidx, gw = HN[j % 2], EIDX[j % 2], GW[j % 2]
                rs = rstd_of(xt, xt.t[:], junk, junk.t[:], 1024, 8)
                P.op("dve", lambda e: e.scalar_tensor_tensor(out=hn.t[:], in0=xt.t[:], scalar=rs, in1=g_f.t[:], op0=ALU.mult, op1=ALU.mult),
                     reads=[xt, ss, g_f], writes=[hn])
                P.op("act", lambda e: e.activation(hnb.t[:], hn.t[:], AF.Copy), reads=[hn], writes=[hnb])
                transposes(hnT, hnT.t, hnb, lambda i: hnb.t[:, i * 128:(i + 1) * 128], 8)
                proj(hnT, hnT.t, pwq, pwq.t, 8, 2048,
                     lambda n, bk: P.op("act", lambda e: e.activation(qb.t[:, n * 512:(n + 1) * 512], bk.t[:, 0:512], AF.Copy),
                                        reads=[bk], writes=[qb]))
                transposes(qT, qT.t, qb, lambda i: qb.t[:, i * 128:(i + 1) * 128], 16)
                for g4 in range(4):
                    bk = nextbank()
                    for i in range(4):
                        hc = g4 * 4 + i
                        P.op("pe", lambda e, hc=hc, i=i: e.matmul(bk.t[:, i * 128:(i + 1) * 128], qT.t[:, hc, :], keysT.t[:, hc, :], start=True, stop=True),
                             reads=[qT, keysT], writes=[bk])
                    P.op("act", lambda e, g4=g4: e.activation(sc.t[:, g4 * 4:(g4 + 1) * 4, :], bk.t[:, 0:512].rearrange("p (a b) -> p a b", b=128), AF.Copy),
                         reads=[bk], writes=[sc])
                for hc in range(16):
                    P.op("dve", lambda e, hc=hc: e.max(sv.t[:, hc, 0:8], sc.t[:, hc, :]), reads=[sc], writes=[sv])
                    P.op("dve", lambda e, hc=hc: e.max_index(si.t[:, hc, 0:8], sv.t[:, hc, 0:8], sc.t[:, hc, :]), reads=[sc, sv], writes=[si])
                    P.op("dve", lambda e, hc=hc: e.match_replace(wk.t[:, 0:128], sv.t[:, hc, 0:8], sc.t[:, hc, :], -1e30), reads=[sc, sv], writes=[wk])
                    P.op("dve", lambda e, hc=hc: e.max(sv.t[:, hc, 8:16], wk.t[:, 0:128]), reads=[wk], writes=[sv])
                    P.op("dve", lambda e, hc=hc: e.max_index(si.t[:, hc, 8:16], sv.t[:, hc, 8:16], wk.t[:, 0:128]), reads=[wk, sv], writes=[si])
                P.op("dve", lambda e: e.tensor_copy(sif.t[:], si.t[:]), reads=[si], writes=[sif])
                sv4 = sv.t[:].rearrange("p (h c) k -> p h c k", c=2)
                sif4 = sif.t[:].rearrange("p (h c) k -> p h c k", c=2)
                cand4 = cand.t[:].rearrange("p h (a b) -> p h a b", b=16)
                P.op("dve", lambda e: e.tensor_tensor(out=cand4, in0=sv4[:, :, 0, :].unsqueeze(3).to_broadcast([128, 8, 16, 16]),
                                                      in1=sv4[:, :, 1, :].unsqueeze(2).to_broadcast([128, 8, 16, 16]), op=ALU.add),
                     reads=[sv], writes=[cand])
                for h in range(8):
                    P.op("dve", lambda e, h=h: e.max(tops.t[:, h, 0:8], cand.t[:, h, :]), reads=[cand], writes=[tops])
                    P.op("dve", lambda e, h=h: e.max_index(pos.t[:, h, 0:8], tops.t[:, h, 0:8], cand.t[:, h, :]), reads=[cand, tops], writes=[pos])
                    P.op("dve", lambda e, h=h: e.match_replace(wk.t[:], tops.t[:, h, 0:8], cand.t[:, h, :], -1e30), reads=[cand, tops], writes=[wk])
                    P.op("dve", lambda e, h=h: e.max(tops.t[:, h, 8:16], wk.t[:]), reads=[wk], writes=[tops])
                    P.op("dve", lambda e, h=h: e.max_index(pos.t[:, h, 8:16], tops.t[:, h, 8:16], wk.t[:]), reads=[wk, tops], writes=[pos])
                P.op("dve", lambda e: e.tensor_tensor(out=ex.t[:], in0=tops.t[:], in1=tops.t[:, :, 0:1].to_broadcast([128, 8, 16]), op=ALU.subtract),
                     reads=[tops], writes=[ex])
                P.op("act", lambda e: e.activation(ex.t[:], ex.t[:], AF.Exp), reads=[ex], writes=[ex])
                P.op("dve", lambda e: e.tensor_reduce(out=zz.t[:, 0:8], in_=ex.t[:], axis=AX.X, op=ALU.add), reads=[ex], writes=[zz])
                P.op("dve", lambda e: e.reciprocal(zz.t[:, 8:16], zz.t[:, 0:8]), reads=[zz], writes=[zz])
                P.op("dve", lambda e: e.tensor_tensor(out=gw.t[:].rearrange("p (h k) -> p h k", k=16), in0=ex.t[:],
                                                      in1=zz.t[:, 8:16].unsqueeze(2).to_broadcast([128, 8, 16]), op=ALU.mult),
                     reads=[ex, zz], writes=[gw])
                P.op("dve", lambda e: e.tensor_copy(posf.t[:], pos.t[:]), reads=[pos], writes=[posf])
                P.op("dve", lambda e: e.tensor_scalar(out=af.t[:], in0=posf.t[:], scalar1=16.0, scalar2=None, op0=ALU.is_ge), reads=[posf], writes=[af])
                for k in range(2, 16):
                    P.op("dve", lambda e, k=k: e.scalar_tensor_tensor(out=af.t[:], in0=posf.t[:], scalar=16.0 * k, in1=af.t[:], op0=ALU.is_ge, op1=ALU.add),
                         reads=[posf, af], writes=[af])
                P.op("dve", lambda e: e.scalar_tensor_tensor(out=bf.t[:], in0=af.t[:], scalar=-16.0, in1=posf.t[:], op0=ALU.mult, op1=ALU.add),
                     reads=[posf, af], writes=[bf])
                oh = P.view("oh", cand.t[:].rearrange("p h (a b) -> p h a b", b=16))
                io4 = iota16.t[:].unsqueeze(1).unsqueeze(1).to_broadcast([128, 8, 16, 16])
                for (sel, c, dst) in ((af, 0, i0), (bf, 1, i1)):
                    P.op("dve", lambda e, sel=sel: e.tensor_tensor(out=oh.t[:], in0=sel.t[:].unsqueeze(3).to_broadcast([128, 8, 16, 16]), in1=io4, op=ALU.is_equal),
                         reads=[sel, iota16], writes=[cand])
                    P.op("dve", lambda e, c=c: e.tensor_tensor(out=oh.t[:], in0=oh.t[:], in1=sif4[:, :, c, :].unsqueeze(2).to_broadcast([128, 8, 16, 16]), op=ALU.mult),
                         reads=[cand, sif], writes=[cand])
                    P.op("dve", lambda e, dst=dst: e.tensor_reduce(out=dst.t[:], in_=oh.t[:], axis=AX.X, op=ALU.add), reads=[cand], writes=[dst])
                P.op("dve", lambda e: e.tensor_scalar(out=i0.t[:], in0=i0.t[:], scalar1=128.0, scalar2=float(l * N_EXP), op0=ALU.mult, op1=ALU.add),
                     reads=[i0], writes=[i0])
                P.op("dve", lambda e: e.tensor_tensor(out=eidx.t[:].rearrange("p (h k) -> p h k", k=16), in0=i0.t[:], in1=i1.t[:], op=ALU.add),
                     reads=[i0, i1], writes=[eidx])

            gb_rr = [0]

            def experts(j):
                hn, eidx, gw = HN[j % 2], EIDX[j % 2], GW[j % 2]
                for s in range(128):
                    gb = GB[gb_rr[0] % NB]
                    gb_rr[0] += 1
                    P.gather(gb.t[:], peer_u, eidx.t[:, s:s + 1], reads=[eidx], writes=[gb])
                    P.op("dve", lambda e, s=s, gb=gb: e.scalar_tensor_tensor(out=junk.t[:], in0=gb.t[:], scalar=1.0, in1=hn.t[:], op0=ALU.mult, op1=ALU.mult,
                                                                             accum_out=actp.t[:, s:s + 1]),
                         reads=[gb, hn], writes=[junk, actp])
                P.op("act", lambda e: e.activation(wgt.t[:], actp.t[:], AF.Gelu), reads=[actp], writes=[wgt])
                P.op("dve", lambda e: e.tensor_tensor(out=wgt.t[:], in0=wgt.t[:], in1=gw.t[:], op=ALU.mult), reads=[wgt, gw], writes=[wgt])
                for s in range(128):
                    gb = GB[gb_rr[0] % NB]
                    gb_rr[0] += 1
                    dg = DG[s % 3]
                    P.gather(gb.t[:], peer_v, eidx.t[:, s:s + 1], reads=[eidx], writes=[gb])
                    P.op("dve", lambda e, s=s, dg=dg: e.tensor_scalar(out=dg.t[:], in0=ident_f.t[:], scalar1=wgt.t[:, s:s + 1], scalar2=None, op0=ALU.mult),
                         reads=[ident_f, wgt], writes=[dg])
                    for n in range(2):
                        P.op("pe", lambda e, n=n, s=s, dg=dg, gb=gb: e.matmul(BK[6 + n].t[:, 0:512], dg.t[:], gb.t[:, n * 512:(n + 1) * 512],
                                                                               start=(s == 0), stop=(s == 127)),
                             reads=[dg, gb], writes=[BK[6 + n]])
                for n in range(2):
                    x_add(XT[j], j, n, BK[6 + n])
                x_end(j, XT[j])

            nt = min(n_tiles_peer, NALL if l < 2 else NTILE)
            routing(0)
            for j in range(nt):
                if j + 1 < nt:
                    routing(j + 1)
                experts(j)
        XSB[0] = XS

    def kv_phase():
        with P.scope():
            wk_ = P.sb("w_k", [128, 8, 1024], BF16)
            wv_ = P.sb("w_v", [128, 8, 1024], BF16)
            g_kv = P.sb("g_kv", [128, 1024], F32)
            g_kn = P.sb("g_kn", [128, 64], F32)
            hnb = P.sb("hnb", [128, 1024], BF16)
            hnT = P.sb("hnT", [128, 8, 128], BF16)
            KF = [P.sb(f"kf{i}", [128, 1024], F32) for i in range(2)]
            VF = [P.sb(f"vf{i}", [128, 1024], F32) for i in range(2)]
            sq = P.sb("sq", [128, 1024], F32)
            st = P.sb("st", [128, 48], F32)
            knb = P.sb("knb", [128, 8, 2, 64], BF16)
            KT = [P.sb(f"kt{i}", [128, 8, 128], BF16) for i in range(2)]
            VE = [P.sb(f"ve{i}", [128, 8, 129], BF16) for i in range(2)]
            load_w_bf16(wk_, w_k, 8)
            load_w_bf16(wv_, w_v, 8)
            bcast_load(g_kv, kv_norm_g[0])
            bcast_load(g_kn, k_norm_g[0])
            for i in range(2):
                P.op("pool", lambda e, i=i: e.memset(VE[i].t[:], 1.0), writes=[VE[i]])
            for j in range(NALL):
                xt = x_begin(j)
                own = j < NPT
                slot = 2 * j if own else 2 * (j - NTILE) + 1
                kf = kn_s if j == 16 else KF[j % 2]
                vf = v_s if j == 16 else VF[j % 2]
                rs = rstd_of(xt, xt.t[:], sq, sq.t[:], 1024, 12)
                P.op("dve", lambda e: e.scalar_tensor_tensor(out=hnb.t[:], in0=xt.t[:], scalar=rs, in1=g_kv.t[:], op0=ALU.mult, op1=ALU.mult),
                     reads=[xt, ss, g_kv], writes=[hnb])
                transposes(hnT, hnT.t, hnb, lambda i: hnb.t[:, i * 128:(i + 1) * 128], 8)
                proj(hnT, hnT.t, wk_, wk_.t, 8, 1024,
                     lambda n, bk: P.op("act", lambda e: e.activation(kf.t[:, n * 512:(n + 1) * 512], bk.t[:, 0:512], AF.Copy), reads=[bk], writes=[kf]))
                proj(hnT, hnT.t, wv_, wv_.t, 8, 1024,
                     lambda n, bk: P.op("act", lambda e: e.activation(vf.t[:, n * 512:(n + 1) * 512], bk.t[:, 0:512], AF.Copy), reads=[bk], writes=[vf]))
                k3 = kf.t[:].rearrange("p (a b) -> p a b", b=64)
                rg = group_rstd(kf, k3, sq, sq.t[:].rearrange("p (a b) -> p a b", b=64), 16, 64, st, 0)
                P.op("dve", lambda e: e.tensor_tensor(out=k3, in0=k3, in1=rg.unsqueeze(2).to_broadcast([128, 16, 64]), op=ALU.mult),
                     reads=[kf, st], writes=[kf])
                P.op("dve", lambda e: e.tensor_tensor(out=k3, in0=k3, in1=g_kn.t[:].unsqueeze(1).to_broadcast([128, 16, 64]), op=ALU.mult),
                     reads=[kf, g_kn], writes=[kf])
                if j == 16:
                    P.dma("sync", nks, kf.t[0:16, :], reads=[kf], out=True)
                    P.dma("sync", nvs, vf.t[0:16, :], reads=[vf], out=True)
                    continue
                if own:
                    P.dma("sync", nkp[j * 128:(j + 1) * 128, :], kf.t[:], reads=[kf], out=True)
                    P.dma("sync", nvp[j * 128:(j + 1) * 128, :], vf.t[:], reads=[vf], out=True)
                kt, ve = KT[j % 2], VE[j % 2]
                P.op("act", lambda e: e.activation(knb.t[:], kf.t[:].rearrange("p (m h d) -> p h m d", m=2, d=64), AF.Copy), reads=[kf], writes=[knb])
                transposes(kt, kt.t, knb, lambda h: knb.t[:, h, :, :].rearrange("p m d -> p (m d)"), 8)
                P.op("act", lambda e: e.activation(ve.t[:, :, 0:128], vf.t[:].rearrange("p (h d) -> p h d", d=128), AF.Copy), reads=[vf], writes=[ve])
                P.dma("sync", kvx_all.t[slot, :, 0:1024], kt.t[:].rearrange("p a b -> p (a b)"), reads=[kt], writes=[kvx_all], prim=kvx_all)
                P.dma("sync", kvx_all.t[slot, :, 1024:2056], ve.t[:].rearrange("p a b -> p (a b)"), reads=[ve], writes=[kvx_all], prim=kvx_all)

    def attn_layer(li):
        l = 2 + li
        lam0 = _lam_init(l)
        slopes = [2.0 ** (-8.0 * (h + 1) / 8) for h in range(8)]
        with P.scope():
            wq_ = P.sb("w_q", [128, 8, 1024], BF16)
            wo_ = P.sb("w_o", [128, 8, 1024], BF16)
            g_b = P.sb("g_b", [128, 1024], F32)
            g_qn = P.sb("g_qn", [128, 64], F32)
            g_sub = P.sb("g_sub", [128, 128], F32)
            lamv = P.sb("lamv", [128, 256], F32)
            lamc = P.sb("lamc", [128, 8], F32)
            hfc = P.sb("hfc", [128, 1], F32)
            biasT = P.sb("biasT", [128, 16, 2, 8], F32)
            bbase = P.sb("bbase", [128, 16, 2], F32)
            sbias = P.sb("sbias", [128, 16, 16], F32)
            sbase = P.sb("sbase", [128, 16], F32)
            mskf = P.sb("mskf", [128, 2, 128], F32)
            mskb = P.sb("mskb", [128, 2, 128], BF16)
            ptb = P.sb("ptb", [128, 256], I32)
            rowidx = P.sb("rowidx", [128, 256], I32)
            ones1 = P.sb("ones1", [128, 1], F32)
            bmask = P.sb("bmask", [128, 8], F32)
            hnb = P.sb("hnb", [128, 1024], BF16)
            hnT = P.sb("hnT", [128, 8, 128], BF16)
            qf = P.sb("qf", [128, 1024], F32)
            sq = P.sb("sq", [128, 1024], F32)
            st = P.sb("st", [128, 64], F32)
            qnb = P.sb("qnb", [128, 8, 2, 64], BF16)
            QT = P.sb("QT", [128, 8, 128], BF16)
            QTm = [P.sb(f"QTm{m}", [128, 8, 128], BF16) for m in range(2)]
            KV = [P.sb(f"kv{i}", [128, 2056], BF16) for i in range(3)]
            PT = [P.sb(f"pt{i}", [128, 256], BF16) for i in range(4)]
            oev = P.sb("oev", [128, 16, 129], F32)
            rz = P.sb("rz", [128, 16], F32)
            o0 = P.sb("o0", [128, 8, 128], F32)
            o1 = P.sb("o1", [128, 8, 128], F32)
            onb = P.sb("onb", [128, 1024], BF16)
            oT = P.sb("oT", [128, 8, 128], BF16)
            qbc = P.sb("qbc", [128, 1024], F32)
            KP = [P.sb(f"kp{i}", [128, 1024], F32) for i in range(2)]
            VP = [P.sb(f"vp{i}", [128, 1024], F32) for i in range(2)]
            sc16 = P.sb("sc16", [128, 32], F32)
            E16 = [P.sb(f"e16_{i}", [128, 16], F32) for i in range(2)]
            dall = P.sb("dall", [16, 16, 129], F32)
            enew = P.sb("enew", [128, 16], F32)
            ST = [P.view(f"st{i}", BK[6 + i].t[:, 0:256]) for i in range(2)]

            load_w_bf16(wq_, w_q[li], 8)
            load_w_bf16(wo_, w_o[li], 8)
            bcast_load(g_b, b_norm_g[li])
            bcast_load(g_qn, q_norm_g[li])
            bcast_load(g_sub, subln_g[li])
            bcast_load(lamv, lam_in[li])
            P.dma("sync", hfc.t[:], hfcol_in, writes=[hfc])
            P.dma("sync", mskf.t[:], maskadd, writes=[mskf])
            P.op("dve", lambda e: e.tensor_copy(mskb.t[:], mskf.t[:]), reads=[mskf], writes=[mskb])
            P.dma("sync", ptb.t[:], pt[0].partition_broadcast(128), writes=[ptb])
            P.op("dve", lambda e: e.scalar_tensor_tensor(out=rowidx.t[:], in0=ptb.t[:], scalar=128.0, in1=pcol.t[:, 0:1].to_broadcast([128, 256]),
                                                         op0=ALU.mult, op1=ALU.add), reads=[ptb, pcol], writes=[rowidx])
            P.op("pool", lambda e: e.memset(ones1.t[:], 1.0), writes=[ones1])
            P.op("dve", lambda e: e.scalar_tensor_tensor(out=sq.t[:, 0:64], in0=lamv.t[:, 0:64], scalar=1.0, in1=lamv.t[:, 64:128], op0=ALU.mult, op1=ALU.mult,
                                                         accum_out=lamc.t[:, 0:1]), reads=[lamv], writes=[sq, lamc])
            P.op("dve", lambda e: e.scalar_tensor_tensor(out=sq.t[:, 0:64], in0=lamv.t[:, 128:192], scalar=1.0, in1=lamv.t[:, 192:256], op0=ALU.mult, op1=ALU.mult,
                                                         accum_out=lamc.t[:, 1:2]), reads=[lamv], writes=[sq, lamc])
            P.op("act", lambda e: e.activation(lamc.t[:, 2:4], lamc.t[:, 0:2], AF.Exp), reads=[lamc], writes=[lamc])
            P.op("dve", lambda e: e.tensor_tensor(out=lamc.t[:, 4:5], in0=lamc.t[:, 3:4], in1=lamc.t[:, 2:3], op=ALU.subtract), reads=[lamc], writes=[lamc])
            P.op("dve", lambda e: e.tensor_scalar(out=lamc.t[:, 4:5], in0=lamc.t[:, 4:5], scalar1=-lam0, scalar2=None, op0=ALU.add), reads=[lamc], writes=[lamc])
            neg_lam = lamc.t[:, 4:5]
            P.op("pool", lambda e: e.iota(bbase.t[:], [[-256, 16], [0, 2]], base=-64, channel_multiplier=1, allow_small_or_imprecise_dtypes=True), writes=[bbase])
            P.op("dve", lambda e: e.tensor_scalar(out=bbase.t[:, :, 1], in0=bbase.t[:, :, 1], scalar1=hfc.t[:, 0:1], scalar2=None, op0=ALU.add), reads=[bbase, hfc], writes=[bbase])
            for h in range(8):
                P.op("dve", lambda e, h=h: e.tensor_scalar(out=biasT.t[:, :, :, h], in0=bbase.t[:], scalar1=slopes[h], scalar2=None, op0=ALU.mult),
                     reads=[bbase], writes=[biasT])
            P.op("pool", lambda e: e.iota(sbase.t[:], [[128, 16]], base=-2048, channel_multiplier=1, allow_small_or_imprecise_dtypes=True), writes=[sbase])
            for mh in range(16):
                P.op("dve", lambda e, mh=mh: e.tensor_scalar(out=sbias.t[:, :, mh], in0=sbase.t[:], scalar1=slopes[mh % 8], scalar2=None, op0=ALU.mult),
                     reads=[sbase], writes=[sbias])
            P.op("dve", lambda e: e.tensor_tensor(out=bmask.t[0:16, :], in0=ident_f.t[0:16, 0:8], in1=ident_f.t[0:16, 8:16], op=ALU.add),
                 reads=[ident_f], writes=[bmask])

            for m in range(2):
                P.op("pool", lambda e, m=m: e.memset(QTm[m].t[:], 0.0), writes=[QTm[m]])

            def q_side(j):
                xt = X[j]
                rs = rstd_of(xt, xt.t[:], sq, sq.t[:], 1024, 16)
                P.op("dve", lambda e: e.scalar_tensor_tensor(out=hnb.t[:], in0=xt.t[:], scalar=rs, in1=g_b.t[:], op0=ALU.mult, op1=ALU.mult),
                     reads=[xt, ss, g_b], writes=[hnb])
                transposes(hnT, hnT.t, hnb, lambda i: hnb.t[:, i * 128:(i + 1) * 128], 8)
                proj(hnT, hnT.t, wq_, wq_.t, 8, 1024,
                     lambda n, bk: P.op("act", lambda e: e.activation(qf.t[:, n * 512:(n + 1) * 512], bk.t[:, 0:512], AF.Copy), reads=[bk], writes=[qf]))
                q3 = qf.t[:].rearrange("p (a b) -> p a b", b=64)
                rg = group_rstd(qf, q3, sq, sq.t[:].rearrange("p (a b) -> p a b", b=64), 16, 64, st, 0)
                P.op("dve", lambda e: e.tensor_tensor(out=q3, in0=q3, in1=rg.unsqueeze(2).to_broadcast([128, 16, 64]), op=ALU.mult),
                     reads=[qf, st], writes=[qf])
                P.op("dve", lambda e: e.tensor_tensor(out=q3, in0=q3, in1=g_qn.t[:].unsqueeze(1).to_broadcast([128, 16, 64]), op=ALU.mult),
                     reads=[qf, g_qn], writes=[qf])

            def epilogue(j):
                P.op("dve", lambda e: e.reciprocal(rz.t[:], oev.t[:, :, 128]), reads=[oev], writes=[rz])
                P.op("dve", lambda e: e.tensor_tensor(out=o0.t[:], in0=oev.t[:, 0:8, 0:128], in1=rz.t[:, 0:8].unsqueeze(2).to_broadcast([128, 8, 128]), op=ALU.mult),
                     reads=[oev, rz], writes=[o0])
                P.op("dve", lambda e: e.tensor_tensor(out=o1.t[:], in0=oev.t[:, 8:16, 0:128], in1=rz.t[:, 8:16].unsqueeze(2).to_broadcast([128, 8, 128]), op=ALU.mult),
                     reads=[oev, rz], writes=[o1])
                of = o0.t[:].rearrange("p a b -> p (a b)")
                P.op("dve", lambda e: e.scalar_tensor_tensor(out=of, in0=o1.t[:].rearrange("p a b -> p (a b)"), scalar=neg_lam, in1=of, op0=ALU.mult, op1=ALU.add),
                     reads=[o0, o1, lamc], writes=[o0])
                rg = group_rstd(o0, o0.t[:], o1, o1.t[:], 8, 128, st, 48 - 24)
                P.op("dve", lambda e: e.tensor_tensor(out=o0.t[:], in0=o0.t[:], in1=rg.unsqueeze(2).to_broadcast([128, 8, 128]), op=ALU.mult),
                     reads=[o0, st], writes=[o0])
                P.op("dve", lambda e: e.scalar_tensor_tensor(out=onb.t[:].rearrange("p (a b) -> p a b", b=128), in0=o0.t[:], scalar=1.0 - lam0,
                                                             in1=g_sub.t[:].unsqueeze(1).to_broadcast([128, 8, 128]), op0=ALU.mult, op1=ALU.mult),
                     reads=[o0, g_sub], writes=[onb])
                transposes(oT, oT.t, onb, lambda i: onb.t[:, i * 128:(i + 1) * 128], 8)
                proj(oT, oT.t, wo_, wo_.t, 8, 1024, lambda n, bk: x_add(X[j], j, n, bk))

            kv_rr = [0]
            pt_rr = [0]
            for i in range(dbg_qtiles):
                q_side(i)
                P.op("act", lambda e: e.activation(qnb.t[:], qf.t[:].rearrange("p (m h d) -> p h m d", m=2, d=64), AF.Copy), reads=[qf], writes=[qnb])
                transposes(QT, QT.t, qnb, lambda h: qnb.t[:, h, :, :].rearrange("p m d -> p (m d)"), 8)
                for m in range(2):
                    P.op("dve", lambda e, m=m: e.tensor_copy(QTm[m].t[m * 64:(m + 1) * 64], QT.t[m * 64:(m + 1) * 64]), reads=[QT], writes=[QTm[m]])
                nkb = 2 * i + 2
                if dbg_att < 2:
                    continue
                bank_started = set()
                for kb in range(nkb):
                    kvb = KV[kv_rr[0] % 3]
                    kv_rr[0] += 1
                    P.dma("sync", kvb.t[:], kvx_all.t[kb], reads=[kvx_all], writes=[kvb])
                    ip, w = kb // 2, kb % 2
                    di = i - ip
                    special = (ip == i) or FORCE_SPECIAL
                    for h in range(8):
                        stv = ST[pt_rr[0] % 2]
                        ptb_ = PT[pt_rr[0] % 4]
                        pt_rr[0] += 1
                        for m in range(2):
                            P.op("pe", lambda e, m=m, h=h: e.matmul(stv.t[:, m * 128:(m + 1) * 128], kvb.t[:, h * 128:(h + 1) * 128],
                                                                    QTm[m].t[:, h, :], start=True, stop=not special),
                                 reads=[kvb, QTm[m]], writes=[stv])
                            if special:
                                P.op("pe", lambda e, m=m: e.matmul(stv.t[:, m * 128:(m + 1) * 128], ident_b.t[:], mskb.t[:, w, :], start=False, stop=True),
                                     reads=[ident_b, mskb], writes=[stv])
                        P.op("act", lambda e, h=h: e.activation(ptb_.t[:], stv.t[:], AF.Exp, bias=biasT.t[:, di, w, h:h + 1], scale=0.125),
                             reads=[stv, biasT], writes=[ptb_])
                        for m in range(2):
                            if dbg_att < 3:
                                continue
                            gi = m * 8 + h
                            ob = BK[gi // 3]
                            first = (gi // 3) not in bank_started
                            bank_started.add(gi // 3)
                            P.op("pe", lambda e, m=m, h=h, gi=gi, ob=ob, first=first: e.matmul(ob.t[:, (gi % 3) * 129:(gi % 3) * 129 + 129], ptb_.t[:, m * 128:(m + 1) * 128],
                                                                                kvb.t[:, 1024 + h * 129:1024 + (h + 1) * 129], start=first, stop=(kb == nkb - 1),
                                                                                skip_group_check=True),
                                 reads=[ptb_, kvb], writes=[ob])
                if dbg_att < 4:
                    continue
                for b in range(6):
                    n3 = 3 if b < 5 else 1
                    P.op("act" if b % 2 else "dve",
                         (lambda e, b=b, n3=n3: e.activation(oev.t[:, b * 3:b * 3 + n3, :], BK[b].t[:, 0:n3 * 129].rearrange("p (a c) -> p a c", c=129), AF.Copy)) if b % 2 else
                         (lambda e, b=b, n3=n3: e.tensor_copy(oev.t[:, b * 3:b * 3 + n3, :], BK[b].t[:, 0:n3 * 129].rearrange("p (a c) -> p a c", c=129))),
                         reads=[BK[b]], writes=[oev])
                epilogue(i)

            if dbg_skip_sample:
                return
            q_side(16)
            q3 = qf.t[:].rearrange("p (a b) -> p a b", b=64)
            P.dma("sync", dq.t, qf.t[0:16, :], reads=[qf], writes=[dq])
            P.op("dve", lambda e: e.tensor_tensor(out=sq.t[:], in0=qf.t[:], in1=kn_s.t[:], op=ALU.mult), reads=[qf, kn_s], writes=[sq])
            P.op("dve", lambda e: e.tensor_reduce(out=enew.t[:], in_=sq.t[:].rearrange("p (a b) -> p a b", b=64), axis=AX.X, op=ALU.add), reads=[sq], writes=[enew])
            P.op("act", lambda e: e.activation(enew.t[:], enew.t[:], AF.Exp, scale=0.125), reads=[enew], writes=[enew])
            pg_rr = [0]
            rtmp_ap = o1.t[0:16, :, :]
            for s in range(16):
                P.dma("sync", qbc.t[:], dq.t[s].partition_broadcast(128), reads=[dq], writes=[qbc])
                for pg in range(16):
                    kp, vp, e16 = KP[pg_rr[0] % 2], VP[pg_rr[0] % 2], E16[pg_rr[0] % 2]
                    pg_rr[0] += 1
                    P.gather(kp.t[:], cache_k, rowidx.t[:, s * 16 + pg:s * 16 + pg + 1], reads=[rowidx], writes=[kp])
                    P.gather(vp.t[:], cache_v, rowidx.t[:, s * 16 + pg:s * 16 + pg + 1], reads=[rowidx], writes=[vp])
                    P.op("dve", lambda e, kp=kp: e.tensor_tensor(out=sq.t[:], in0=kp.t[:], in1=qbc.t[:], op=ALU.mult), reads=[kp, qbc], writes=[sq])
                    P.op("dve", lambda e: e.tensor_reduce(out=sc16.t[:, 0:16], in_=sq.t[:].rearrange("p (a b) -> p a b", b=64), axis=AX.X, op=ALU.add),
                         reads=[sq], writes=[sc16])
                    P.op("dve", lambda e, pg=pg: e.scalar_tensor_tensor(out=sc16.t[:, 16:32], in0=sc16.t[:, 0:16], scalar=0.125, in1=sbias.t[:, pg, :], op0=ALU.mult, op1=ALU.add),
                         reads=[sc16, sbias], writes=[sc16])
                    P.op("act", lambda e, e16=e16: e.activation(e16.t[:], sc16.t[:, 16:32], AF.Exp), reads=[sc16], writes=[e16])
                    for n in range(2):
                        P.op("pe", lambda e, n=n, e16=e16, vp=vp, pg=pg: e.matmul(BK[n].t[0:16, 0:512], e16.t[:], vp.t[:, n * 512:(n + 1) * 512], start=(pg == 0), stop=(pg == 15)),
                             reads=[e16, vp], writes=[BK[n]])
                    P.op("pe", lambda e, e16=e16, pg=pg: e.matmul(BK[2].t[0:16, 0:1], e16.t[:], ones1.t[:], start=(pg == 0), stop=(pg == 15)),
                         reads=[e16, ones1], writes=[BK[2]])
                for n in range(2):
                    P.op("dve", lambda e, n=n: e.tensor_tensor(out=rtmp_ap[:, n * 4:(n + 1) * 4, :], in0=BK[n].t[0:16, 0:512].rearrange("p (a b) -> p a b", b=128),
                                                               in1=bmask.t[0:16, n * 4:(n + 1) * 4].unsqueeze(2).to_broadcast([16, 4, 128]), op=ALU.mult),
                         reads=[BK[n], bmask], writes=[o1])
                P.op("dve", lambda e, s=s: e.tensor_reduce(out=dall.t[:, s, 0:128], in_=rtmp_ap.rearrange("p h d -> p d h"), axis=AX.X, op=ALU.add),
                     reads=[o1], writes=[dall])
                P.op("dve", lambda e, s=s: e.tensor_copy(dall.t[:, s, 128:129], BK[2].t[0:16, 0:1]), reads=[BK[2]], writes=[dall])
            P.op("pool", lambda e: e.memset(oev.t[:], 1.0), writes=[oev])
            P.dma("sync", dD.t, dall.t[:], reads=[dall], writes=[dD])
            P.dma("sync", oev.t[0:16, :, :], dD.t.rearrange("m s d -> s m d"), reads=[dD], writes=[oev])
            v3 = v_s.t[:].rearrange("p (h d) -> p h d", d=128)
            for m in range(2):
                P.op("dve", lambda e, m=m: e.tensor_tensor(out=o1.t[:], in0=v3, in1=enew.t[:, m * 8:(m + 1) * 8].unsqueeze(2).to_broadcast([128, 8, 128]), op=ALU.mult),
                     reads=[v_s, enew], writes=[o1])
                P.op("dve", lambda e, m=m: e.tensor_tensor(out=oev.t[:, m * 8:(m + 1) * 8, 0:128], in0=oev.t[:, m * 8:(m + 1) * 8, 0:128], in1=o1.t[:], op=ALU.add),
                     reads=[oev, o1], writes=[oev])
            P.op("dve", lambda e: e.tensor_tensor(out=oev.t[:, :, 128], in0=oev.t[:, :, 128], in1=enew.t[:], op=ALU.add), reads=[oev, enew], writes=[oev])
            epilogue(16)

    steps = {"g0": lambda: gmlp_layer(0), "p0": lambda: peer_layer(0), "g1": lambda: gmlp_layer(1), "p1": lambda: peer_layer(1),
             "kv": kv_phase, "a0": lambda: attn_layer(0), "p2": lambda: peer_layer(2), "a1": lambda: attn_layer(1), "p3": lambda: peer_layer(3)}
    if "A" in stages:
        stages = ["g0", "p0", "g1", "p1"] + [x for x in stages if x != "A"]
    if "KV" in stages:
        stages = [("kv" if x == "KV" else x) for x in stages]
    if "B" in stages:
        stages = [x for x in stages if x != "B"] + ["a0", "p2", "a1", "p3"]
    kn_alloc = [False]
    for st_ in stages:
        if st_ in ("kv", "a0", "a1") and not kn_alloc[0]:
            kn_alloc[0] = True
            kn_s = P.sb("kn_s", [128, 1024], F32)
            v_s = P.sb("v_s", [128, 1024], F32)
        steps[st_]()
    for j in range(NTILE):
        P.dma("sync", y[j * 128:(j + 1) * 128, :], X[j].t[:], reads=[X[j]], out=True)
    P.finish()
    return nc, P


_CACHE = {}


def kernel(x_prompt, x_sample, cache_k, cache_v, page_table,
           a_norm_g, a_w_in, a_vnorm_g, a_w_s, a_b_s, a_w_out,
           kv_norm_g, w_k, w_v, k_norm_g,
           b_norm_g, w_q, q_norm_g, lambda_q1, lambda_k1, lambda_q2, lambda_k2, subln_g, w_o,
           f_norm_g, peer_w_q, peer_keys, peer_u, peer_v):
    f = lambda a: np.ascontiguousarray(np.asarray(a, dtype=np.float32))
    xp = f(x_prompt)
    xs = f(x_sample).reshape(128, 1024)
    if "nc" not in _CACHE:
        _CACHE["nc"] = build_program()[0]
    nc = _CACHE.get("nc_override", _CACHE["nc"])
    lam_in = np.ascontiguousarray(np.stack([f(lambda_q1), f(lambda_k1), f(lambda_q2), f(lambda_k2)], axis=1).reshape(2, 256))
    shared = {
        "cache_k": f(cache_k).reshape(2560 * 128, 1024), "cache_v": f(cache_v).reshape(2560 * 128, 1024),
        "a_norm_g": f(a_norm_g), "a_w_in": f(a_w_in), "a_vnorm_g": f(a_vnorm_g), "a_w_s": f(a_w_s), "a_b_s": f(a_b_s),
        "a_w_out": f(a_w_out), "kv_norm_g": f(kv_norm_g).reshape(1, 1024), "w_k": f(w_k), "w_v": f(w_v),
        "k_norm_g": f(k_norm_g).reshape(1, 64), "b_norm_g": f(b_norm_g), "w_q": f(w_q), "q_norm_g": f(q_norm_g),
        "lam_in": lam_in, "subln_g": f(subln_g), "w_o": f(w_o), "f_norm_g": f(f_norm_g), "peer_w_q": f(peer_w_q),
        "peer_keys": f(peer_keys).reshape(4, 16, 128, 128), "peer_u": f(peer_u).reshape(4 * N_EXP, 1024),
        "peer_v": f(peer_v).reshape(4 * N_EXP, 1024),
    }
    ptab = np.ascontiguousarray(np.asarray(page_table, dtype=np.int32))
    kk = np.arange(128)[:, None]
    qq = np.arange(128)[None, :]
    causal = np.where(kk <= qq, 0.0, -30000.0).astype(np.float32)
    in_maps = []
    for c in range(8):
        b, hf = c // 2, c % 2
        xin = np.zeros((NALL * 128, 1024), np.float32)
        xin[:NPT * 128] = xp[b].reshape(16, 2, 128, 1024)[:, hf].reshape(NPT * 128, 1024)
        xin[NPT * 128:NPT * 128 + 16] = xs[c * 16:(c + 1) * 16]
        xin[NTILE * 128:] = xp[b].reshape(16, 2, 128, 1024)[:, 1 - hf].reshape(NPT * 128, 1024)
        msk = np.zeros((128, 2, 128), np.float32)
        msk[:, 0, :] = causal
        msk[:, 1, :] = -30000.0 if hf == 0 else 0.0
        m = dict(shared)
        m["xin"] = xin
        m["pt"] = ptab[c * 16:(c + 1) * 16].reshape(1, 256)
        m["maskadd"] = msk
        m["hfcol"] = np.full((128, 1), 128.0 * (1 - 2 * hf), np.float32)
        in_maps.append(m)
    res = run_bass_kernel_spmd(nc, in_maps, core_ids=list(range(8)))
    R = res.results
    y_prompt = np.zeros((4, 4096, 1024), np.float32)
    nk_p = np.zeros((4, 4096, 1024), np.float32)
    nv_p = np.zeros((4, 4096, 1024), np.float32)
    y_sample = np.zeros((128, 1024), np.float32)
    nk_s = np.zeros((128, 1024), np.float32)
    nv_s = np.zeros((128, 1024), np.float32)
    gv = np.zeros((2, 128, 2048), np.float32)
    for c in range(8):
        b, hf = c // 2, c % 2
        r = R[c]
        y_prompt[b].reshape(16, 2, 128, 1024)[:, hf] = r["y"][:NPT * 128].reshape(16, 128, 1024)
        nk_p[b].reshape(16, 2, 128, 1024)[:, hf] = r["nkp"].reshape(16, 128, 1024)
        nv_p[b].reshape(16, 2, 128, 1024)[:, hf] = r["nvp"].reshape(16, 128, 1024)
        y_sample[c * 16:(c + 1) * 16] = r["y"][NPT * 128:NPT * 128 + 16]
        nk_s[c * 16:(c + 1) * 16] = r["nks"]
        nv_s[c * 16:(c + 1) * 16] = r["nvs"]
        gv[:, c * 16:(c + 1) * 16] = r["gv"]
    return (y_prompt, y_sample.reshape(128, 1, 1024), nk_p.reshape(4, 4096, 16, 64), nv_p.reshape(4, 4096, 8, 128),
            nk_s.reshape(128, 1, 16, 64), nv_s.reshape(128, 1, 8, 128), gv.reshape(2, 128, 1, 2048))
```

```python
import math
import numpy as np
import concourse.bass as bass
import concourse.mybir as mybir
from concourse.bass_utils import run_bass_kernel_spmd

F32 = mybir.dt.float32
BF16 = mybir.dt.bfloat16
I32 = mybir.dt.int32
U32 = mybir.dt.uint32
ALU = mybir.AluOpType
AF = mybir.ActivationFunctionType
AX = mybir.AxisListType

SEM_ROT = 30000
import os as _os
FORCE_SPECIAL = bool(int(_os.environ.get('FORCE_SPECIAL', '0')))
N_DSEM = 72
EPS = 1e-6
NTILE = 17
NALL = 33
NPT = 16
DEPTH = 4
N_EXP = 16384


class Res:
    __slots__ = ("name", "t", "w", "r", "dslot")

    def __init__(self, name, t=None):
        self.name = name
        self.t = t
        self.w = None
        self.r = []
        self.dslot = None


class Prog:
    def __init__(self, nc):
        self.nc = nc
        self.eng = {"pe": nc.tensor, "dve": nc.vector, "act": nc.scalar, "pool": nc.gpsimd, "sync": nc.sync}
        self.sem = {}
        self.cnt = {}
        self.nsem = 0
        self._keep = []
        self._scopes = []
        for k in ("pe", "dve", "act", "pool"):
            self._new_eng_sem(k)
        self.dfree = [[self._alloc_sem(f"d{i}"), 0] for i in range(N_DSEM)]
        self.dall = list(self.dfree)
        self.seen = {k: {} for k in self.eng}
        self.out_events = []
        self.n_inst = 0

    def _alloc_sem(self, name):
        g = self.nc.semaphore(name)
        s = g.__enter__()
        self.nsem += 1
        return s

    def _new_eng_sem(self, k):
        self.sem[k] = self._alloc_sem(f"s_{k}_{self.nsem}")
        self.cnt[k] = 0

    def _reg(self, res, g):
        if self._scopes:
            self._scopes[-1].append((res, g))
        else:
            self._keep.append((res, g))
        return res

    def sb(self, name, shape, dtype):
        self._uid = getattr(self, "_uid", 0) + 1
        name = f"{name}_{self._uid}"
        g = self.nc.sbuf_tensor(name, list(shape), dtype)
        return self._reg(Res(name, g.__enter__()), g)

    def ps(self, name, shape, dtype=F32):
        g = self.nc.psum_tensor(name, list(shape), dtype)
        return self._reg(Res(name, g.__enter__()), g)

    def dram(self, name, shape, dtype):
        t = self.nc.dram_tensor(name, list(shape), dtype, kind="Internal")
        return Res(name, t.ap())

    def view(self, name, ap):
        return Res(name, ap)

    def scope(self):
        prog = self

        class _S:
            def __enter__(s):
                prog._scopes.append([])

            def __exit__(s, *a):
                if a[0] is not None:
                    return False
                prog.barrier()
                items = prog._scopes.pop()
                for res, g in reversed(items):
                    if res.dslot is not None:
                        prog.dfree.append(res.dslot)
                        res.dslot = None
                    g.__exit__(None, None, None)
                return False
        return _S()

    def barrier(self):
        evs = []
        for k in ("pe", "dve", "act", "pool"):
            if self.cnt[k] > 0:
                evs.append((self.sem[k], self.cnt[k]))
        for sl in self.dall:
            if sl[1] > 0:
                evs.append((sl[0], 16 * sl[1]))
        for q in ("pe", "dve", "act", "pool", "sync"):
            for ev in evs:
                self._wait(q, ev)

    def _wait(self, k, ev):
        if ev is None:
            return
        sem, val = ev
        sid = id(sem)
        if self.seen[k].get(sid, 0) >= val:
            return
        self.eng[k].wait_ge(sem, val)
        self.seen[k][sid] = val
        self.n_inst += 1

    @staticmethod
    def _compact(evs):
        best = {}
        for sem, val in evs:
            sid = id(sem)
            if sid not in best or best[sid][1] < val:
                best[sid] = (sem, val)
        return list(best.values())

    def _commit(self, ev, reads, writes):
        for r in reads:
            r.r.append(ev)
            if len(r.r) > 16:
                r.r = self._compact(r.r)
        for w in writes:
            w.w = ev
            w.r = []

    def op(self, k, fn, reads=(), writes=()):
        if self.cnt[k] >= SEM_ROT:
            self._new_eng_sem(k)
        if k == "pe":
            self.seen[k][id(self.sem[k])] = 1 << 60
        for r in reads:
            self._wait(k, r.w)
        for w in writes:
            self._wait(k, w.w)
            for ev in w.r:
                self._wait(k, ev)
        inst = fn(self.eng[k])
        self.cnt[k] += 1
        inst.then_inc(self.sem[k], 1)
        ev = (self.sem[k], self.cnt[k])
        self._commit(ev, reads, writes)
        self.n_inst += 1
        return ev

    def _dma_event(self, prim):
        if prim.dslot is None:
            prim.dslot = self.dfree.pop(0)
        prim.dslot[1] += 1
        return (prim.dslot[0], 16 * prim.dslot[1])

    def _dma_deps(self, q, reads, writes):
        for r in reads:
            self._wait(q, r.w)
        for w in writes:
            if w.w is not None and not (w.dslot is not None and w.w[0] is w.dslot[0]):
                self._wait(q, w.w)
            for ev in w.r:
                self._wait(q, ev)

    def dma(self, q, out_ap, in_ap, reads=(), writes=(), out=False, prim=None, **kw):
        self._dma_deps(q, reads, writes)
        if prim is None:
            prim = (list(writes) + list(reads))[0]
        ev = self._dma_event(prim)
        kw.setdefault("allow_slow_non_contiguous", True)
        self.eng[q].dma_start(out=out_ap, in_=in_ap, **kw).then_inc(ev[0], 16)
        for r in reads:
            r.r.append(ev)
        for w in writes:
            w.w = ev
            w.r = []
        if out:
            self.out_events.append(ev)
        self.n_inst += 1
        return ev

    def gather(self, out_ap, table_ap, idx_ap, reads=(), writes=(), prim=None):
        q = "pool"
        self._dma_deps(q, reads, writes)
        if prim is None:
            prim = list(writes)[0]
        ev = self._dma_event(prim)
        self.nc.gpsimd.indirect_dma_start(
            out=out_ap, out_offset=None, in_=table_ap,
            in_offset=bass.IndirectOffsetOnAxis(ap=idx_ap, axis=0),
        ).then_inc(ev[0], 16)
        for r in reads:
            r.r.append(ev)
        for w in writes:
            w.w = ev
            w.r = []
        self.n_inst += 1
        return ev

    def collective(self, fn, reads, writes):
        q = "pool"
        self._dma_deps(q, reads, writes)
        ev = self._dma_event(list(writes)[0])
        fn(self.nc.gpsimd).then_inc(ev[0], 16)
        for r in reads:
            r.r.append(ev)
        for w in writes:
            w.w = ev
            w.r = []
        return ev

    def finish(self):
        for ev in self._compact(self.out_events):
            self._wait("sync", ev)
        self.barrier()


def _lam_init(l):
    return 0.8 - 0.6 * math.exp(-0.3 * l)


def build_program(stages=("A", "KV", "B"), n_tiles_peer=NALL, cache_rows=2560 * 128, dbg_skip_sample=False, dbg_qtiles=NPT, dbg_att=9):
    nc = bass.Bass("TRN2", target_bir_lowering=False)
    P = Prog(nc)
    stages = list(stages)

    def DI(name, shape, dt=F32):
        return nc.dram_tensor(name, list(shape), dt, kind="ExternalInput").ap()

    def DO(name, shape, dt=F32):
        return nc.dram_tensor(name, list(shape), dt, kind="ExternalOutput").ap()

    xin = DI("xin", [NALL * 128, 1024])
    cache_k = DI("cache_k", [cache_rows, 1024])
    cache_v = DI("cache_v", [cache_rows, 1024])
    pt = DI("pt", [1, 256], I32)
    a_norm_g = DI("a_norm_g", [2, 1024])
    a_w_in = DI("a_w_in", [2, 1024, 4096])
    a_vnorm_g = DI("a_vnorm_g", [2, 2048])
    a_w_s = DI("a_w_s", [2, 8, 128, 128])
    a_b_s = DI("a_b_s", [2, 8, 128])
    a_w_out = DI("a_w_out", [2, 2048, 1024])
    kv_norm_g = DI("kv_norm_g", [1, 1024])
    w_k = DI("w_k", [1024, 1024])
    w_v = DI("w_v", [1024, 1024])
    k_norm_g = DI("k_norm_g", [1, 64])
    b_norm_g = DI("b_norm_g", [2, 1024])
    w_q = DI("w_q", [2, 1024, 1024])
    q_norm_g = DI("q_norm_g", [2, 64])
    lam_in = DI("lam_in", [2, 256])
    subln_g = DI("subln_g", [2, 128])
    w_o = DI("w_o", [2, 1024, 1024])
    f_norm_g = DI("f_norm_g", [4, 1024])
    peer_w_q = DI("peer_w_q", [4, 1024, 2048])
    peer_keys = DI("peer_keys", [4, 16, 128, 128])
    peer_u = DI("peer_u", [4 * N_EXP, 1024])
    peer_v = DI("peer_v", [4 * N_EXP, 1024])
    maskadd = DI("maskadd", [128, 2, 128])
    hfcol_in = DI("hfcol", [128, 1])

    y = DO("y", [NTILE * 128, 1024])
    nkp = DO("nkp", [NPT * 128, 1024])
    nvp = DO("nvp", [NPT * 128, 1024])
    nks = DO("nks", [16, 1024])
    nvs = DO("nvs", [16, 1024])
    gvo = DO("gv", [2, 16, 2048])

    kvx_all = P.dram("kvx_all", [2 * NPT, 128, 2056], BF16)
    PUB = [P.dram(f"pu_b{l}", [N_EXP, 1024], BF16) for l in range(DEPTH)]
    PVB = [P.dram(f"pv_b{l}", [N_EXP, 1024], BF16) for l in range(DEPTH)]
    xp = P.dram("xp", [NPT * 128, 1024], F32)
    XPR = [P.view(f"xp{k}", xp.t[k * 128:(k + 1) * 128, :]) for k in range(NPT)]
    dq = P.dram("dq", [16, 1024], F32)
    dD = P.dram("dD", [16, 16, 129], F32)

    X = [P.sb(f"x{j}", [128, 1024], F32) for j in range(NTILE)]
    ident_f = P.sb("ident_f", [128, 128], F32)
    ident_b = P.sb("ident_b", [128, 128], BF16)
    ss = P.sb("ss", [128, 64], F32)
    iota16 = P.sb("iota16", [128, 16], F32)
    pcol = P.sb("pcol", [128, 1], F32)
    BK = [P.ps(f"bank{b}", [128, 512], F32) for b in range(8)]
    bank_rr = [0]

    def nextbank(lo=0, hi=6):
        b = lo + bank_rr[0] % (hi - lo)
        bank_rr[0] += 1
        return BK[b]

    XS = [P.sb("xs0", [128, 1024], F32)]
    XSB = [XS]
    for j in range(NTILE):
        P.dma("sync", X[j].t[:], xin[j * 128:(j + 1) * 128, :], writes=[X[j]])
    for k in range(NPT):
        P.dma("sync", XPR[k].t, xin[(NTILE + k) * 128:(NTILE + k + 1) * 128, :], writes=[XPR[k]])

    for l in range(DEPTH):
        for (src, dst) in ((peer_u, PUB[l]), (peer_v, PVB[l])):
            for c in range(8):
                P.dma("pool", dst.t[c * 2048:(c + 1) * 2048, :], src[l * N_EXP + c * 2048:l * N_EXP + (c + 1) * 2048, :], writes=[dst])

    def x_begin(j, buf=0):
        if j < NTILE:
            return X[j]
        r = XSB[0][buf]
        P.dma("sync", r.t[:], XPR[j - NTILE].t, reads=[XPR[j - NTILE]], writes=[r])
        return r

    def x_end(j, r):
        if j >= NTILE:
            P.dma("sync", XPR[j - NTILE].t, r.t[:], reads=[r], writes=[XPR[j - NTILE]])
    P.op("pool", lambda e: e.memset(ident_f.t[:], 1.0), writes=[ident_f])
    P.op("pool", lambda e: e.affine_select(out=ident_f.t[:], in_=ident_f.t[:], pattern=[[-1, 128]],
                                            compare_op=ALU.is_equal, fill=0.0, base=0, channel_multiplier=1),
         reads=[ident_f], writes=[ident_f])
    P.op("dve", lambda e: e.tensor_copy(ident_b.t[:], ident_f.t[:]), reads=[ident_f], writes=[ident_b])
    P.op("pool", lambda e: e.iota(iota16.t[:], [[1, 16]], base=0, channel_multiplier=0,
                                   allow_small_or_imprecise_dtypes=True), writes=[iota16])
    P.op("pool", lambda e: e.iota(pcol.t[:], [[0, 1]], base=0, channel_multiplier=1,
                                   allow_small_or_imprecise_dtypes=True), writes=[pcol])

    def rows(j):
        return 16 if j == 16 else 128

    def rstd_of(src_res, src_ap, junk_res, junk_ap, D, col):
        P.op("dve", lambda e: e.scalar_tensor_tensor(out=junk_ap, in0=src_ap, scalar=1.0, in1=src_ap,
                                                     op0=ALU.mult, op1=ALU.mult, accum_out=ss.t[:, col:col + 1]),
             reads=[src_res], writes=[junk_res, ss])
        P.op("dve", lambda e: e.tensor_scalar(out=ss.t[:, col + 1:col + 2], in0=ss.t[:, col:col + 1],
                                              scalar1=1.0 / D, scalar2=EPS, op0=ALU.mult, op1=ALU.add),
             reads=[ss], writes=[ss])
        P.op("act", lambda e: e.activation(ss.t[:, col + 2:col + 3], ss.t[:, col + 1:col + 2], AF.Sqrt),
             reads=[ss], writes=[ss])
        P.op("dve", lambda e: e.reciprocal(ss.t[:, col + 3:col + 4], ss.t[:, col + 2:col + 3]),
             reads=[ss], writes=[ss])
        return ss.t[:, col + 3:col + 4]

    def group_rstd(src_res, src3, sq_res, sq3, n, gd, st_res, base):
        P.op("dve", lambda e: e.tensor_tensor(out=sq3, in0=src3, in1=src3, op=ALU.mult),
             reads=[src_res], writes=[sq_res])
        a = st_res.t[:, base:base + n]
        b = st_res.t[:, base + n:base + 2 * n]
        c = st_res.t[:, base + 2 * n:base + 3 * n]
        P.op("dve", lambda e: e.tensor_reduce(out=a, in_=sq3, axis=AX.X, op=ALU.add), reads=[sq_res], writes=[st_res])
        P.op("dve", lambda e: e.tensor_scalar(out=a, in0=a, scalar1=1.0 / gd, scalar2=EPS, op0=ALU.mult, op1=ALU.add),
             reads=[st_res], writes=[st_res])
        P.op("act", lambda e: e.activation(b, a, AF.Sqrt), reads=[st_res], writes=[st_res])
        P.op("dve", lambda e: e.reciprocal(c, b), reads=[st_res], writes=[st_res])
        return c

    def transposes(dst_res, dst3, src_res, src_fn, n, ident=None, evac="act"):
        ident = ident or ident_b
        for g0 in range(0, n, 8):
            bk = nextbank()
            bv = bk.t[:].bitcast(BF16)
            cnt = min(8, n - g0)
            for i in range(cnt):
                P.op("pe", lambda e, i=i: e.transpose(bv[:, i * 128:(i + 1) * 128], src_fn(g0 + i), ident.t[:]),
                     reads=[src_res, ident], writes=[bk])
            P.op(evac, (lambda e: e.activation(dst3[:, g0:g0 + cnt, :], bv[:, 0:cnt * 128].rearrange("p (a b) -> p a b", b=128), AF.Copy))
                 if evac == "act" else
                 (lambda e: e.tensor_copy(dst3[:, g0:g0 + cnt, :], bv[:, 0:cnt * 128].rearrange("p (a b) -> p a b", b=128))),
                 reads=[bk], writes=[dst_res])

    def load_w_bf16(dst_res, src2d, kc_n):
        v = src2d.rearrange("(kc p) n -> p kc n", p=128)
        for kc in range(kc_n):
            P.dma("pool", dst_res.t[:, kc, :], v[:, kc, :], writes=[dst_res])

    def bcast_load(dst_res, row_ap):
        P.dma("sync", dst_res.t[:], row_ap.partition_broadcast(128), writes=[dst_res])

    def proj(lhsT_res, lhsT3, w_res, w3, kc_n, n_out, evac_fn, bank_lo=0, bank_hi=6):
        for n in range(n_out // 512):
            bk = nextbank(bank_lo, bank_hi)
            for kc in range(kc_n):
                P.op("pe", lambda e, kc=kc: e.matmul(bk.t[:, 0:512], lhsT3[:, kc, :], w3[:, kc, n * 512:(n + 1) * 512],
                                                     start=(kc == 0), stop=(kc == kc_n - 1)),
                     reads=[lhsT_res, w_res], writes=[bk])
            evac_fn(n, bk)

    def x_add(xr, j, n, bk):
        r = rows(j)
        P.op("dve", lambda e: e.tensor_tensor(out=xr.t[0:r, n * 512:(n + 1) * 512], in0=xr.t[0:r, n * 512:(n + 1) * 512],
                                              in1=bk.t[0:r, 0:512], op=ALU.add),
             reads=[xr, bk], writes=[xr])

    def gmlp_layer(l):
        with P.scope():
            w_in = P.sb("w_in", [128, 8, 4096], BF16)
            w_out = P.sb("w_out", [128, 16, 1024], BF16)
            g_v = P.sb("g_v", [128, 2048], F32)
            ga = P.sb("ga", [128, 8], F32)
            wsT = P.sb("wsT", [128, 8, 128], BF16)
            wsS = P.sb("wsS", [128, 8, 128], BF16)
            w00 = P.sb("w00", [128, 8, 1], F32)
            bsP = P.sb("bsP", [128, 8], F32)
            bsS = P.sb("bsS", [128, 8, 1], F32)
            u = P.sb("u", [128, 2048], BF16)
            v = P.sb("v", [128, 2048], F32)
            vg = P.sb("vg", [128, 2048], BF16)
            hnT = P.view("hnT", vg.t[:, 0:1024].rearrange("p (a b) -> p a b", b=128))
            gT = P.sb("gT", [128, 16, 128], BF16)

            load_w_bf16(w_in, a_w_in[l], 8)
            load_w_bf16(w_out, a_w_out[l], 16)
            bcast_load(g_v, a_vnorm_g[l])
            P.dma("sync", ga.t[:], a_norm_g[l].rearrange("(kc p) -> p kc", p=128), writes=[ga], allow_slow_non_contiguous=True)
            for kc in range(8):
                P.op("dve", lambda e, kc=kc: e.tensor_scalar(out=w_in.t[:, kc, :], in0=w_in.t[:, kc, :], scalar1=ga.t[:, kc:kc + 1],
                                                             scalar2=None, op0=ALU.mult), reads=[w_in, ga], writes=[w_in])
            wsf3 = v.t[:, 0:1024].rearrange("p (g s) -> p g s", s=128)
            wsb3 = vg.t[:, 0:1024].rearrange("p (g s) -> p g s", s=128)
            P.dma("sync", wsf3, a_w_s[l].rearrange("g t s -> t g s"), writes=[v])
            P.op("pool", lambda e: e.affine_select(out=wsf3, in_=wsf3, pattern=[[0, 8], [-1, 128]], compare_op=ALU.is_ge,
                                                    fill=0.0, base=0, channel_multiplier=1), reads=[v], writes=[v])
            P.op("dve", lambda e: e.tensor_copy(wsb3, wsf3), reads=[v], writes=[vg])
            transposes(wsT, wsT.t, vg, lambda i: wsb3[:, i, :], 8)
            P.dma("sync", w00.t[:], a_w_s[l, :, 0, 0:1].partition_broadcast(128), writes=[w00])
            for g in range(8):
                P.op("dve", lambda e, g=g: e.tensor_scalar(out=wsS.t[:, g, :], in0=ident_f.t[:], scalar1=w00.t[:, g, 0:1], scalar2=None,
                                                           op0=ALU.mult), reads=[ident_f, w00], writes=[wsS])
            P.dma("sync", bsP.t[:], a_b_s[l].rearrange("g t -> t g"), writes=[bsP], allow_slow_non_contiguous=True)
            P.dma("sync", bsS.t[:], a_b_s[l, :, 0:1].partition_broadcast(128), writes=[bsS])

            for j in range(NALL):
                xt = x_begin(j)
                hn_ap = gT.t[:, 0:8, :].rearrange("p a b -> p (a b)")
                rs = rstd_of(xt, xt.t[:], v, v.t[:, 0:1024], 1024, 0)
                P.op("dve", lambda e: e.tensor_scalar(out=hn_ap, in0=xt.t[:], scalar1=rs, scalar2=None, op0=ALU.mult),
                     reads=[xt, ss], writes=[gT])
                transposes(vg, hnT.t, gT, lambda i: hn_ap[:, i * 128:(i + 1) * 128], 8)

                def evac_z(n, bk):
                    if n < 4:
                        P.op("act", lambda e: e.activation(u.t[:, n * 512:(n + 1) * 512], bk.t[:, 0:512], AF.Gelu), reads=[bk], writes=[u])
                    else:
                        P.op("act", lambda e: e.activation(v.t[:, (n - 4) * 512:(n - 3) * 512], bk.t[:, 0:512], AF.Gelu), reads=[bk], writes=[v])
                proj(vg, hnT.t, w_in, w_in.t, 8, 4096, evac_z)
                rs2 = rstd_of(v, v.t[:], gT, gT.t[:].rearrange("p a b -> p (a b)"), 2048, 4)
                P.op("dve", lambda e: e.scalar_tensor_tensor(out=v.t[:], in0=v.t[:], scalar=rs2, in1=g_v.t[:], op0=ALU.mult, op1=ALU.mult),
                     reads=[v, ss, g_v], writes=[v])
                if j == 16:
                    P.dma("sync", gvo[l], v.t[0:16, :], reads=[v], out=True)
                P.op("act", lambda e: e.activation(vg.t[:], v.t[:], AF.Copy), reads=[v], writes=[vg])
                wmix = wsS if j == 16 else wsT
                bmix = (lambda g: bsS.t[:, g, 0:1]) if j == 16 else (lambda g: bsP.t[:, g:g + 1])
                for g in range(8):
                    bk = nextbank()
                    P.op("pe", lambda e, g=g: e.matmul(bk.t[:, 0:256], wmix.t[:, g, :], vg.t[:, g * 256:(g + 1) * 256], start=True, stop=True),
                         reads=[wmix, vg], writes=[bk])
                    P.op("dve", lambda e, g=g: e.scalar_tensor_tensor(out=vg.t[:, g * 256:(g + 1) * 256], in0=bk.t[:, 0:256], scalar=bmix(g),
                                                                      in1=u.t[:, g * 256:(g + 1) * 256], op0=ALU.add, op1=ALU.mult),
                         reads=[bk, u, bsP, bsS], writes=[vg])
                transposes(gT, gT.t, vg, lambda i: vg.t[:, i * 128:(i + 1) * 128], 16)
                proj(gT, gT.t, w_out, w_out.t, 16, 1024, lambda n, bk: x_add(xt, j, n, bk))
                x_end(j, xt)

    NB = 10

    def peer_layer(l):
        with P.scope():
            pwq = P.sb("pwq", [128, 8, 2048], BF16)
            kf = P.sb("kf", [128, 16, 128], F32)
            kb16 = P.sb("kb16", [128, 16, 128], BF16)
            keysT = P.sb("keysT", [128, 16, 128], BF16)
            g_f = P.sb("g_f", [128, 1024], F32)
            HN = [P.sb(f"hn{i}", [128, 1024], F32) for i in range(2)]
            EIDX = [P.sb(f"eidx{i}", [128, 128], I32) for i in range(2)]
            GW = [P.sb(f"gw{i}", [128, 128], F32) for i in range(2)]
            hnb = P.sb("hnb", [128, 1024], BF16)
            hnT = P.sb("hnT", [128, 8, 128], BF16)
            qb = P.sb("qb", [128, 2048], BF16)
            qT = P.sb("qT", [128, 16, 128], BF16)
            sc = P.sb("sc", [128, 16, 128], F32)
            wk = P.sb("wk", [128, 256], F32)
            sv = P.sb("sv", [128, 16, 16], F32)
            si = P.sb("si", [128, 16, 16], U32)
            sif = P.sb("sif", [128, 16, 16], F32)
            cand = P.sb("cand", [128, 8, 256], F32)
            tops = P.sb("tops", [128, 8, 16], F32)
            pos = P.sb("pos", [128, 8, 16], U32)
            posf = P.sb("posf", [128, 8, 16], F32)
            af = P.sb("af", [128, 8, 16], F32)
            bf = P.sb("bf", [128, 8, 16], F32)
            i0 = P.sb("i0", [128, 8, 16], F32)
            i1 = P.sb("i1", [128, 8, 16], F32)
            ex = P.sb("ex", [128, 8, 16], F32)
            zz = P.sb("zz", [128, 16], F32)
            actp = P.sb("actp", [128, 128], F32)
            wgt = P.sb("wgt", [128, 128], F32)
            junk = P.sb("junk", [128, 1024], F32)
            GB = [P.sb(f"gb{i}", [128, 1024], BF16) for i in range(NB)]
            DG = [P.sb(f"dg{i}", [128, 128], BF16) for i in range(3)]

            load_w_bf16(pwq, peer_w_q[l], 8)
            bcast_load(g_f, f_norm_g[l])
            P.dma("sync", kf.t[:], peer_keys[l].rearrange("c n d -> n c d"), writes=[kf])
            P.op("dve", lambda e: e.tensor_copy(kb16.t[:], kf.t[:]), reads=[kf], writes=[kb16])
            transposes(keysT, keysT.t, kb16, lambda i: kb16.t[:, i, :], 16)

            if l < 2:
                XSB[0] = [XS[0], P.sb("xs1", [128, 1024], F32)]
            XT = {}

            def routing(j):
                xt = XT[j] = x_begin(j, j % 2) if j >= NTILE else X[j]
                hn, eidx, gw = HN[j % 2], EIDX[j % 2], GW[j % 2]
                rs = rstd_of(xt, xt.t[:], junk, junk.t[:], 1024, 8)
                P.op("dve", lambda e: e.scalar_tensor_tensor(out=hn.t[:], in0=xt.t[:], scalar=rs, in1=g_f.t[:], op0=ALU.mult, op1=ALU.mult),
                     reads=[xt, ss, g_f], writes=[hn])
                P.op("act", lambda e: e.activation(hnb.t[:], hn.t[:], AF.Copy), reads=[hn], writes=[hnb])
                transposes(hnT, hnT.t, hnb, lambda i: hnb.t[:, i * 128:(i + 1) * 128], 8)
                yield
                proj(hnT, hnT.t, pwq, pwq.t, 8, 2048,
                     lambda n, bk: P.op("act", lambda e: e.activation(qb.t[:, n * 512:(n + 1) * 512], bk.t[:, 0:512], AF.Copy),
                                        reads=[bk], writes=[qb]))
                yield
                transposes(qT, qT.t, qb, lambda i: qb.t[:, i * 128:(i + 1) * 128], 16)
                yield
                for g4 in range(4):
                    bk = nextbank()
                    for i in range(4):
                        hc = g4 * 4 + i
                        P.op("pe", lambda e, hc=hc, i=i: e.matmul(bk.t[:, i * 128:(i + 1) * 128], qT.t[:, hc, :], keysT.t[:, hc, :], start=True, stop=True),
                             reads=[qT, keysT], writes=[bk])
                    P.op("act", lambda e, g4=g4: e.activation(sc.t[:, g4 * 4:(g4 + 1) * 4, :], bk.t[:, 0:512].rearrange("p (a b) -> p a b", b=128), AF.Copy),
                         reads=[bk], writes=[sc])
                    yield
                for hc in range(16):
                    P.op("dve", lambda e, hc=hc: e.max(sv.t[:, hc, 0:8], sc.t[:, hc, :]), reads=[sc], writes=[sv])
                    P.op("dve", lambda e, hc=hc: e.max_index(si.t[:, hc, 0:8], sv.t[:, hc, 0:8], sc.t[:, hc, :]), reads=[sc, sv], writes=[si])
                    P.op("dve", lambda e, hc=hc: e.match_replace(wk.t[:, 0:128], sv.t[:, hc, 0:8], sc.t[:, hc, :], -1e30), reads=[sc, sv], writes=[wk])
                    P.op("dve", lambda e, hc=hc: e.max(sv.t[:, hc, 8:16], wk.t[:, 0:128]), reads=[wk], writes=[sv])
                    P.op("dve", lambda e, hc=hc: e.max_index(si.t[:, hc, 8:16], sv.t[:, hc, 8:16], wk.t[:, 0:128]), reads=[wk, sv], writes=[si])
                    yield
                P.op("dve", lambda e: e.tensor_copy(sif.t[:], si.t[:]), reads=[si], writes=[sif])
                sv4 = sv.t[:].rearrange("p (h c) k -> p h c k", c=2)
                sif4 = sif.t[:].rearrange("p (h c) k -> p h c k", c=2)
                cand4 = cand.t[:].rearrange("p h (a b) -> p h a b", b=16)
                P.op("dve", lambda e: e.tensor_tensor(out=cand4, in0=sv4[:, :, 0, :].unsqueeze(3).to_broadcast([128, 8, 16, 16]),
                                                      in1=sv4[:, :, 1, :].unsqueeze(2).to_broadcast([128, 8, 16, 16]), op=ALU.add),
                     reads=[sv], writes=[cand])
                for h in range(8):
                    P.op("dve", lambda e, h=h: e.max(tops.t[:, h, 0:8], cand.t[:, h, :]), reads=[cand], writes=[tops])
                    P.op("dve", lambda e, h=h: e.max_index(pos.t[:, h, 0:8], tops.t[:, h, 0:8], cand.t[:, h, :]), reads=[cand, tops], writes=[pos])
                    P.op("dve", lambda e, h=h: e.match_replace(wk.t[:], tops.t[:, h, 0:8], cand.t[:, h, :], -1e30), reads=[cand, tops], writes=[wk])
                    P.op("dve", lambda e, h=h: e.max(tops.t[:, h, 8:16], wk.t[:]), reads=[wk], writes=[tops])
                    P.op("dve", lambda e, h=h: e.max_index(pos.t[:, h, 8:16], tops.t[:, h, 8:16], wk.t[:]), reads=[wk, tops], writes=[pos])
                    yield
                P.op("dve", lambda e: e.tensor_tensor(out=ex.t[:], in0=tops.t[:], in1=tops.t[:, :, 0:1].to_broadcast([128, 8, 16]), op=ALU.subtract),
                     reads=[tops], writes=[ex])
                P.op("act", lambda e: e.activation(ex.t[:], ex.t[:], AF.Exp), reads=[ex], writes=[ex])
                P.op("dve", lambda e: e.tensor_reduce(out=zz.t[:, 0:8], in_=ex.t[:], axis=AX.X, op=ALU.add), reads=[ex], writes=[zz])
                P.op("dve", lambda e: e.reciprocal(zz.t[:, 8:16], zz.t[:, 0:8]), reads=[zz], writes=[zz])
                P.op("dve", lambda e: e.tensor_tensor(out=gw.t[:].rearrange("p (h k) -> p h k", k=16), in0=ex.t[:],
                                                      in1=zz.t[:, 8:16].unsqueeze(2).to_broadcast([128, 8, 16]), op=ALU.mult),
                     reads=[ex, zz], writes=[gw])
                yield
                P.op("dve", lambda e: e.tensor_copy(posf.t[:], pos.t[:]), reads=[pos], writes=[posf])
                P.op("dve", lambda e: e.tensor_scalar(out=af.t[:], in0=posf.t[:], scalar1=16.0, scalar2=None, op0=ALU.is_ge), reads=[posf], writes=[af])
                for k in range(2, 16):
                    P.op("dve", lambda e, k=k: e.scalar_tensor_tensor(out=af.t[:], in0=posf.t[:], scalar=16.0 * k, in1=af.t[:], op0=ALU.is_ge, op1=ALU.add),
                         reads=[posf, af], writes=[af])
                P.op("dve", lambda e: e.scalar_tensor_tensor(out=bf.t[:], in0=af.t[:], scalar=-16.0, in1=posf.t[:], op0=ALU.mult, op1=ALU.add),
                     reads=[posf, af], writes=[bf])
                yield
                oh = P.view("oh", cand.t[:].rearrange("p h (a b) -> p h a b", b=16))
                io4 = iota16.t[:].unsqueeze(1).unsqueeze(1).to_broadcast([128, 8, 16, 16])
                for (sel, c, dst) in ((af, 0, i0), (bf, 1, i1)):
                    P.op("dve", lambda e, sel=sel: e.tensor_tensor(out=oh.t[:], in0=sel.t[:].unsqueeze(3).to_broadcast([128, 8, 16, 16]), in1=io4, op=ALU.is_equal),
                         reads=[sel, iota16], writes=[cand])
                    P.op("dve", lambda e, c=c: e.tensor_tensor(out=oh.t[:], in0=oh.t[:], in1=sif4[:, :, c, :].unsqueeze(2).to_broadcast([128, 8, 16, 16]), op=ALU.mult),
                         reads=[cand, sif], writes=[cand])
                    P.op("dve", lambda e, dst=dst: e.tensor_reduce(out=dst.t[:], in_=oh.t[:], axis=AX.X, op=ALU.add), reads=[cand], writes=[dst])
                    yield
                P.op("dve", lambda e: e.tensor_scalar(out=i0.t[:], in0=i0.t[:], scalar1=128.0, scalar2=None, op0=ALU.mult),
                     reads=[i0], writes=[i0])
                P.op("dve", lambda e: e.tensor_tensor(out=eidx.t[:].rearrange("p (h k) -> p h k", k=16), in0=i0.t[:], in1=i1.t[:], op=ALU.add),
                     reads=[i0, i1], writes=[eidx])

            gb_rr = [0]

            def experts(j, gen=None):
                hn, eidx, gw = HN[j % 2], EIDX[j % 2], GW[j % 2]
                for s in range(128):
                    gb = GB[gb_rr[0] % NB]
                    gb_rr[0] += 1
                    P.gather(gb.t[:], PUB[l].t, eidx.t[:, s:s + 1], reads=[eidx, PUB[l]], writes=[gb])
                    P.op("dve", lambda e, s=s, gb=gb: e.scalar_tensor_tensor(out=junk.t[:], in0=gb.t[:], scalar=1.0, in1=hn.t[:], op0=ALU.mult, op1=ALU.mult,
                                                                             accum_out=actp.t[:, s:s + 1]),
                         reads=[gb, hn], writes=[junk, actp])
                    if gen is not None and s >= 8:
                        next(gen, None)
                P.op("act", lambda e: e.activation(wgt.t[:], actp.t[:], AF.Gelu), reads=[actp], writes=[wgt])
                P.op("dve", lambda e: e.tensor_tensor(out=wgt.t[:], in0=wgt.t[:], in1=gw.t[:], op=ALU.mult), reads=[wgt, gw], writes=[wgt])
                for s in range(128):
                    gb = GB[gb_rr[0] % NB]
                    gb_rr[0] += 1
                    dg = DG[s % 3]
                    P.gather(gb.t[:], PVB[l].t, eidx.t[:, s:s + 1], reads=[eidx, PVB[l]], writes=[gb])
                    P.op("dve", lambda e, s=s, dg=dg: e.tensor_scalar(out=dg.t[:], in0=ident_f.t[:], scalar1=wgt.t[:, s:s + 1], scalar2=None, op0=ALU.mult),
                         reads=[ident_f, wgt], writes=[dg])
                    for n in range(2):
                        P.op("pe", lambda e, n=n, s=s, dg=dg, gb=gb: e.matmul(BK[6 + n].t[:, 0:512], dg.t[:], gb.t[:, n * 512:(n + 1) * 512],
                                                                               start=(s == 0), stop=(s == 127)),
                             reads=[dg, gb], writes=[BK[6 + n]])
                for n in range(2):
                    x_add(XT[j], j, n, BK[6 + n])
                x_end(j, XT[j])

            nt = min(n_tiles_peer, NALL if l < 2 else NTILE)
            for _ in routing(0):
                pass
            for j in range(nt):
                gen = routing(j + 1) if j + 1 < nt else None
                experts(j, gen)
                if gen is not None:
                    for _ in gen:
                        pass
        XSB[0] = XS

    def kv_phase():
        with P.scope():
            wk_ = P.sb("w_k", [128, 8, 1024], BF16)
            wv_ = P.sb("w_v", [128, 8, 1024], BF16)
            g_kv = P.sb("g_kv", [128, 1024], F32)
            g_kn = P.sb("g_kn", [128, 64], F32)
            hnb = P.sb("hnb", [128, 1024], BF16)
            hnT = P.sb("hnT", [128, 8, 128], BF16)
            KF = [P.sb(f"kf{i}", [128, 1024], F32) for i in range(2)]
            VF = [P.sb(f"vf{i}", [128, 1024], F32) for i in range(2)]
            sq = P.sb("sq", [128, 1024], F32)
            st = P.sb("st", [128, 48], F32)
            knb = P.sb("knb", [128, 8, 2, 64], BF16)
            KT = [P.sb(f"kt{i}", [128, 8, 128], BF16) for i in range(2)]
            VE = [P.sb(f"ve{i}", [128, 8, 129], BF16) for i in range(2)]
            load_w_bf16(wk_, w_k, 8)
            load_w_bf16(wv_, w_v, 8)
            bcast_load(g_kv, kv_norm_g[0])
            bcast_load(g_kn, k_norm_g[0])
            for i in range(2):
                P.op("pool", lambda e, i=i: e.memset(VE[i].t[:], 1.0), writes=[VE[i]])
            for j in range(NALL):
                xt = x_begin(j)
                own = j < NPT
                slot = 2 * j if own else 2 * (j - NTILE) + 1
                kf = kn_s if j == 16 else KF[j % 2]
                vf = v_s if j == 16 else VF[j % 2]
                rs = rstd_of(xt, xt.t[:], sq, sq.t[:], 1024, 12)
                P.op("dve", lambda e: e.scalar_tensor_tensor(out=hnb.t[:], in0=xt.t[:], scalar=rs, in1=g_kv.t[:], op0=ALU.mult, op1=ALU.mult),
                     reads=[xt, ss, g_kv], writes=[hnb])
                transposes(hnT, hnT.t, hnb, lambda i: hnb.t[:, i * 128:(i + 1) * 128], 8)
                proj(hnT, hnT.t, wk_, wk_.t, 8, 1024,
                     lambda n, bk: P.op("act", lambda e: e.activation(kf.t[:, n * 512:(n + 1) * 512], bk.t[:, 0:512], AF.Copy), reads=[bk], writes=[kf]))
                proj(hnT, hnT.t, wv_, wv_.t, 8, 1024,
                     lambda n, bk: P.op("act", lambda e: e.activation(vf.t[:, n * 512:(n + 1) * 512], bk.t[:, 0:512], AF.Copy), reads=[bk], writes=[vf]))
                k3 = kf.t[:].rearrange("p (a b) -> p a b", b=64)
                rg = group_rstd(kf, k3, sq, sq.t[:].rearrange("p (a b) -> p a b", b=64), 16, 64, st, 0)
                P.op("dve", lambda e: e.tensor_tensor(out=k3, in0=k3, in1=rg.unsqueeze(2).to_broadcast([128, 16, 64]), op=ALU.mult),
                     reads=[kf, st], writes=[kf])
                P.op("dve", lambda e: e.tensor_tensor(out=k3, in0=k3, in1=g_kn.t[:].unsqueeze(1).to_broadcast([128, 16, 64]), op=ALU.mult),
                     reads=[kf, g_kn], writes=[kf])
                if j == 16:
                    P.dma("sync", nks, kf.t[0:16, :], reads=[kf], out=True)
                    P.dma("sync", nvs, vf.t[0:16, :], reads=[vf], out=True)
                    continue
                if own:
                    P.dma("sync", nkp[j * 128:(j + 1) * 128, :], kf.t[:], reads=[kf], out=True)
                    P.dma("sync", nvp[j * 128:(j + 1) * 128, :], vf.t[:], reads=[vf], out=True)
                kt, ve = KT[j % 2], VE[j % 2]
                P.op("act", lambda e: e.activation(knb.t[:], kf.t[:].rearrange("p (m h d) -> p h m d", m=2, d=64), AF.Copy), reads=[kf], writes=[knb])
                transposes(kt, kt.t, knb, lambda h: knb.t[:, h, :, :].rearrange("p m d -> p (m d)"), 8)
                P.op("act", lambda e: e.activation(ve.t[:, :, 0:128], vf.t[:].rearrange("p (h d) -> p h d", d=128), AF.Copy), reads=[vf], writes=[ve])
                P.dma("sync", kvx_all.t[slot, :, 0:1024], kt.t[:].rearrange("p a b -> p (a b)"), reads=[kt], writes=[kvx_all], prim=kvx_all)
                P.dma("sync", kvx_all.t[slot, :, 1024:2056], ve.t[:].rearrange("p a b -> p (a b)"), reads=[ve], writes=[kvx_all], prim=kvx_all)

    def attn_layer(li):
        l = 2 + li
        lam0 = _lam_init(l)
        slopes = [2.0 ** (-8.0 * (h + 1) / 8) for h in range(8)]
        with P.scope():
            wq_ = P.sb("w_q", [128, 8, 1024], BF16)
            wo_ = P.sb("w_o", [128, 8, 1024], BF16)
            g_b = P.sb("g_b", [128, 1024], F32)
            g_qn = P.sb("g_qn", [128, 64], F32)
            g_sub = P.sb("g_sub", [128, 128], F32)
            lamv = P.sb("lamv", [128, 256], F32)
            lamc = P.sb("lamc", [128, 8], F32)
            hfc = P.sb("hfc", [128, 1], F32)
            biasT = P.sb("biasT", [128, 16, 2, 8], F32)
            bbase = P.sb("bbase", [128, 16, 2], F32)
            sbias = P.sb("sbias", [128, 16, 16], F32)
            sbase = P.sb("sbase", [128, 16], F32)
            mskf = P.sb("mskf", [128, 2, 128], F32)
            mskb = P.sb("mskb", [128, 2, 128], BF16)
            ptb = P.sb("ptb", [128, 256], I32)
            rowidx = P.sb("rowidx", [128, 256], I32)
            ones1 = P.sb("ones1", [128, 1], F32)
            bmask = P.sb("bmask", [128, 8], F32)
            hnb = P.sb("hnb", [128, 1024], BF16)
            hnT = P.sb("hnT", [128, 8, 128], BF16)
            qf = P.sb("qf", [128, 1024], F32)
            sq = P.sb("sq", [128, 1024], F32)
            st = P.sb("st", [128, 64], F32)
            qnb = P.sb("qnb", [128, 8, 2, 64], BF16)
            QT = P.sb("QT", [128, 8, 128], BF16)
            QTm = [P.sb(f"QTm{m}", [128, 8, 128], BF16) for m in range(2)]
            KV = [P.sb(f"kv{i}", [128, 2056], BF16) for i in range(3)]
            PT = [P.sb(f"pt{i}", [128, 256], BF16) for i in range(4)]
            oev = P.sb("oev", [128, 16, 129], F32)
            rz = P.sb("rz", [128, 16], F32)
            o0 = P.sb("o0", [128, 8, 128], F32)
            o1 = P.sb("o1", [128, 8, 128], F32)
            onb = P.sb("onb", [128, 1024], BF16)
            oT = P.sb("oT", [128, 8, 128], BF16)
            qbc = P.sb("qbc", [128, 1024], F32)
            KP = [P.sb(f"kp{i}", [128, 1024], F32) for i in range(2)]
            VP = [P.sb(f"vp{i}", [128, 1024], F32) for i in range(2)]
            sc16 = P.sb("sc16", [128, 32], F32)
            E16 = [P.sb(f"e16_{i}", [128, 16], F32) for i in range(2)]
            dall = P.sb("dall", [16, 16, 129], F32)
            enew = P.sb("enew", [128, 16], F32)
            ST = [P.view(f"st{i}", BK[6 + i].t[:, 0:256]) for i in range(2)]

            load_w_bf16(wq_, w_q[li], 8)
            load_w_bf16(wo_, w_o[li], 8)
            bcast_load(g_b, b_norm_g[li])
            bcast_load(g_qn, q_norm_g[li])
            bcast_load(g_sub, subln_g[li])
            bcast_load(lamv, lam_in[li])
            P.dma("sync", hfc.t[:], hfcol_in, writes=[hfc])
            P.dma("sync", mskf.t[:], maskadd, writes=[mskf])
            P.op("dve", lambda e: e.tensor_copy(mskb.t[:], mskf.t[:]), reads=[mskf], writes=[mskb])
            P.dma("sync", ptb.t[:], pt[0].partition_broadcast(128), writes=[ptb])
            P.op("dve", lambda e: e.scalar_tensor_tensor(out=rowidx.t[:], in0=ptb.t[:], scalar=128.0, in1=pcol.t[:, 0:1].to_broadcast([128, 256]),
                                                         op0=ALU.mult, op1=ALU.add), reads=[ptb, pcol], writes=[rowidx])
            P.op("pool", lambda e: e.memset(ones1.t[:], 1.0), writes=[ones1])
            P.op("dve", lambda e: e.scalar_tensor_tensor(out=sq.t[:, 0:64], in0=lamv.t[:, 0:64], scalar=1.0, in1=lamv.t[:, 64:128], op0=ALU.mult, op1=ALU.mult,
                                                         accum_out=lamc.t[:, 0:1]), reads=[lamv], writes=[sq, lamc])
            P.op("dve", lambda e: e.scalar_tensor_tensor(out=sq.t[:, 0:64], in0=lamv.t[:, 128:192], scalar=1.0, in1=lamv.t[:, 192:256], op0=ALU.mult, op1=ALU.mult,
                                                         accum_out=lamc.t[:, 1:2]), reads=[lamv], writes=[sq, lamc])
            P.op("act", lambda e: e.activation(lamc.t[:, 2:4], lamc.t[:, 0:2], AF.Exp), reads=[lamc], writes=[lamc])
            P.op("dve", lambda e: e.tensor_tensor(out=lamc.t[:, 4:5], in0=lamc.t[:, 3:4], in1=lamc.t[:, 2:3], op=ALU.subtract), reads=[lamc], writes=[lamc])
            P.op("dve", lambda e: e.tensor_scalar(out=lamc.t[:, 4:5], in0=lamc.t[:, 4:5], scalar1=-lam0, scalar2=None, op0=ALU.add), reads=[lamc], writes=[lamc])
            neg_lam = lamc.t[:, 4:5]
            P.op("pool", lambda e: e.iota(bbase.t[:], [[-256, 16], [0, 2]], base=-64, channel_multiplier=1, allow_small_or_imprecise_dtypes=True), writes=[bbase])
            P.op("dve", lambda e: e.tensor_scalar(out=bbase.t[:, :, 1], in0=bbase.t[:, :, 1], scalar1=hfc.t[:, 0:1], scalar2=None, op0=ALU.add), reads=[bbase, hfc], writes=[bbase])
            for h in range(8):
                P.op("dve", lambda e, h=h: e.tensor_scalar(out=biasT.t[:, :, :, h], in0=bbase.t[:], scalar1=slopes[h], scalar2=None, op0=ALU.mult),
                     reads=[bbase], writes=[biasT])
            P.op("pool", lambda e: e.iota(sbase.t[:], [[128, 16]], base=-2048, channel_multiplier=1, allow_small_or_imprecise_dtypes=True), writes=[sbase])
            for mh in range(16):
                P.op("dve", lambda e, mh=mh: e.tensor_scalar(out=sbias.t[:, :, mh], in0=sbase.t[:], scalar1=slopes[mh % 8], scalar2=None, op0=ALU.mult),
                     reads=[sbase], writes=[sbias])
            P.op("dve", lambda e: e.tensor_tensor(out=bmask.t[0:16, :], in0=ident_f.t[0:16, 0:8], in1=ident_f.t[0:16, 8:16], op=ALU.add),
                 reads=[ident_f], writes=[bmask])

            for m in range(2):
                P.op("pool", lambda e, m=m: e.memset(QTm[m].t[:], 0.0), writes=[QTm[m]])

            def q_side(j):
                xt = X[j]
                rs = rstd_of(xt, xt.t[:], sq, sq.t[:], 1024, 16)
                P.op("dve", lambda e: e.scalar_tensor_tensor(out=hnb.t[:], in0=xt.t[:], scalar=rs, in1=g_b.t[:], op0=ALU.mult, op1=ALU.mult),
                     reads=[xt, ss, g_b], writes=[hnb])
                transposes(hnT, hnT.t, hnb, lambda i: hnb.t[:, i * 128:(i + 1) * 128], 8)
                proj(hnT, hnT.t, wq_, wq_.t, 8, 1024,
                     lambda n, bk: P.op("act", lambda e: e.activation(qf.t[:, n * 512:(n + 1) * 512], bk.t[:, 0:512], AF.Copy), reads=[bk], writes=[qf]))
                q3 = qf.t[:].rearrange("p (a b) -> p a b", b=64)
                rg = group_rstd(qf, q3, sq, sq.t[:].rearrange("p (a b) -> p a b", b=64), 16, 64, st, 0)
                P.op("dve", lambda e: e.tensor_tensor(out=q3, in0=q3, in1=rg.unsqueeze(2).to_broadcast([128, 16, 64]), op=ALU.mult),
                     reads=[qf, st], writes=[qf])
                P.op("dve", lambda e: e.tensor_tensor(out=q3, in0=q3, in1=g_qn.t[:].unsqueeze(1).to_broadcast([128, 16, 64]), op=ALU.mult),
                     reads=[qf, g_qn], writes=[qf])

            def epilogue(j):
                P.op("dve", lambda e: e.reciprocal(rz.t[:], oev.t[:, :, 128]), reads=[oev], writes=[rz])
                P.op("dve", lambda e: e.tensor_tensor(out=o0.t[:], in0=oev.t[:, 0:8, 0:128], in1=rz.t[:, 0:8].unsqueeze(2).to_broadcast([128, 8, 128]), op=ALU.mult),
                     reads=[oev, rz], writes=[o0])
                P.op("dve", lambda e: e.tensor_tensor(out=o1.t[:], in0=oev.t[:, 8:16, 0:128], in1=rz.t[:, 8:16].unsqueeze(2).to_broadcast([128, 8, 128]), op=ALU.mult),
                     reads=[oev, rz], writes=[o1])
                of = o0.t[:].rearrange("p a b -> p (a b)")
                P.op("dve", lambda e: e.scalar_tensor_tensor(out=of, in0=o1.t[:].rearrange("p a b -> p (a b)"), scalar=neg_lam, in1=of, op0=ALU.mult, op1=ALU.add),
                     reads=[o0, o1, lamc], writes=[o0])
                rg = group_rstd(o0, o0.t[:], o1, o1.t[:], 8, 128, st, 48 - 24)
                P.op("dve", lambda e: e.tensor_tensor(out=o0.t[:], in0=o0.t[:], in1=rg.unsqueeze(2).to_broadcast([128, 8, 128]), op=ALU.mult),
                     reads=[o0, st], writes=[o0])
                P.op("dve", lambda e: e.scalar_tensor_tensor(out=onb.t[:].rearrange("p (a b) -> p a b", b=128), in0=o0.t[:], scalar=1.0 - lam0,
                                                             in1=g_sub.t[:].unsqueeze(1).to_broadcast([128, 8, 128]), op0=ALU.mult, op1=ALU.mult),
                     reads=[o0, g_sub], writes=[onb])
                transposes(oT, oT.t, onb, lambda i: onb.t[:, i * 128:(i + 1) * 128], 8)
                proj(oT, oT.t, wo_, wo_.t, 8, 1024, lambda n, bk: x_add(X[j], j, n, bk))

            kv_rr = [0]
            pt_rr = [0]
            for i in range(dbg_qtiles):
                q_side(i)
                P.op("act", lambda e: e.activation(qnb.t[:], qf.t[:].rearrange("p (m h d) -> p h m d", m=2, d=64), AF.Copy), reads=[qf], writes=[qnb])
                transposes(QT, QT.t, qnb, lambda h: qnb.t[:, h, :, :].rearrange("p m d -> p (m d)"), 8)
                for m in range(2):
                    P.op("dve", lambda e, m=m: e.tensor_copy(QTm[m].t[m * 64:(m + 1) * 64], QT.t[m * 64:(m + 1) * 64]), reads=[QT], writes=[QTm[m]])
                nkb = 2 * i + 2
                if dbg_att < 2:
                    continue
                bank_started = set()
                for kb in range(nkb):
                    kvb = KV[kv_rr[0] % 3]
                    kv_rr[0] += 1
                    P.dma("sync", kvb.t[:], kvx_all.t[kb], reads=[kvx_all], writes=[kvb])
                    ip, w = kb // 2, kb % 2
                    di = i - ip
                    special = (ip == i) or FORCE_SPECIAL
                    for h in range(8):
                        stv = ST[pt_rr[0] % 2]
                        ptb_ = PT[pt_rr[0] % 4]
                        pt_rr[0] += 1
                        for m in range(2):
                            P.op("pe", lambda e, m=m, h=h: e.matmul(stv.t[:, m * 128:(m + 1) * 128], kvb.t[:, h * 128:(h + 1) * 128],
                                                                    QTm[m].t[:, h, :], start=True, stop=not special),
                                 reads=[kvb, QTm[m]], writes=[stv])
                            if special:
                                P.op("pe", lambda e, m=m: e.matmul(stv.t[:, m * 128:(m + 1) * 128], ident_b.t[:], mskb.t[:, w, :], start=False, stop=True),
                                     reads=[ident_b, mskb], writes=[stv])
                        P.op("act", lambda e, h=h: e.activation(ptb_.t[:], stv.t[:], AF.Exp, bias=biasT.t[:, di, w, h:h + 1], scale=0.125),
                             reads=[stv, biasT], writes=[ptb_])
                        for m in range(2):
                            if dbg_att < 3:
                                continue
                            gi = m * 8 + h
                            ob = BK[gi // 3]
                            first = (gi // 3) not in bank_started
                            bank_started.add(gi // 3)
                            P.op("pe", lambda e, m=m, h=h, gi=gi, ob=ob, first=first: e.matmul(ob.t[:, (gi % 3) * 129:(gi % 3) * 129 + 129], ptb_.t[:, m * 128:(m + 1) * 128],
                                                                                kvb.t[:, 1024 + h * 129:1024 + (h + 1) * 129], start=first, stop=(kb == nkb - 1),
                                                                                skip_group_check=True),
                                 reads=[ptb_, kvb], writes=[ob])
                if dbg_att < 4:
                    continue
                for b in range(6):
                    n3 = 3 if b < 5 else 1
                    P.op("act" if b % 2 else "dve",
                         (lambda e, b=b, n3=n3: e.activation(oev.t[:, b * 3:b * 3 + n3, :], BK[b].t[:, 0:n3 * 129].rearrange("p (a c) -> p a c", c=129), AF.Copy)) if b % 2 else
                         (lambda e, b=b, n3=n3: e.tensor_copy(oev.t[:, b * 3:b * 3 + n3, :], BK[b].t[:, 0:n3 * 129].rearrange("p (a c) -> p a c", c=129))),
                         reads=[BK[b]], writes=[oev])
                epilogue(i)

            if dbg_skip_sample:
                return
            q_side(16)
            q3 = qf.t[:].rearrange("p (a b) -> p a b", b=64)
            P.dma("sync", dq.t, qf.t[0:16, :], reads=[qf], writes=[dq])
            P.op("dve", lambda e: e.tensor_tensor(out=sq.t[:], in0=qf.t[:], in1=kn_s.t[:], op=ALU.mult), reads=[qf, kn_s], writes=[sq])
            P.op("dve", lambda e: e.tensor_reduce(out=enew.t[:], in_=sq.t[:].rearrange("p (a b) -> p a b", b=64), axis=AX.X, op=ALU.add), reads=[sq], writes=[enew])
            P.op("act", lambda e: e.activation(enew.t[:], enew.t[:], AF.Exp, scale=0.125), reads=[enew], writes=[enew])
            pg_rr = [0]
            rtmp_ap = o1.t[0:16, :, :]
            for s in range(16):
                P.dma("sync", qbc.t[:], dq.t[s].partition_broadcast(128), reads=[dq], writes=[qbc])
                for pg in range(16):
                    kp, vp, e16 = KP[pg_rr[0] % 2], VP[pg_rr[0] % 2], E16[pg_rr[0] % 2]
                    pg_rr[0] += 1
                    P.gather(kp.t[:], cache_k, rowidx.t[:, s * 16 + pg:s * 16 + pg + 1], reads=[rowidx], writes=[kp])
                    P.gather(vp.t[:], cache_v, rowidx.t[:, s * 16 + pg:s * 16 + pg + 1], reads=[rowidx], writes=[vp])
                    P.op("dve", lambda e, kp=kp: e.tensor_tensor(out=sq.t[:], in0=kp.t[:], in1=qbc.t[:], op=ALU.mult), reads=[kp, qbc], writes=[sq])
                    P.op("dve", lambda e: e.tensor_reduce(out=sc16.t[:, 0:16], in_=sq.t[:].rearrange("p (a b) -> p a b", b=64), axis=AX.X, op=ALU.add),
                         reads=[sq], writes=[sc16])
                    P.op("dve", lambda e, pg=pg: e.scalar_tensor_tensor(out=sc16.t[:, 16:32], in0=sc16.t[:, 0:16], scalar=0.125, in1=sbias.t[:, pg, :], op0=ALU.mult, op1=ALU.add),
                         reads=[sc16, sbias], writes=[sc16])
                    P.op("act", lambda e, e16=e16: e.activation(e16.t[:], sc16.t[:, 16:32], AF.Exp), reads=[sc16], writes=[e16])
                    for n in range(2):
                        P.op("pe", lambda e, n=n, e16=e16, vp=vp, pg=pg: e.matmul(BK[n].t[0:16, 0:512], e16.t[:], vp.t[:, n * 512:(n + 1) * 512], start=(pg == 0), stop=(pg == 15)),
                             reads=[e16, vp], writes=[BK[n]])
                    P.op("pe", lambda e, e16=e16, pg=pg: e.matmul(BK[2].t[0:16, 0:1], e16.t[:], ones1.t[:], start=(pg == 0), stop=(pg == 15)),
                         reads=[e16, ones1], writes=[BK[2]])
                for n in range(2):
                    P.op("dve", lambda e, n=n: e.tensor_tensor(out=rtmp_ap[:, n * 4:(n + 1) * 4, :], in0=BK[n].t[0:16, 0:512].rearrange("p (a b) -> p a b", b=128),
                                                               in1=bmask.t[0:16, n * 4:(n + 1) * 4].unsqueeze(2).to_broadcast([16, 4, 128]), op=ALU.mult),
                         reads=[BK[n], bmask], writes=[o1])
                P.op("dve", lambda e, s=s: e.tensor_reduce(out=dall.t[:, s, 0:128], in_=rtmp_ap.rearrange("p h d -> p d h"), axis=AX.X, op=ALU.add),
                     reads=[o1], writes=[dall])
                P.op("dve", lambda e, s=s: e.tensor_copy(dall.t[:, s, 128:129], BK[2].t[0:16, 0:1]), reads=[BK[2]], writes=[dall])
            P.op("pool", lambda e: e.memset(oev.t[:], 1.0), writes=[oev])
            P.dma("sync", dD.t, dall.t[:], reads=[dall], writes=[dD])
            P.dma("sync", oev.t[0:16, :, :], dD.t.rearrange("m s d -> s m d"), reads=[dD], writes=[oev])
            v3 = v_s.t[:].rearrange("p (h d) -> p h d", d=128)
            for m in range(2):
                P.op("dve", lambda e, m=m: e.tensor_tensor(out=o1.t[:], in0=v3, in1=enew.t[:, m * 8:(m + 1) * 8].unsqueeze(2).to_broadcast([128, 8, 128]), op=ALU.mult),
                     reads=[v_s, enew], writes=[o1])
                P.op("dve", lambda e, m=m: e.tensor_tensor(out=oev.t[:, m * 8:(m + 1) * 8, 0:128], in0=oev.t[:, m * 8:(m + 1) * 8, 0:128], in1=o1.t[:], op=ALU.add),
                     reads=[oev, o1], writes=[oev])
            P.op("dve", lambda e: e.tensor_tensor(out=oev.t[:, :, 128], in0=oev.t[:, :, 128], in1=enew.t[:], op=ALU.add), reads=[oev, enew], writes=[oev])
            epilogue(16)

    steps = {"g0": lambda: gmlp_layer(0), "p0": lambda: peer_layer(0), "g1": lambda: gmlp_layer(1), "p1": lambda: peer_layer(1),
             "kv": kv_phase, "a0": lambda: attn_layer(0), "p2": lambda: peer_layer(2), "a1": lambda: attn_layer(1), "p3": lambda: peer_layer(3)}
    if "A" in stages:
        stages = ["g0", "p0", "g1", "p1"] + [x for x in stages if x != "A"]
    if "KV" in stages:
        stages = [("kv" if x == "KV" else x) for x in stages]
    if "B" in stages:
        stages = [x for x in stages if x != "B"] + ["a0", "p2", "a1", "p3"]
    kn_alloc = [False]
    for st_ in stages:
        if st_ in ("kv", "a0", "a1") and not kn_alloc[0]:
            kn_alloc[0] = True
            kn_s = P.sb("kn_s", [128, 1024], F32)
            v_s = P.sb("v_s", [128, 1024], F32)
        steps[st_]()
    for j in range(NTILE):
        P.dma("sync", y[j * 128:(j + 1) * 128, :], X[j].t[:], reads=[X[j]], out=True)
    P.finish()
    return nc, P


_CACHE = {}


def kernel(x_prompt, x_sample, cache_k, cache_v, page_table,
           a_norm_g, a_w_in, a_vnorm_g, a_w_s, a_b_s, a_w_out,
           kv_norm_g, w_k, w_v, k_norm_g,
           b_norm_g, w_q, q_norm_g, lambda_q1, lambda_k1, lambda_q2, lambda_k2, subln_g, w_o,
           f_norm_g, peer_w_q, peer_keys, peer_u, peer_v):
    f = lambda a: np.ascontiguousarray(np.asarray(a, dtype=np.float32))
    xp = f(x_prompt)
    xs = f(x_sample).reshape(128, 1024)
    if "nc" not in _CACHE:
        _CACHE["nc"] = build_program()[0]
    nc = _CACHE.get("nc_override", _CACHE["nc"])
    lam_in = np.ascontiguousarray(np.stack([f(lambda_q1), f(lambda_k1), f(lambda_q2), f(lambda_k2)], axis=1).reshape(2, 256))
    shared = {
        "cache_k": f(cache_k).reshape(2560 * 128, 1024), "cache_v": f(cache_v).reshape(2560 * 128, 1024),
        "a_norm_g": f(a_norm_g), "a_w_in": f(a_w_in), "a_vnorm_g": f(a_vnorm_g), "a_w_s": f(a_w_s), "a_b_s": f(a_b_s),
        "a_w_out": f(a_w_out), "kv_norm_g": f(kv_norm_g).reshape(1, 1024), "w_k": f(w_k), "w_v": f(w_v),
        "k_norm_g": f(k_norm_g).reshape(1, 64), "b_norm_g": f(b_norm_g), "w_q": f(w_q), "q_norm_g": f(q_norm_g),
        "lam_in": lam_in, "subln_g": f(subln_g), "w_o": f(w_o), "f_norm_g": f(f_norm_g), "peer_w_q": f(peer_w_q),
        "peer_keys": f(peer_keys).reshape(4, 16, 128, 128), "peer_u": f(peer_u).reshape(4 * N_EXP, 1024),
        "peer_v": f(peer_v).reshape(4 * N_EXP, 1024),
    }
    ptab = np.ascontiguousarray(np.asarray(page_table, dtype=np.int32))
    kk = np.arange(128)[:, None]
    qq = np.arange(128)[None, :]
    causal = np.where(kk <= qq, 0.0, -30000.0).astype(np.float32)
    in_maps = []
    for c in range(8):
        b, hf = c // 2, c % 2
        xin = np.zeros((NALL * 128, 1024), np.float32)
        xin[:NPT * 128] = xp[b].reshape(16, 2, 128, 1024)[:, hf].reshape(NPT * 128, 1024)
        xin[NPT * 128:NPT * 128 + 16] = xs[c * 16:(c + 1) * 16]
        xin[NTILE * 128:] = xp[b].reshape(16, 2, 128, 1024)[:, 1 - hf].reshape(NPT * 128, 1024)
        msk = np.zeros((128, 2, 128), np.float32)
        msk[:, 0, :] = causal
        msk[:, 1, :] = -30000.0 if hf == 0 else 0.0
        m = dict(shared)
        m["xin"] = xin
        m["pt"] = ptab[c * 16:(c + 1) * 16].reshape(1, 256)
        m["maskadd"] = msk
        m["hfcol"] = np.full((128, 1), 128.0 * (1 - 2 * hf), np.float32)
        in_maps.append(m)
    res = run_bass_kernel_spmd(nc, in_maps, core_ids=list(range(8)))
    R = res.results
    y_prompt = np.zeros((4, 4096, 1024), np.float32)
    nk_p = np.zeros((4, 4096, 1024), np.float32)
    nv_p = np.zeros((4, 4096, 1024), np.float32)
    y_sample = np.zeros((128, 1024), np.float32)
    nk_s = np.zeros((128, 1024), np.float32)
    nv_s = np.zeros((128, 1024), np.float32)
    gv = np.zeros((2, 128, 2048), np.float32)
    for c in range(8):
        b, hf = c // 2, c % 2
        r = R[c]
        y_prompt[b].reshape(16, 2, 128, 1024)[:, hf] = r["y"][:NPT * 128].reshape(16, 128, 1024)
        nk_p[b].reshape(16, 2, 128, 1024)[:, hf] = r["nkp"].reshape(16, 128, 1024)
        nv_p[b].reshape(16, 2, 128, 1024)[:, hf] = r["nvp"].reshape(16, 128, 1024)
        y_sample[c * 16:(c + 1) * 16] = r["y"][NPT * 128:NPT * 128 + 16]
        nk_s[c * 16:(c + 1) * 16] = r["nks"]
        nv_s[c * 16:(c + 1) * 16] = r["nvs"]
        gv[:, c * 16:(c + 1) * 16] = r["gv"]
    return (y_prompt, y_sample.reshape(128, 1, 1024), nk_p.reshape(4, 4096, 16, 64), nv_p.reshape(4, 4096, 8, 128),
            nk_s.reshape(128, 1, 16, 64), nv_s.reshape(128, 1, 8, 128), gv.reshape(2, 128, 1, 2048))
```

```python
import math
import numpy as np
import concourse.bass as bass
import concourse.mybir as mybir
from concourse.bass_utils import run_bass_kernel_spmd

F32 = mybir.dt.float32
BF16 = mybir.dt.bfloat16
I32 = mybir.dt.int32
U32 = mybir.dt.uint32
ALU = mybir.AluOpType
AF = mybir.ActivationFunctionType
AX = mybir.AxisListType

SEM_ROT = 30000
import os as _os
FORCE_SPECIAL = bool(int(_os.environ.get('FORCE_SPECIAL', '0')))
N_DSEM = 72
EPS = 1e-6
NTILE = 17
NALL = 33
NPT = 16
DEPTH = 4
N_EXP = 16384


class Res:
    __slots__ = ("name", "t", "w", "r", "dslot")

    def __init__(self, name, t=None):
        self.name = name
        self.t = t
        self.w = None
        self.r = []
        self.dslot = None


class Prog:
    def __init__(self, nc):
        self.nc = nc
        self.eng = {"pe": nc.tensor, "dve": nc.vector, "act": nc.scalar, "pool": nc.gpsimd, "sync": nc.sync}
        self.sem = {}
        self.cnt = {}
        self.nsem = 0
        self._keep = []
        self._scopes = []
        for k in ("pe", "dve", "act", "pool"):
            self._new_eng_sem(k)
        self.dfree = [[self._alloc_sem(f"d{i}"), 0] for i in range(N_DSEM)]
        self.dall = list(self.dfree)
        self.seen = {k: {} for k in self.eng}
        self.out_events = []
        self.n_inst = 0

    def _alloc_sem(self, name):
        g = self.nc.semaphore(name)
        s = g.__enter__()
        self.nsem += 1
        return s

    def _new_eng_sem(self, k):
        self.sem[k] = self._alloc_sem(f"s_{k}_{self.nsem}")
        self.cnt[k] = 0

    def _reg(self, res, g):
        if self._scopes:
            self._scopes[-1].append((res, g))
        else:
            self._keep.append((res, g))
        return res

    def sb(self, name, shape, dtype):
        self._uid = getattr(self, "_uid", 0) + 1
        name = f"{name}_{self._uid}"
        g = self.nc.sbuf_tensor(name, list(shape), dtype)
        return self._reg(Res(name, g.__enter__()), g)

    def ps(self, name, shape, dtype=F32):
        g = self.nc.psum_tensor(name, list(shape), dtype)
        return self._reg(Res(name, g.__enter__()), g)

    def dram(self, name, shape, dtype):
        t = self.nc.dram_tensor(name, list(shape), dtype, kind="Internal")
        return Res(name, t.ap())

    def view(self, name, ap):
        return Res(name, ap)

    def scope(self):
        prog = self

        class _S:
            def __enter__(s):
                prog._scopes.append([])

            def __exit__(s, *a):
                if a[0] is not None:
                    return False
                prog.barrier()
                items = prog._scopes.pop()
                for res, g in reversed(items):
                    if res.dslot is not None:
                        prog.dfree.append(res.dslot)
                        res.dslot = None
                    g.__exit__(None, None, None)
                return False
        return _S()

    def barrier(self):
        evs = []
        for k in ("pe", "dve", "act", "pool"):
            if self.cnt[k] > 0:
                evs.append((self.sem[k], self.cnt[k]))
        for sl in self.dall:
            if sl[1] > 0:
                evs.append((sl[0], 16 * sl[1]))
        for q in ("pe", "dve", "act", "pool", "sync"):
            for ev in evs:
                self._wait(q, ev)

    def _wait(self, k, ev):
        if ev is None:
            return
        sem, val = ev
        sid = id(sem)
        if self.seen[k].get(sid, 0) >= val:
            return
        self.eng[k].wait_ge(sem, val)
        self.seen[k][sid] = val
        self.n_inst += 1

    @staticmethod
    def _compact(evs):
        best = {}
        for sem, val in evs:
            sid = id(sem)
            if sid not in best or best[sid][1] < val:
                best[sid] = (sem, val)
        return list(best.values())

    def _commit(self, ev, reads, writes):
        for r in reads:
            r.r.append(ev)
            if len(r.r) > 16:
                r.r = self._compact(r.r)
        for w in writes:
            w.w = ev
            w.r = []

    def op(self, k, fn, reads=(), writes=()):
        if self.cnt[k] >= SEM_ROT:
            self._new_eng_sem(k)
        if k == "pe":
            self.seen[k][id(self.sem[k])] = 1 << 60
        for r in reads:
            self._wait(k, r.w)
        for w in writes:
            self._wait(k, w.w)
            for ev in w.r:
                self._wait(k, ev)
        inst = fn(self.eng[k])
        self.cnt[k] += 1
        inst.then_inc(self.sem[k], 1)
        ev = (self.sem[k], self.cnt[k])
        self._commit(ev, reads, writes)
        self.n_inst += 1
        return ev

    def _dma_event(self, prim):
        if prim.dslot is None:
            prim.dslot = self.dfree.pop(0)
        prim.dslot[1] += 1
        return (prim.dslot[0], 16 * prim.dslot[1])

    def _dma_deps(self, q, reads, writes):
        for r in reads:
            self._wait(q, r.w)
        for w in writes:
            if w.w is not None and not (w.dslot is not None and w.w[0] is w.dslot[0]):
                self._wait(q, w.w)
            for ev in w.r:
                self._wait(q, ev)

    def dma(self, q, out_ap, in_ap, reads=(), writes=(), out=False, prim=None, **kw):
        self._dma_deps(q, reads, writes)
        if prim is None:
            prim = (list(writes) + list(reads))[0]
        ev = self._dma_event(prim)
        kw.setdefault("allow_slow_non_contiguous", True)
        self.eng[q].dma_start(out=out_ap, in_=in_ap, **kw).then_inc(ev[0], 16)
        for r in reads:
            r.r.append(ev)
        for w in writes:
            w.w = ev
            w.r = []
        if out:
            self.out_events.append(ev)
        self.n_inst += 1
        return ev

    def gather(self, out_ap, table_ap, idx_ap, reads=(), writes=(), prim=None):
        q = "pool"
        self._dma_deps(q, reads, writes)
        if prim is None:
            prim = list(writes)[0]
        ev = self._dma_event(prim)
        self.nc.gpsimd.indirect_dma_start(
            out=out_ap, out_offset=None, in_=table_ap,
            in_offset=bass.IndirectOffsetOnAxis(ap=idx_ap, axis=0),
        ).then_inc(ev[0], 16)
        for r in reads:
            r.r.append(ev)
        for w in writes:
            w.w = ev
            w.r = []
        self.n_inst += 1
        return ev

    def collective(self, fn, reads, writes):
        q = "pool"
        self._dma_deps(q, reads, writes)
        ev = self._dma_event(list(writes)[0])
        fn(self.nc.gpsimd).then_inc(ev[0], 16)
        for r in reads:
            r.r.append(ev)
        for w in writes:
            w.w = ev
            w.r = []
        return ev

    def finish(self):
        for ev in self._compact(self.out_events):
            self._wait("sync", ev)
        self.barrier()


def _lam_init(l):
    return 0.8 - 0.6 * math.exp(-0.3 * l)


def build_program(stages=("A", "KV", "B"), n_tiles_peer=NALL, cache_rows=2560 * 128, dbg_skip_sample=False, dbg_qtiles=NPT, dbg_att=9):
    nc = bass.Bass("TRN2", target_bir_lowering=False)
    P = Prog(nc)
    stages = list(stages)

    def DI(name, shape, dt=F32):
        return nc.dram_tensor(name, list(shape), dt, kind="ExternalInput").ap()

    def DO(name, shape, dt=F32):
        return nc.dram_tensor(name, list(shape), dt, kind="ExternalOutput").ap()

    xin = DI("xin", [NALL * 128, 1024])
    cache_k = DI("cache_k", [cache_rows, 1024])
    cache_v = DI("cache_v", [cache_rows, 1024])
    pt = DI("pt", [1, 256], I32)
    a_norm_g = DI("a_norm_g", [2, 1024])
    a_w_in = DI("a_w_in", [2, 1024, 4096])
    a_vnorm_g = DI("a_vnorm_g", [2, 2048])
    a_w_s = DI("a_w_s", [2, 8, 128, 128])
    a_b_s = DI("a_b_s", [2, 8, 128])
    a_w_out = DI("a_w_out", [2, 2048, 1024])
    kv_norm_g = DI("kv_norm_g", [1, 1024])
    w_k = DI("w_k", [1024, 1024])
    w_v = DI("w_v", [1024, 1024])
    k_norm_g = DI("k_norm_g", [1, 64])
    b_norm_g = DI("b_norm_g", [2, 1024])
    w_q = DI("w_q", [2, 1024, 1024])
    q_norm_g = DI("q_norm_g", [2, 64])
    lam_in = DI("lam_in", [2, 256])
    subln_g = DI("subln_g", [2, 128])
    w_o = DI("w_o", [2, 1024, 1024])
    f_norm_g = DI("f_norm_g", [4, 1024])
    peer_w_q = DI("peer_w_q", [4, 1024, 2048])
    peer_keys = DI("peer_keys", [4, 16, 128, 128])
    peer_u = DI("peer_u", [4 * N_EXP, 1024])
    peer_v = DI("peer_v", [4 * N_EXP, 1024])
    maskadd = DI("maskadd", [128, 2, 128])
    hfcol_in = DI("hfcol", [128, 1])

    y = DO("y", [NTILE * 128, 1024])
    nkp = DO("nkp", [NPT * 128, 1024])
    nvp = DO("nvp", [NPT * 128, 1024])
    nks = DO("nks", [16, 1024])
    nvs = DO("nvs", [16, 1024])
    gvo = DO("gv", [2, 16, 2048])

    kvx_all = P.dram("kvx_all", [2 * NPT, 128, 2056], BF16)
    PUV = [P.dram(f"puv_b{l}", [N_EXP, 2048], BF16) for l in range(DEPTH)]
    xp = P.dram("xp", [NPT * 128, 1024], F32)
    XPR = [P.view(f"xp{k}", xp.t[k * 128:(k + 1) * 128, :]) for k in range(NPT)]
    dq = P.dram("dq", [16, 1024], F32)
    dD = P.dram("dD", [16, 16, 129], F32)

    X = [P.sb(f"x{j}", [128, 1024], F32) for j in range(NTILE)]
    ident_f = P.sb("ident_f", [128, 128], F32)
    ident_b = P.sb("ident_b", [128, 128], BF16)
    ss = P.sb("ss", [128, 64], F32)
    iota16 = P.sb("iota16", [128, 16], F32)
    pcol = P.sb("pcol", [128, 1], F32)
    BK = [P.ps(f"bank{b}", [128, 512], F32) for b in range(8)]
    bank_rr = [0]

    def nextbank(lo=0, hi=6):
        b = lo + bank_rr[0] % (hi - lo)
        bank_rr[0] += 1
        return BK[b]

    XS = [P.sb("xs0", [128, 1024], F32)]
    XSB = [XS]
    for j in range(NTILE):
        P.dma("sync", X[j].t[:], xin[j * 128:(j + 1) * 128, :], writes=[X[j]])
    for k in range(NPT):
        P.dma("sync", XPR[k].t, xin[(NTILE + k) * 128:(NTILE + k + 1) * 128, :], writes=[XPR[k]])

    for l in range(DEPTH):
        for (src, half) in ((peer_u, 0), (peer_v, 1)):
            for c in range(8):
                P.dma("pool", PUV[l].t[c * 2048:(c + 1) * 2048, half * 1024:(half + 1) * 1024],
                      src[l * N_EXP + c * 2048:l * N_EXP + (c + 1) * 2048, :], writes=[PUV[l]])

    def x_begin(j, buf=0):
        if j < NTILE:
            return X[j]
        r = XSB[0][buf]
        P.dma("sync", r.t[:], XPR[j - NTILE].t, reads=[XPR[j - NTILE]], writes=[r])
        return r

    def x_end(j, r):
        if j >= NTILE:
            P.dma("sync", XPR[j - NTILE].t, r.t[:], reads=[r], writes=[XPR[j - NTILE]])
    P.op("pool", lambda e: e.memset(ident_f.t[:], 1.0), writes=[ident_f])
    P.op("pool", lambda e: e.affine_select(out=ident_f.t[:], in_=ident_f.t[:], pattern=[[-1, 128]],
                                            compare_op=ALU.is_equal, fill=0.0, base=0, channel_multiplier=1),
         reads=[ident_f], writes=[ident_f])
    P.op("dve", lambda e: e.tensor_copy(ident_b.t[:], ident_f.t[:]), reads=[ident_f], writes=[ident_b])
    P.op("pool", lambda e: e.iota(iota16.t[:], [[1, 16]], base=0, channel_multiplier=0,
                                   allow_small_or_imprecise_dtypes=True), writes=[iota16])
    P.op("pool", lambda e: e.iota(pcol.t[:], [[0, 1]], base=0, channel_multiplier=1,
                                   allow_small_or_imprecise_dtypes=True), writes=[pcol])

    def rows(j):
        return 16 if j == 16 else 128

    def rstd_of(src_res, src_ap, junk_res, junk_ap, D, col):
        P.op("dve", lambda e: e.scalar_tensor_tensor(out=junk_ap, in0=src_ap, scalar=1.0, in1=src_ap,
                                                     op0=ALU.mult, op1=ALU.mult, accum_out=ss.t[:, col:col + 1]),
             reads=[src_res], writes=[junk_res, ss])
        P.op("dve", lambda e: e.tensor_scalar(out=ss.t[:, col + 1:col + 2], in0=ss.t[:, col:col + 1],
                                              scalar1=1.0 / D, scalar2=EPS, op0=ALU.mult, op1=ALU.add),
             reads=[ss], writes=[ss])
        P.op("act", lambda e: e.activation(ss.t[:, col + 2:col + 3], ss.t[:, col + 1:col + 2], AF.Sqrt),
             reads=[ss], writes=[ss])
        P.op("dve", lambda e: e.reciprocal(ss.t[:, col + 3:col + 4], ss.t[:, col + 2:col + 3]),
             reads=[ss], writes=[ss])
        return ss.t[:, col + 3:col + 4]

    def group_rstd(src_res, src3, sq_res, sq3, n, gd, st_res, base):
        P.op("dve", lambda e: e.tensor_tensor(out=sq3, in0=src3, in1=src3, op=ALU.mult),
             reads=[src_res], writes=[sq_res])
        a = st_res.t[:, base:base + n]
        b = st_res.t[:, base + n:base + 2 * n]
        c = st_res.t[:, base + 2 * n:base + 3 * n]
        P.op("dve", lambda e: e.tensor_reduce(out=a, in_=sq3, axis=AX.X, op=ALU.add), reads=[sq_res], writes=[st_res])
        P.op("dve", lambda e: e.tensor_scalar(out=a, in0=a, scalar1=1.0 / gd, scalar2=EPS, op0=ALU.mult, op1=ALU.add),
             reads=[st_res], writes=[st_res])
        P.op("act", lambda e: e.activation(b, a, AF.Sqrt), reads=[st_res], writes=[st_res])
        P.op("dve", lambda e: e.reciprocal(c, b), reads=[st_res], writes=[st_res])
        return c

    def transposes(dst_res, dst3, src_res, src_fn, n, ident=None, evac="act"):
        ident = ident or ident_b
        for g0 in range(0, n, 8):
            bk = nextbank()
            bv = bk.t[:].bitcast(BF16)
            cnt = min(8, n - g0)
            for i in range(cnt):
                P.op("pe", lambda e, i=i: e.transpose(bv[:, i * 128:(i + 1) * 128], src_fn(g0 + i), ident.t[:]),
                     reads=[src_res, ident], writes=[bk])
            P.op(evac, (lambda e: e.activation(dst3[:, g0:g0 + cnt, :], bv[:, 0:cnt * 128].rearrange("p (a b) -> p a b", b=128), AF.Copy))
                 if evac == "act" else
                 (lambda e: e.tensor_copy(dst3[:, g0:g0 + cnt, :], bv[:, 0:cnt * 128].rearrange("p (a b) -> p a b", b=128))),
                 reads=[bk], writes=[dst_res])

    def load_w_bf16(dst_res, src2d, kc_n):
        v = src2d.rearrange("(kc p) n -> p kc n", p=128)
        for kc in range(kc_n):
            P.dma("pool", dst_res.t[:, kc, :], v[:, kc, :], writes=[dst_res])

    def bcast_load(dst_res, row_ap):
        P.dma("sync", dst_res.t[:], row_ap.partition_broadcast(128), writes=[dst_res])

    def proj(lhsT_res, lhsT3, w_res, w3, kc_n, n_out, evac_fn, bank_lo=0, bank_hi=6):
        for n in range(n_out // 512):
            bk = nextbank(bank_lo, bank_hi)
            for kc in range(kc_n):
                P.op("pe", lambda e, kc=kc: e.matmul(bk.t[:, 0:512], lhsT3[:, kc, :], w3[:, kc, n * 512:(n + 1) * 512],
                                                     start=(kc == 0), stop=(kc == kc_n - 1)),
                     reads=[lhsT_res, w_res], writes=[bk])
            evac_fn(n, bk)

    def x_add(xr, j, n, bk):
        r = rows(j)
        P.op("dve", lambda e: e.tensor_tensor(out=xr.t[0:r, n * 512:(n + 1) * 512], in0=xr.t[0:r, n * 512:(n + 1) * 512],
                                              in1=bk.t[0:r, 0:512], op=ALU.add),
             reads=[xr, bk], writes=[xr])

    def gmlp_layer(l):
        with P.scope():
            w_in = P.sb("w_in", [128, 8, 4096], BF16)
            w_out = P.sb("w_out", [128, 16, 1024], BF16)
            g_v = P.sb("g_v", [128, 2048], F32)
            ga = P.sb("ga", [128, 8], F32)
            wsT = P.sb("wsT", [128, 8, 128], BF16)
            wsS = P.sb("wsS", [128, 8, 128], BF16)
            w00 = P.sb("w00", [128, 8, 1], F32)
            bsP = P.sb("bsP", [128, 8], F32)
            bsS = P.sb("bsS", [128, 8, 1], F32)
            u = P.sb("u", [128, 2048], BF16)
            v = P.sb("v", [128, 2048], F32)
            vg = P.sb("vg", [128, 2048], BF16)
            hnT = P.view("hnT", vg.t[:, 0:1024].rearrange("p (a b) -> p a b", b=128))
            gT = P.sb("gT", [128, 16, 128], BF16)

            load_w_bf16(w_in, a_w_in[l], 8)
            load_w_bf16(w_out, a_w_out[l], 16)
            bcast_load(g_v, a_vnorm_g[l])
            P.dma("sync", ga.t[:], a_norm_g[l].rearrange("(kc p) -> p kc", p=128), writes=[ga], allow_slow_non_contiguous=True)
            for kc in range(8):
                P.op("dve", lambda e, kc=kc: e.tensor_scalar(out=w_in.t[:, kc, :], in0=w_in.t[:, kc, :], scalar1=ga.t[:, kc:kc + 1],
                                                             scalar2=None, op0=ALU.mult), reads=[w_in, ga], writes=[w_in])
            wsf3 = v.t[:, 0:1024].rearrange("p (g s) -> p g s", s=128)
            wsb3 = vg.t[:, 0:1024].rearrange("p (g s) -> p g s", s=128)
            P.dma("sync", wsf3, a_w_s[l].rearrange("g t s -> t g s"), writes=[v])
            P.op("pool", lambda e: e.affine_select(out=wsf3, in_=wsf3, pattern=[[0, 8], [-1, 128]], compare_op=ALU.is_ge,
                                                    fill=0.0, base=0, channel_multiplier=1), reads=[v], writes=[v])
            P.op("dve", lambda e: e.tensor_copy(wsb3, wsf3), reads=[v], writes=[vg])
            transposes(wsT, wsT.t, vg, lambda i: wsb3[:, i, :], 8)
            P.dma("sync", w00.t[:], a_w_s[l, :, 0, 0:1].partition_broadcast(128), writes=[w00])
            for g in range(8):
                P.op("dve", lambda e, g=g: e.tensor_scalar(out=wsS.t[:, g, :], in0=ident_f.t[:], scalar1=w00.t[:, g, 0:1], scalar2=None,
                                                           op0=ALU.mult), reads=[ident_f, w00], writes=[wsS])
            P.dma("sync", bsP.t[:], a_b_s[l].rearrange("g t -> t g"), writes=[bsP], allow_slow_non_contiguous=True)
            P.dma("sync", bsS.t[:], a_b_s[l, :, 0:1].partition_broadcast(128), writes=[bsS])

            for j in range(NALL):
                xt = x_begin(j)
                hn_ap = gT.t[:, 0:8, :].rearrange("p a b -> p (a b)")
                rs = rstd_of(xt, xt.t[:], v, v.t[:, 0:1024], 1024, 0)
                P.op("dve", lambda e: e.tensor_scalar(out=hn_ap, in0=xt.t[:], scalar1=rs, scalar2=None, op0=ALU.mult),
                     reads=[xt, ss], writes=[gT])
                transposes(vg, hnT.t, gT, lambda i: hn_ap[:, i * 128:(i + 1) * 128], 8)

                def evac_z(n, bk):
                    if n < 4:
                        P.op("act", lambda e: e.activation(u.t[:, n * 512:(n + 1) * 512], bk.t[:, 0:512], AF.Gelu), reads=[bk], writes=[u])
                    else:
                        P.op("act", lambda e: e.activation(v.t[:, (n - 4) * 512:(n - 3) * 512], bk.t[:, 0:512], AF.Gelu), reads=[bk], writes=[v])
                proj(vg, hnT.t, w_in, w_in.t, 8, 4096, evac_z)
                rs2 = rstd_of(v, v.t[:], gT, gT.t[:].rearrange("p a b -> p (a b)"), 2048, 4)
                P.op("dve", lambda e: e.scalar_tensor_tensor(out=v.t[:], in0=v.t[:], scalar=rs2, in1=g_v.t[:], op0=ALU.mult, op1=ALU.mult),
                     reads=[v, ss, g_v], writes=[v])
                if j == 16:
                    P.dma("sync", gvo[l], v.t[0:16, :], reads=[v], out=True)
                P.op("act", lambda e: e.activation(vg.t[:], v.t[:], AF.Copy), reads=[v], writes=[vg])
                wmix = wsS if j == 16 else wsT
                bmix = (lambda g: bsS.t[:, g, 0:1]) if j == 16 else (lambda g: bsP.t[:, g:g + 1])
                for g in range(8):
                    bk = nextbank()
                    P.op("pe", lambda e, g=g: e.matmul(bk.t[:, 0:256], wmix.t[:, g, :], vg.t[:, g * 256:(g + 1) * 256], start=True, stop=True),
                         reads=[wmix, vg], writes=[bk])
                    P.op("dve", lambda e, g=g: e.scalar_tensor_tensor(out=vg.t[:, g * 256:(g + 1) * 256], in0=bk.t[:, 0:256], scalar=bmix(g),
                                                                      in1=u.t[:, g * 256:(g + 1) * 256], op0=ALU.add, op1=ALU.mult),
                         reads=[bk, u, bsP, bsS], writes=[vg])
                transposes(gT, gT.t, vg, lambda i: vg.t[:, i * 128:(i + 1) * 128], 16)
                proj(gT, gT.t, w_out, w_out.t, 16, 1024, lambda n, bk: x_add(xt, j, n, bk))
                x_end(j, xt)

    NB = 8
    GRP = 4

    def peer_layer(l):
        with P.scope():
            pwq = P.sb("pwq", [128, 8, 2048], BF16)
            keysT = P.sb("keysT", [128, 16, 128], BF16)
            g_f = P.sb("g_f", [128, 1024], F32)
            HN = [P.sb(f"hn{i}", [128, 1024], F32) for i in range(2)]
            EIDX = [P.sb(f"eidx{i}", [128, 128], I32) for i in range(2)]
            GW = [P.sb(f"gw{i}", [128, 128], F32) for i in range(2)]
            hnb = P.sb("hnb", [128, 1024], BF16)
            hnT = P.sb("hnT", [128, 8, 128], BF16)
            qb = P.sb("qb", [128, 2048], BF16)
            qT = P.sb("qT", [128, 16, 128], BF16)
            sc = P.sb("sc", [128, 16, 128], F32)
            wk = P.sb("wk", [128, 256], F32)
            sv = P.sb("sv", [128, 16, 16], F32)
            si = P.sb("si", [128, 16, 16], U32)
            sif = P.sb("sif", [128, 16, 16], F32)
            cand = P.sb("cand", [128, 8, 256], F32)
            tops = P.sb("tops", [128, 8, 16], F32)
            pos = P.sb("pos", [128, 8, 16], U32)
            posf = P.sb("posf", [128, 8, 16], F32)
            af = P.sb("af", [128, 8, 16], F32)
            bf = P.sb("bf", [128, 8, 16], F32)
            i0 = P.sb("i0", [128, 8, 16], F32)
            i1 = P.sb("i1", [128, 8, 16], F32)
            ex = P.sb("ex", [128, 8, 16], F32)
            zz = P.sb("zz", [128, 16], F32)
            actp = P.sb("actp", [128, 128], F32)
            wgt = P.sb("wgt", [128, 128], F32)
            junk = P.sb("junk", [128, 1024], F32)
            gel = P.sb("gel", [128, 128], F32)
            GB = [P.sb(f"gb{i}", [128, 2048], BF16) for i in range(NB)]
            DG = [P.sb(f"dg{i}", [128, 128], BF16) for i in range(3)]

            load_w_bf16(pwq, peer_w_q[l], 8)
            bcast_load(g_f, f_norm_g[l])
            P.dma("sync", sc.t[:], peer_keys[l].rearrange("c n d -> n c d"), writes=[sc])
            P.op("dve", lambda e: e.tensor_copy(qT.t[:], sc.t[:]), reads=[sc], writes=[qT])
            transposes(keysT, keysT.t, qT, lambda i: qT.t[:, i, :], 16)

            if l < 2:
                XSB[0] = [XS[0], P.sb("xs1", [128, 1024], F32)]
            XT = {}

            def routing(j):
                xt = XT[j] = x_begin(j, j % 2) if j >= NTILE else X[j]
                hn, eidx, gw = HN[j % 2], EIDX[j % 2], GW[j % 2]
                rs = rstd_of(xt, xt.t[:], junk, junk.t[:], 1024, 8)
                P.op("dve", lambda e: e.scalar_tensor_tensor(out=hn.t[:], in0=xt.t[:], scalar=rs, in1=g_f.t[:], op0=ALU.mult, op1=ALU.mult),
                     reads=[xt, ss, g_f], writes=[hn])
                P.op("act", lambda e: e.activation(hnb.t[:], hn.t[:], AF.Copy), reads=[hn], writes=[hnb])
                transposes(hnT, hnT.t, hnb, lambda i: hnb.t[:, i * 128:(i + 1) * 128], 8)
                yield
                proj(hnT, hnT.t, pwq, pwq.t, 8, 2048,
                     lambda n, bk: P.op("act", lambda e: e.activation(qb.t[:, n * 512:(n + 1) * 512], bk.t[:, 0:512], AF.Copy),
                                        reads=[bk], writes=[qb]))
                yield
                transposes(qT, qT.t, qb, lambda i: qb.t[:, i * 128:(i + 1) * 128], 16)
                yield
                for g4 in range(4):
                    bk = nextbank()
                    for i in range(4):
                        hc = g4 * 4 + i
                        P.op("pe", lambda e, hc=hc, i=i: e.matmul(bk.t[:, i * 128:(i + 1) * 128], qT.t[:, hc, :], keysT.t[:, hc, :], start=True, stop=True),
                             reads=[qT, keysT], writes=[bk])
                    P.op("act", lambda e, g4=g4: e.activation(sc.t[:, g4 * 4:(g4 + 1) * 4, :], bk.t[:, 0:512].rearrange("p (a b) -> p a b", b=128), AF.Copy),
                         reads=[bk], writes=[sc])
                    yield
                for hc in range(16):
                    P.op("dve", lambda e, hc=hc: e.max(sv.t[:, hc, 0:8], sc.t[:, hc, :]), reads=[sc], writes=[sv])
                    P.op("dve", lambda e, hc=hc: e.max_index(si.t[:, hc, 0:8], sv.t[:, hc, 0:8], sc.t[:, hc, :]), reads=[sc, sv], writes=[si])
                    P.op("dve", lambda e, hc=hc: e.match_replace(wk.t[:, 0:128], sv.t[:, hc, 0:8], sc.t[:, hc, :], -1e30), reads=[sc, sv], writes=[wk])
                    P.op("dve", lambda e, hc=hc: e.max(sv.t[:, hc, 8:16], wk.t[:, 0:128]), reads=[wk], writes=[sv])
                    P.op("dve", lambda e, hc=hc: e.max_index(si.t[:, hc, 8:16], sv.t[:, hc, 8:16], wk.t[:, 0:128]), reads=[wk, sv], writes=[si])
                    yield
                P.op("dve", lambda e: e.tensor_copy(sif.t[:], si.t[:]), reads=[si], writes=[sif])
                sv4 = sv.t[:].rearrange("p (h c) k -> p h c k", c=2)
                sif4 = sif.t[:].rearrange("p (h c) k -> p h c k", c=2)
                cand4 = cand.t[:].rearrange("p h (a b) -> p h a b", b=16)
                P.op("dve", lambda e: e.tensor_tensor(out=cand4, in0=sv4[:, :, 0, :].unsqueeze(3).to_broadcast([128, 8, 16, 16]),
                                                      in1=sv4[:, :, 1, :].unsqueeze(2).to_broadcast([128, 8, 16, 16]), op=ALU.add),
                     reads=[sv], writes=[cand])
                for h in range(8):
                    P.op("dve", lambda e, h=h: e.max(tops.t[:, h, 0:8], cand.t[:, h, :]), reads=[cand], writes=[tops])
                    P.op("dve", lambda e, h=h: e.max_index(pos.t[:, h, 0:8], tops.t[:, h, 0:8], cand.t[:, h, :]), reads=[cand, tops], writes=[pos])
                    P.op("dve", lambda e, h=h: e.match_replace(wk.t[:], tops.t[:, h, 0:8], cand.t[:, h, :], -1e30), reads=[cand, tops], writes=[wk])
                    P.op("dve", lambda e, h=h: e.max(tops.t[:, h, 8:16], wk.t[:]), reads=[wk], writes=[tops])
                    P.op("dve", lambda e, h=h: e.max_index(pos.t[:, h, 8:16], tops.t[:, h, 8:16], wk.t[:]), reads=[wk, tops], writes=[pos])
                    yield
                P.op("dve", lambda e: e.tensor_tensor(out=ex.t[:], in0=tops.t[:], in1=tops.t[:, :, 0:1].to_broadcast([128, 8, 16]), op=ALU.subtract),
                     reads=[tops], writes=[ex])
                P.op("act", lambda e: e.activation(ex.t[:], ex.t[:], AF.Exp), reads=[ex], writes=[ex])
                P.op("dve", lambda e: e.tensor_reduce(out=zz.t[:, 0:8], in_=ex.t[:], axis=AX.X, op=ALU.add), reads=[ex], writes=[zz])
                P.op("dve", lambda e: e.reciprocal(zz.t[:, 8:16], zz.t[:, 0:8]), reads=[zz], writes=[zz])
                P.op("dve", lambda e: e.tensor_tensor(out=gw.t[:].rearrange("p (h k) -> p h k", k=16), in0=ex.t[:],
                                                      in1=zz.t[:, 8:16].unsqueeze(2).to_broadcast([128, 8, 16]), op=ALU.mult),
                     reads=[ex, zz], writes=[gw])
                yield
                P.op("dve", lambda e: e.tensor_copy(posf.t[:], pos.t[:]), reads=[pos], writes=[posf])
                P.op("dve", lambda e: e.tensor_scalar(out=af.t[:], in0=posf.t[:], scalar1=16.0, scalar2=None, op0=ALU.is_ge), reads=[posf], writes=[af])
                for k in range(2, 16):
                    P.op("dve", lambda e, k=k: e.scalar_tensor_tensor(out=af.t[:], in0=posf.t[:], scalar=16.0 * k, in1=af.t[:], op0=ALU.is_ge, op1=ALU.add),
                         reads=[posf, af], writes=[af])
                P.op("dve", lambda e: e.scalar_tensor_tensor(out=bf.t[:], in0=af.t[:], scalar=-16.0, in1=posf.t[:], op0=ALU.mult, op1=ALU.add),
                     reads=[posf, af], writes=[bf])
                yield
                oh = P.view("oh", cand.t[:].rearrange("p h (a b) -> p h a b", b=16))
                io4 = iota16.t[:].unsqueeze(1).unsqueeze(1).to_broadcast([128, 8, 16, 16])
                for (sel, c, dst) in ((af, 0, i0), (bf, 1, i1)):
                    P.op("dve", lambda e, sel=sel: e.tensor_tensor(out=oh.t[:], in0=sel.t[:].unsqueeze(3).to_broadcast([128, 8, 16, 16]), in1=io4, op=ALU.is_equal),
                         reads=[sel, iota16], writes=[cand])
                    P.op("dve", lambda e, c=c: e.tensor_tensor(out=oh.t[:], in0=oh.t[:], in1=sif4[:, :, c, :].unsqueeze(2).to_broadcast([128, 8, 16, 16]), op=ALU.mult),
                         reads=[cand, sif], writes=[cand])
                    P.op("dve", lambda e, dst=dst: e.tensor_reduce(out=dst.t[:], in_=oh.t[:], axis=AX.X, op=ALU.add), reads=[cand], writes=[dst])
                    yield
                P.op("dve", lambda e: e.tensor_scalar(out=i0.t[:], in0=i0.t[:], scalar1=128.0, scalar2=None, op0=ALU.mult),
                     reads=[i0], writes=[i0])
                P.op("dve", lambda e: e.tensor_tensor(out=eidx.t[:].rearrange("p (h k) -> p h k", k=16), in0=i0.t[:], in1=i1.t[:], op=ALU.add),
                     reads=[i0, i1], writes=[eidx])

            gb_rr = [0]

            def experts(j, gen=None):
                hn, eidx, gw = HN[j % 2], EIDX[j % 2], GW[j % 2]
                for g0 in range(0, 128, GRP):
                    gbs = []
                    for s in range(g0, g0 + GRP):
                        gb = GB[gb_rr[0] % NB]
                        gb_rr[0] += 1
                        gbs.append(gb)
                        P.gather(gb.t[:], PUV[l].t, eidx.t[:, s:s + 1], reads=[eidx, PUV[l]], writes=[gb])
                        P.op("dve", lambda e, s=s, gb=gb: e.scalar_tensor_tensor(out=junk.t[:], in0=gb.t[:, 0:1024], scalar=1.0, in1=hn.t[:], op0=ALU.mult, op1=ALU.mult,
                                                                                 accum_out=actp.t[:, s:s + 1]),
                             reads=[gb, hn], writes=[junk, actp])
                    P.op("act", lambda e, g0=g0: e.activation(gel.t[:, g0:g0 + GRP], actp.t[:, g0:g0 + GRP], AF.Gelu), reads=[actp], writes=[gel])
                    for k, s in enumerate(range(g0, g0 + GRP)):
                        gb = gbs[k]
                        dg = DG[s % 3]
                        P.op("dve", lambda e, s=s, dg=dg: e.tensor_scalar(out=dg.t[:], in0=ident_f.t[:], scalar1=gel.t[:, s:s + 1], scalar2=gw.t[:, s:s + 1],
                                                                          op0=ALU.mult, op1=ALU.mult),
                             reads=[ident_f, gel, gw], writes=[dg])
                        for n in range(2):
                            P.op("pe", lambda e, n=n, s=s, dg=dg, gb=gb: e.matmul(BK[6 + n].t[:, 0:512], dg.t[:], gb.t[:, 1024 + n * 512:1024 + (n + 1) * 512],
                                                                                   start=(s == 0), stop=(s == 127)),
                                 reads=[dg, gb], writes=[BK[6 + n]])
                    if gen is not None and g0 >= 8:
                        for _ in range(2):
                            next(gen, None)
                for n in range(2):
                    x_add(XT[j], j, n, BK[6 + n])
                x_end(j, XT[j])

            nt = min(n_tiles_peer, NALL if l < 2 else NTILE)
            for _ in routing(0):
                pass
            for j in range(nt):
                gen = routing(j + 1) if j + 1 < nt else None
                experts(j, gen)
                if gen is not None:
                    for _ in gen:
                        pass
        XSB[0] = XS

    def kv_phase():
        with P.scope():
            wk_ = P.sb("w_k", [128, 8, 1024], BF16)
            wv_ = P.sb("w_v", [128, 8, 1024], BF16)
            g_kv = P.sb("g_kv", [128, 1024], F32)
            g_kn = P.sb("g_kn", [128, 64], F32)
            hnb = P.sb("hnb", [128, 1024], BF16)
            hnT = P.sb("hnT", [128, 8, 128], BF16)
            KF = [P.sb(f"kf{i}", [128, 1024], F32) for i in range(2)]
            VF = [P.sb(f"vf{i}", [128, 1024], F32) for i in range(2)]
            sq = P.sb("sq", [128, 1024], F32)
            st = P.sb("st", [128, 48], F32)
            knb = P.sb("knb", [128, 8, 2, 64], BF16)
            KT = [P.sb(f"kt{i}", [128, 8, 128], BF16) for i in range(2)]
            VE = [P.sb(f"ve{i}", [128, 8, 129], BF16) for i in range(2)]
            load_w_bf16(wk_, w_k, 8)
            load_w_bf16(wv_, w_v, 8)
            bcast_load(g_kv, kv_norm_g[0])
            bcast_load(g_kn, k_norm_g[0])
            for i in range(2):
                P.op("pool", lambda e, i=i: e.memset(VE[i].t[:], 1.0), writes=[VE[i]])
            for j in range(NALL):
                xt = x_begin(j)
                own = j < NPT
                slot = 2 * j if own else 2 * (j - NTILE) + 1
                kf = kn_s if j == 16 else KF[j % 2]
                vf = v_s if j == 16 else VF[j % 2]
                rs = rstd_of(xt, xt.t[:], sq, sq.t[:], 1024, 12)
                P.op("dve", lambda e: e.scalar_tensor_tensor(out=hnb.t[:], in0=xt.t[:], scalar=rs, in1=g_kv.t[:], op0=ALU.mult, op1=ALU.mult),
                     reads=[xt, ss, g_kv], writes=[hnb])
                transposes(hnT, hnT.t, hnb, lambda i: hnb.t[:, i * 128:(i + 1) * 128], 8)
                proj(hnT, hnT.t, wk_, wk_.t, 8, 1024,
                     lambda n, bk: P.op("act", lambda e: e.activation(kf.t[:, n * 512:(n + 1) * 512], bk.t[:, 0:512], AF.Copy), reads=[bk], writes=[kf]))
                proj(hnT, hnT.t, wv_, wv_.t, 8, 1024,
                     lambda n, bk: P.op("act", lambda e: e.activation(vf.t[:, n * 512:(n + 1) * 512], bk.t[:, 0:512], AF.Copy), reads=[bk], writes=[vf]))
                k3 = kf.t[:].rearrange("p (a b) -> p a b", b=64)
                rg = group_rstd(kf, k3, sq, sq.t[:].rearrange("p (a b) -> p a b", b=64), 16, 64, st, 0)
                P.op("dve", lambda e: e.tensor_tensor(out=k3, in0=k3, in1=rg.unsqueeze(2).to_broadcast([128, 16, 64]), op=ALU.mult),
                     reads=[kf, st], writes=[kf])
                P.op("dve", lambda e: e.tensor_tensor(out=k3, in0=k3, in1=g_kn.t[:].unsqueeze(1).to_broadcast([128, 16, 64]), op=ALU.mult),
                     reads=[kf, g_kn], writes=[kf])
                if j == 16:
                    P.dma("sync", nks, kf.t[0:16, :], reads=[kf], out=True)
                    P.dma("sync", nvs, vf.t[0:16, :], reads=[vf], out=True)
                    continue
                if own:
                    P.dma("sync", nkp[j * 128:(j + 1) * 128, :], kf.t[:], reads=[kf], out=True)
                    P.dma("sync", nvp[j * 128:(j + 1) * 128, :], vf.t[:], reads=[vf], out=True)
                kt, ve = KT[j % 2], VE[j % 2]
                P.op("act", lambda e: e.activation(knb.t[:], kf.t[:].rearrange("p (m h d) -> p h m d", m=2, d=64), AF.Copy), reads=[kf], writes=[knb])
                transposes(kt, kt.t, knb, lambda h: knb.t[:, h, :, :].rearrange("p m d -> p (m d)"), 8)
                P.op("act", lambda e: e.activation(ve.t[:, :, 0:128], vf.t[:].rearrange("p (h d) -> p h d", d=128), AF.Copy), reads=[vf], writes=[ve])
                P.dma("sync", kvx_all.t[slot, :, 0:1024], kt.t[:].rearrange("p a b -> p (a b)"), reads=[kt], writes=[kvx_all], prim=kvx_all)
                P.dma("sync", kvx_all.t[slot, :, 1024:2056], ve.t[:].rearrange("p a b -> p (a b)"), reads=[ve], writes=[kvx_all], prim=kvx_all)

    def attn_layer(li):
        l = 2 + li
        lam0 = _lam_init(l)
        slopes = [2.0 ** (-8.0 * (h + 1) / 8) for h in range(8)]
        with P.scope():
            wq_ = P.sb("w_q", [128, 8, 1024], BF16)
            wo_ = P.sb("w_o", [128, 8, 1024], BF16)
            g_b = P.sb("g_b", [128, 1024], F32)
            g_qn = P.sb("g_qn", [128, 64], F32)
            g_sub = P.sb("g_sub", [128, 128], F32)
            lamv = P.sb("lamv", [128, 256], F32)
            lamc = P.sb("lamc", [128, 8], F32)
            hfc = P.sb("hfc", [128, 1], F32)
            biasT = P.sb("biasT", [128, 16, 2, 8], F32)
            bbase = P.sb("bbase", [128, 16, 2], F32)
            sbias = P.sb("sbias", [128, 16, 16], F32)
            sbase = P.sb("sbase", [128, 16], F32)
            mskf = P.sb("mskf", [128, 2, 128], F32)
            mskb = P.sb("mskb", [128, 2, 128], BF16)
            ptb = P.sb("ptb", [128, 256], I32)
            rowidx = P.sb("rowidx", [128, 256], I32)
            ones1 = P.sb("ones1", [128, 1], F32)
            bmask = P.sb("bmask", [128, 8], F32)
            hnb = P.sb("hnb", [128, 1024], BF16)
            hnT = P.sb("hnT", [128, 8, 128], BF16)
            qf = P.sb("qf", [128, 1024], F32)
            sq = P.sb("sq", [128, 1024], F32)
            st = P.sb("st", [128, 64], F32)
            qnb = P.sb("qnb", [128, 8, 2, 64], BF16)
            QT = P.sb("QT", [128, 8, 128], BF16)
            QTm = [P.sb(f"QTm{m}", [128, 8, 128], BF16) for m in range(2)]
            KV = [P.sb(f"kv{i}", [128, 2056], BF16) for i in range(3)]
            PT = [P.sb(f"pt{i}", [128, 256], BF16) for i in range(4)]
            oev = P.sb("oev", [128, 16, 129], F32)
            rz = P.sb("rz", [128, 16], F32)
            o0 = P.sb("o0", [128, 8, 128], F32)
            o1 = P.sb("o1", [128, 8, 128], F32)
            onb = P.sb("onb", [128, 1024], BF16)
            oT = P.sb("oT", [128, 8, 128], BF16)
            qbc = P.sb("qbc", [128, 1024], F32)
            KP = [P.sb(f"kp{i}", [128, 1024], F32) for i in range(2)]
            VP = [P.sb(f"vp{i}", [128, 1024], F32) for i in range(2)]
            sc16 = P.sb("sc16", [128, 32], F32)
            E16 = [P.sb(f"e16_{i}", [128, 16], F32) for i in range(2)]
            dall = P.sb("dall", [16, 16, 129], F32)
            enew = P.sb("enew", [128, 16], F32)
            ST = [P.view(f"st{i}", BK[6 + i].t[:, 0:256]) for i in range(2)]

            load_w_bf16(wq_, w_q[li], 8)
            load_w_bf16(wo_, w_o[li], 8)
            bcast_load(g_b, b_norm_g[li])
            bcast_load(g_qn, q_norm_g[li])
            bcast_load(g_sub, subln_g[li])
            bcast_load(lamv, lam_in[li])
            P.dma("sync", hfc.t[:], hfcol_in, writes=[hfc])
            P.dma("sync", mskf.t[:], maskadd, writes=[mskf])
            P.op("dve", lambda e: e.tensor_copy(mskb.t[:], mskf.t[:]), reads=[mskf], writes=[mskb])
            P.dma("sync", ptb.t[:], pt[0].partition_broadcast(128), writes=[ptb])
            P.op("dve", lambda e: e.scalar_tensor_tensor(out=rowidx.t[:], in0=ptb.t[:], scalar=128.0, in1=pcol.t[:, 0:1].to_broadcast([128, 256]),
                                                         op0=ALU.mult, op1=ALU.add), reads=[ptb, pcol], writes=[rowidx])
            P.op("pool", lambda e: e.memset(ones1.t[:], 1.0), writes=[ones1])
            P.op("dve", lambda e: e.scalar_tensor_tensor(out=sq.t[:, 0:64], in0=lamv.t[:, 0:64], scalar=1.0, in1=lamv.t[:, 64:128], op0=ALU.mult, op1=ALU.mult,
                                                         accum_out=lamc.t[:, 0:1]), reads=[lamv], writes=[sq, lamc])
            P.op("dve", lambda e: e.scalar_tensor_tensor(out=sq.t[:, 0:64], in0=lamv.t[:, 128:192], scalar=1.0, in1=lamv.t[:, 192:256], op0=ALU.mult, op1=ALU.mult,
                                                         accum_out=lamc.t[:, 1:2]), reads=[lamv], writes=[sq, lamc])
            P.op("act", lambda e: e.activation(lamc.t[:, 2:4], lamc.t[:, 0:2], AF.Exp), reads=[lamc], writes=[lamc])
            P.op("dve", lambda e: e.tensor_tensor(out=lamc.t[:, 4:5], in0=lamc.t[:, 3:4], in1=lamc.t[:, 2:3], op=ALU.subtract), reads=[lamc], writes=[lamc])
            P.op("dve", lambda e: e.tensor_scalar(out=lamc.t[:, 4:5], in0=lamc.t[:, 4:5], scalar1=-lam0, scalar2=None, op0=ALU.add), reads=[lamc], writes=[lamc])
            neg_lam = lamc.t[:, 4:5]
            P.op("pool", lambda e: e.iota(bbase.t[:], [[-256, 16], [0, 2]], base=-64, channel_multiplier=1, allow_small_or_imprecise_dtypes=True), writes=[bbase])
            P.op("dve", lambda e: e.tensor_scalar(out=bbase.t[:, :, 1], in0=bbase.t[:, :, 1], scalar1=hfc.t[:, 0:1], scalar2=None, op0=ALU.add), reads=[bbase, hfc], writes=[bbase])
            for h in range(8):
                P.op("dve", lambda e, h=h: e.tensor_scalar(out=biasT.t[:, :, :, h], in0=bbase.t[:], scalar1=slopes[h], scalar2=None, op0=ALU.mult),
                     reads=[bbase], writes=[biasT])
            P.op("pool", lambda e: e.iota(sbase.t[:], [[128, 16]], base=-2048, channel_multiplier=1, allow_small_or_imprecise_dtypes=True), writes=[sbase])
            for mh in range(16):
                P.op("dve", lambda e, mh=mh: e.tensor_scalar(out=sbias.t[:, :, mh], in0=sbase.t[:], scalar1=slopes[mh % 8], scalar2=None, op0=ALU.mult),
                     reads=[sbase], writes=[sbias])
            P.op("dve", lambda e: e.tensor_tensor(out=bmask.t[0:16, :], in0=ident_f.t[0:16, 0:8], in1=ident_f.t[0:16, 8:16], op=ALU.add),
                 reads=[ident_f], writes=[bmask])

            for m in range(2):
                P.op("pool", lambda e, m=m: e.memset(QTm[m].t[:], 0.0), writes=[QTm[m]])

            def q_side(j):
                xt = X[j]
                rs = rstd_of(xt, xt.t[:], sq, sq.t[:], 1024, 16)
                P.op("dve", lambda e: e.scalar_tensor_tensor(out=hnb.t[:], in0=xt.t[:], scalar=rs, in1=g_b.t[:], op0=ALU.mult, op1=ALU.mult),
                     reads=[xt, ss, g_b], writes=[hnb])
                transposes(hnT, hnT.t, hnb, lambda i: hnb.t[:, i * 128:(i + 1) * 128], 8)
                proj(hnT, hnT.t, wq_, wq_.t, 8, 1024,
                     lambda n, bk: P.op("act", lambda e: e.activation(qf.t[:, n * 512:(n + 1) * 512], bk.t[:, 0:512], AF.Copy), reads=[bk], writes=[qf]))
                q3 = qf.t[:].rearrange("p (a b) -> p a b", b=64)
                rg = group_rstd(qf, q3, sq, sq.t[:].rearrange("p (a b) -> p a b", b=64), 16, 64, st, 0)
                P.op("dve", lambda e: e.tensor_tensor(out=q3, in0=q3, in1=rg.unsqueeze(2).to_broadcast([128, 16, 64]), op=ALU.mult),
                     reads=[qf, st], writes=[qf])
                P.op("dve", lambda e: e.tensor_tensor(out=q3, in0=q3, in1=g_qn.t[:].unsqueeze(1).to_broadcast([128, 16, 64]), op=ALU.mult),
                     reads=[qf, g_qn], writes=[qf])

            def epilogue(j):
                P.op("dve", lambda e: e.reciprocal(rz.t[:], oev.t[:, :, 128]), reads=[oev], writes=[rz])
                P.op("dve", lambda e: e.tensor_tensor(out=o0.t[:], in0=oev.t[:, 0:8, 0:128], in1=rz.t[:, 0:8].unsqueeze(2).to_broadcast([128, 8, 128]), op=ALU.mult),
                     reads=[oev, rz], writes=[o0])
                P.op("dve", lambda e: e.tensor_tensor(out=o1.t[:], in0=oev.t[:, 8:16, 0:128], in1=rz.t[:, 8:16].unsqueeze(2).to_broadcast([128, 8, 128]), op=ALU.mult),
                     reads=[oev, rz], writes=[o1])
                of = o0.t[:].rearrange("p a b -> p (a b)")
                P.op("dve", lambda e: e.scalar_tensor_tensor(out=of, in0=o1.t[:].rearrange("p a b -> p (a b)"), scalar=neg_lam, in1=of, op0=ALU.mult, op1=ALU.add),
                     reads=[o0, o1, lamc], writes=[o0])
                rg = group_rstd(o0, o0.t[:], o1, o1.t[:], 8, 128, st, 48 - 24)
                P.op("dve", lambda e: e.tensor_tensor(out=o0.t[:], in0=o0.t[:], in1=rg.unsqueeze(2).to_broadcast([128, 8, 128]), op=ALU.mult),
                     reads=[o0, st], writes=[o0])
                P.op("dve", lambda e: e.scalar_tensor_tensor(out=onb.t[:].rearrange("p (a b) -> p a b", b=128), in0=o0.t[:], scalar=1.0 - lam0,
                                                             in1=g_sub.t[:].unsqueeze(1).to_broadcast([128, 8, 128]), op0=ALU.mult, op1=ALU.mult),
                     reads=[o0, g_sub], writes=[onb])
                transposes(oT, oT.t, onb, lambda i: onb.t[:, i * 128:(i + 1) * 128], 8)
                proj(oT, oT.t, wo_, wo_.t, 8, 1024, lambda n, bk: x_add(X[j], j, n, bk))

            kv_rr = [0]
            pt_rr = [0]
            for i in range(dbg_qtiles):
                q_side(i)
                P.op("act", lambda e: e.activation(qnb.t[:], qf.t[:].rearrange("p (m h d) -> p h m d", m=2, d=64), AF.Copy), reads=[qf], writes=[qnb])
                transposes(QT, QT.t, qnb, lambda h: qnb.t[:, h, :, :].rearrange("p m d -> p (m d)"), 8)
                for m in range(2):
                    P.op("dve", lambda e, m=m: e.tensor_copy(QTm[m].t[m * 64:(m + 1) * 64], QT.t[m * 64:(m + 1) * 64]), reads=[QT], writes=[QTm[m]])
                nkb = 2 * i + 2
                if dbg_att < 2:
                    continue
                bank_started = set()
                for kb in range(nkb):
                    kvb = KV[kv_rr[0] % 3]
                    kv_rr[0] += 1
                    P.dma("sync", kvb.t[:], kvx_all.t[kb], reads=[kvx_all], writes=[kvb])
                    ip, w = kb // 2, kb % 2
                    di = i - ip
                    special = (ip == i) or FORCE_SPECIAL
                    for h in range(8):
                        stv = ST[pt_rr[0] % 2]
                        ptb_ = PT[pt_rr[0] % 4]
                        pt_rr[0] += 1
                        for m in range(2):
                            P.op("pe", lambda e, m=m, h=h: e.matmul(stv.t[:, m * 128:(m + 1) * 128], kvb.t[:, h * 128:(h + 1) * 128],
                                                                    QTm[m].t[:, h, :], start=True, stop=not special),
                                 reads=[kvb, QTm[m]], writes=[stv])
                            if special:
                                P.op("pe", lambda e, m=m: e.matmul(stv.t[:, m * 128:(m + 1) * 128], ident_b.t[:], mskb.t[:, w, :], start=False, stop=True),
                                     reads=[ident_b, mskb], writes=[stv])
                        P.op("act", lambda e, h=h: e.activation(ptb_.t[:], stv.t[:], AF.Exp, bias=biasT.t[:, di, w, h:h + 1], scale=0.125),
                             reads=[stv, biasT], writes=[ptb_])
                        for m in range(2):
                            if dbg_att < 3:
                                continue
                            gi = m * 8 + h
                            ob = BK[gi // 3]
                            first = (gi // 3) not in bank_started
                            bank_started.add(gi // 3)
                            P.op("pe", lambda e, m=m, h=h, gi=gi, ob=ob, first=first: e.matmul(ob.t[:, (gi % 3) * 129:(gi % 3) * 129 + 129], ptb_.t[:, m * 128:(m + 1) * 128],
                                                                                kvb.t[:, 1024 + h * 129:1024 + (h + 1) * 129], start=first, stop=(kb == nkb - 1),
                                                                                skip_group_check=True),
                                 reads=[ptb_, kvb], writes=[ob])
                if dbg_att < 4:
                    continue
                for b in range(6):
                    n3 = 3 if b < 5 else 1
                    P.op("act" if b % 2 else "dve",
                         (lambda e, b=b, n3=n3: e.activation(oev.t[:, b * 3:b * 3 + n3, :], BK[b].t[:, 0:n3 * 129].rearrange("p (a c) -> p a c", c=129), AF.Copy)) if b % 2 else
                         (lambda e, b=b, n3=n3: e.tensor_copy(oev.t[:, b * 3:b * 3 + n3, :], BK[b].t[:, 0:n3 * 129].rearrange("p (a c) -> p a c", c=129))),
                         reads=[BK[b]], writes=[oev])
                epilogue(i)

            if dbg_skip_sample:
                return
            q_side(16)
            q3 = qf.t[:].rearrange("p (a b) -> p a b", b=64)
            P.dma("sync", dq.t, qf.t[0:16, :], reads=[qf], writes=[dq])
            P.op("dve", lambda e: e.tensor_tensor(out=sq.t[:], in0=qf.t[:], in1=kn_s.t[:], op=ALU.mult), reads=[qf, kn_s], writes=[sq])
            P.op("dve", lambda e: e.tensor_reduce(out=enew.t[:], in_=sq.t[:].rearrange("p (a b) -> p a b", b=64), axis=AX.X, op=ALU.add), reads=[sq], writes=[enew])
            P.op("act", lambda e: e.activation(enew.t[:], enew.t[:], AF.Exp, scale=0.125), reads=[enew], writes=[enew])
            pg_rr = [0]
            rtmp_ap = o1.t[0:16, :, :]
            for s in range(16):
                P.dma("sync", qbc.t[:], dq.t[s].partition_broadcast(128), reads=[dq], writes=[qbc])
                for pg in range(16):
                    kp, vp, e16 = KP[pg_rr[0] % 2], VP[pg_rr[0] % 2], E16[pg_rr[0] % 2]
                    pg_rr[0] += 1
                    P.gather(kp.t[:], cache_k, rowidx.t[:, s * 16 + pg:s * 16 + pg + 1], reads=[rowidx], writes=[kp])
                    P.gather(vp.t[:], cache_v, rowidx.t[:, s * 16 + pg:s * 16 + pg + 1], reads=[rowidx], writes=[vp])
                    P.op("dve", lambda e, kp=kp: e.tensor_tensor(out=sq.t[:], in0=kp.t[:], in1=qbc.t[:], op=ALU.mult), reads=[kp, qbc], writes=[sq])
                    P.op("dve", lambda e: e.tensor_reduce(out=sc16.t[:, 0:16], in_=sq.t[:].rearrange("p (a b) -> p a b", b=64), axis=AX.X, op=ALU.add),
                         reads=[sq], writes=[sc16])
                    P.op("dve", lambda e, pg=pg: e.scalar_tensor_tensor(out=sc16.t[:, 16:32], in0=sc16.t[:, 0:16], scalar=0.125, in1=sbias.t[:, pg, :], op0=ALU.mult, op1=ALU.add),
                         reads=[sc16, sbias], writes=[sc16])
                    P.op("act", lambda e, e16=e16: e.activation(e16.t[:], sc16.t[:, 16:32], AF.Exp), reads=[sc16], writes=[e16])
                    for n in range(2):
                        P.op("pe", lambda e, n=n, e16=e16, vp=vp, pg=pg: e.matmul(BK[n].t[0:16, 0:512], e16.t[:], vp.t[:, n * 512:(n + 1) * 512], start=(pg == 0), stop=(pg == 15)),
                             reads=[e16, vp], writes=[BK[n]])
                    P.op("pe", lambda e, e16=e16, pg=pg: e.matmul(BK[2].t[0:16, 0:1], e16.t[:], ones1.t[:], start=(pg == 0), stop=(pg == 15)),
                         reads=[e16, ones1], writes=[BK[2]])
                for n in range(2):
                    P.op("dve", lambda e, n=n: e.tensor_tensor(out=rtmp_ap[:, n * 4:(n + 1) * 4, :], in0=BK[n].t[0:16, 0:512].rearrange("p (a b) -> p a b", b=128),
                                                               in1=bmask.t[0:16, n * 4:(n + 1) * 4].unsqueeze(2).to_broadcast([16, 4, 128]), op=ALU.mult),
                         reads=[BK[n], bmask], writes=[o1])
                P.op("dve", lambda e, s=s: e.tensor_reduce(out=dall.t[:, s, 0:128], in_=rtmp_ap.rearrange("p h d -> p d h"), axis=AX.X, op=ALU.add),
                     reads=[o1], writes=[dall])
                P.op("dve", lambda e, s=s: e.tensor_copy(dall.t[:, s, 128:129], BK[2].t[0:16, 0:1]), reads=[BK[2]], writes=[dall])
            P.op("pool", lambda e: e.memset(oev.t[:], 1.0), writes=[oev])
            P.dma("sync", dD.t, dall.t[:], reads=[dall], writes=[dD])
            P.dma("sync", oev.t[0:16, :, :], dD.t.rearrange("m s d -> s m d"), reads=[dD], writes=[oev])
            v3 = v_s.t[:].rearrange("p (h d) -> p h d", d=128)
            for m in range(2):
                P.op("dve", lambda e, m=m: e.tensor_tensor(out=o1.t[:], in0=v3, in1=enew.t[:, m * 8:(m + 1) * 8].unsqueeze(2).to_broadcast([128, 8, 128]), op=ALU.mult),
                     reads=[v_s, enew], writes=[o1])
                P.op("dve", lambda e, m=m: e.tensor_tensor(out=oev.t[:, m * 8:(m + 1) * 8, 0:128], in0=oev.t[:, m * 8:(m + 1) * 8, 0:128], in1=o1.t[:], op=ALU.add),
                     reads=[oev, o1], writes=[oev])
            P.op("dve", lambda e: e.tensor_tensor(out=oev.t[:, :, 128], in0=oev.t[:, :, 128], in1=enew.t[:], op=ALU.add), reads=[oev, enew], writes=[oev])
            epilogue(16)

    steps = {"g0": lambda: gmlp_layer(0), "p0": lambda: peer_layer(0), "g1": lambda: gmlp_layer(1), "p1": lambda: peer_layer(1),
             "kv": kv_phase, "a0": lambda: attn_layer(0), "p2": lambda: peer_layer(2), "a1": lambda: attn_layer(1), "p3": lambda: peer_layer(3)}
    if "A" in stages:
        stages = ["g0", "p0", "g1", "p1"] + [x for x in stages if x != "A"]
    if "KV" in stages:
        stages = [("kv" if x == "KV" else x) for x in stages]
    if "B" in stages:
        stages = [x for x in stages if x != "B"] + ["a0", "p2", "a1", "p3"]
    kn_alloc = [False]
    for st_ in stages:
        if st_ in ("kv", "a0", "a1") and not kn_alloc[0]:
            kn_alloc[0] = True
            kn_s = P.sb("kn_s", [128, 1024], F32)
            v_s = P.sb("v_s", [128, 1024], F32)
        steps[st_]()
    for j in range(NTILE):
        P.dma("sync", y[j * 128:(j + 1) * 128, :], X[j].t[:], reads=[X[j]], out=True)
    P.finish()
    return nc, P


_CACHE = {}


def kernel(x_prompt, x_sample, cache_k, cache_v, page_table,
           a_norm_g, a_w_in, a_vnorm_g, a_w_s, a_b_s, a_w_out,
           kv_norm_g, w_k, w_v, k_norm_g,
           b_norm_g, w_q, q_norm_g, lambda_q1, lambda_k1, lambda_q2, lambda_k2, subln_g, w_o,
           f_norm_g, peer_w_q, peer_keys, peer_u, peer_v):
    f = lambda a: np.ascontiguousarray(np.asarray(a, dtype=np.float32))
    xp = f(x_prompt)
    xs = f(x_sample).reshape(128, 1024)
    if "nc" not in _CACHE:
        _CACHE["nc"] = build_program()[0]
    nc = _CACHE.get("nc_override", _CACHE["nc"])
    lam_in = np.ascontiguousarray(np.stack([f(lambda_q1), f(lambda_k1), f(lambda_q2), f(lambda_k2)], axis=1).reshape(2, 256))
    shared = {
        "cache_k": f(cache_k).reshape(2560 * 128, 1024), "cache_v": f(cache_v).reshape(2560 * 128, 1024),
        "a_norm_g": f(a_norm_g), "a_w_in": f(a_w_in), "a_vnorm_g": f(a_vnorm_g), "a_w_s": f(a_w_s), "a_b_s": f(a_b_s),
        "a_w_out": f(a_w_out), "kv_norm_g": f(kv_norm_g).reshape(1, 1024), "w_k": f(w_k), "w_v": f(w_v),
        "k_norm_g": f(k_norm_g).reshape(1, 64), "b_norm_g": f(b_norm_g), "w_q": f(w_q), "q_norm_g": f(q_norm_g),
        "lam_in": lam_in, "subln_g": f(subln_g), "w_o": f(w_o), "f_norm_g": f(f_norm_g), "peer_w_q": f(peer_w_q),
        "peer_keys": f(peer_keys).reshape(4, 16, 128, 128), "peer_u": f(peer_u).reshape(4 * N_EXP, 1024),
        "peer_v": f(peer_v).reshape(4 * N_EXP, 1024),
    }
    ptab = np.ascontiguousarray(np.asarray(page_table, dtype=np.int32))
    kk = np.arange(128)[:, None]
    qq = np.arange(128)[None, :]
    causal = np.where(kk <= qq, 0.0, -30000.0).astype(np.float32)
    in_maps = []
    for c in range(8):
        b, hf = c // 2, c % 2
        xin = np.zeros((NALL * 128, 1024), np.float32)
        xin[:NPT * 128] = xp[b].reshape(16, 2, 128, 1024)[:, hf].reshape(NPT * 128, 1024)
        xin[NPT * 128:NPT * 128 + 16] = xs[c * 16:(c + 1) * 16]
        xin[NTILE * 128:] = xp[b].reshape(16, 2, 128, 1024)[:, 1 - hf].reshape(NPT * 128, 1024)
        msk = np.zeros((128, 2, 128), np.float32)
        msk[:, 0, :] = causal
        msk[:, 1, :] = -30000.0 if hf == 0 else 0.0
        m = dict(shared)
        m["xin"] = xin
        m["pt"] = ptab[c * 16:(c + 1) * 16].reshape(1, 256)
        m["maskadd"] = msk
        m["hfcol"] = np.full((128, 1), 128.0 * (1 - 2 * hf), np.float32)
        in_maps.append(m)
    res = run_bass_kernel_spmd(nc, in_maps, core_ids=list(range(8)))
    R = res.results
    y_prompt = np.zeros((4, 4096, 1024), np.float32)
    nk_p = np.zeros((4, 4096, 1024), np.float32)
    nv_p = np.zeros((4, 4096, 1024), np.float32)
    y_sample = np.zeros((128, 1024), np.float32)
    nk_s = np.zeros((128, 1024), np.float32)
    nv_s = np.zeros((128, 1024), np.float32)
    gv = np.zeros((2, 128, 2048), np.float32)
    for c in range(8):
        b, hf = c // 2, c % 2
        r = R[c]
        y_prompt[b].reshape(16, 2, 128, 1024)[:, hf] = r["y"][:NPT * 128].reshape(16, 128, 1024)
        nk_p[b].reshape(16, 2, 128, 1024)[:, hf] = r["nkp"].reshape(16, 128, 1024)
        nv_p[b].reshape(16, 2, 128, 1024)[:, hf] = r["nvp"].reshape(16, 128, 1024)
        y_sample[c * 16:(c + 1) * 16] = r["y"][NPT * 128:NPT * 128 + 16]
        nk_s[c * 16:(c + 1) * 16] = r["nks"]
        nv_s[c * 16:(c + 1) * 16] = r["nvs"]
        gv[:, c * 16:(c + 1) * 16] = r["gv"]
    return (y_prompt, y_sample.reshape(128, 1, 1024), nk_p.reshape(4, 4096, 16, 64), nv_p.reshape(4, 4096, 8, 128),
            nk_s.reshape(128, 1, 16, 64), nv_s.reshape(128, 1, 8, 128), gv.reshape(2, 128, 1, 2048))
```

```python
import math
import numpy as np
import concourse.bass as bass
import concourse.mybir as mybir
from concourse.bass_utils import run_bass_kernel_spmd

F32 = mybir.dt.float32
BF16 = mybir.dt.bfloat16
I32 = mybir.dt.int32
U32 = mybir.dt.uint32
ALU = mybir.AluOpType
AF = mybir.ActivationFunctionType
AX = mybir.AxisListType

SEM_ROT = 30000
import os as _os
FORCE_SPECIAL = bool(int(_os.environ.get('FORCE_SPECIAL', '0')))
N_DSEM = 72
EPS = 1e-6
NTILE = 17
NALL = 33
NPT = 16
DEPTH = 4
N_EXP = 16384


class Res:
    __slots__ = ("name", "t", "w", "r", "dslot")

    def __init__(self, name, t=None):
        self.name = name
        self.t = t
        self.w = None
        self.r = []
        self.dslot = None


class Prog:
    def __init__(self, nc):
        self.nc = nc
        self.eng = {"pe": nc.tensor, "dve": nc.vector, "act": nc.scalar, "pool": nc.gpsimd, "sync": nc.sync}
        self.sem = {}
        self.cnt = {}
        self.nsem = 0
        self._keep = []
        self._scopes = []
        for k in ("pe", "dve", "act", "pool"):
            self._new_eng_sem(k)
        self.dfree = [[self._alloc_sem(f"d{i}"), 0] for i in range(N_DSEM)]
        self.dall = list(self.dfree)
        self.seen = {k: {} for k in self.eng}
        self.out_events = []
        self.n_inst = 0

    def _alloc_sem(self, name):
        g = self.nc.semaphore(name)
        s = g.__enter__()
        self.nsem += 1
        return s

    def _new_eng_sem(self, k):
        self.sem[k] = self._alloc_sem(f"s_{k}_{self.nsem}")
        self.cnt[k] = 0

    def _reg(self, res, g):
        if self._scopes:
            self._scopes[-1].append((res, g))
        else:
            self._keep.append((res, g))
        return res

    def sb(self, name, shape, dtype):
        self._uid = getattr(self, "_uid", 0) + 1
        name = f"{name}_{self._uid}"
        g = self.nc.sbuf_tensor(name, list(shape), dtype)
        return self._reg(Res(name, g.__enter__()), g)

    def ps(self, name, shape, dtype=F32):
        g = self.nc.psum_tensor(name, list(shape), dtype)
        return self._reg(Res(name, g.__enter__()), g)

    def dram(self, name, shape, dtype):
        t = self.nc.dram_tensor(name, list(shape), dtype, kind="Internal")
        return Res(name, t.ap())

    def view(self, name, ap):
        return Res(name, ap)

    def scope(self):
        prog = self

        class _S:
            def __enter__(s):
                prog._scopes.append([])

            def __exit__(s, *a):
                if a[0] is not None:
                    return False
                prog.barrier()
                items = prog._scopes.pop()
                for res, g in reversed(items):
                    if res.dslot is not None:
                        prog.dfree.append(res.dslot)
                        res.dslot = None
                    g.__exit__(None, None, None)
                return False
        return _S()

    def barrier(self):
        evs = []
        for k in ("pe", "dve", "act", "pool"):
            if self.cnt[k] > 0:
                evs.append((self.sem[k], self.cnt[k]))
        for sl in self.dall:
            if sl[1] > 0:
                evs.append((sl[0], 16 * sl[1]))
        for q in ("pe", "dve", "act", "pool", "sync"):
            for ev in evs:
                self._wait(q, ev)

    def _wait(self, k, ev):
        if ev is None:
            return
        sem, val = ev
        sid = id(sem)
        if self.seen[k].get(sid, 0) >= val:
            return
        self.eng[k].wait_ge(sem, val)
        self.seen[k][sid] = val
        self.n_inst += 1

    @staticmethod
    def _compact(evs):
        best = {}
        for sem, val in evs:
            sid = id(sem)
            if sid not in best or best[sid][1] < val:
                best[sid] = (sem, val)
        return list(best.values())

    def _commit(self, ev, reads, writes):
        for r in reads:
            r.r.append(ev)
            if len(r.r) > 16:
                r.r = self._compact(r.r)
        for w in writes:
            w.w = ev
            w.r = []

    def op(self, k, fn, reads=(), writes=()):
        if self.cnt[k] >= SEM_ROT:
            self._new_eng_sem(k)
        if k == "pe":
            self.seen[k][id(self.sem[k])] = 1 << 60
        for r in reads:
            self._wait(k, r.w)
        for w in writes:
            self._wait(k, w.w)
            for ev in w.r:
                self._wait(k, ev)
        inst = fn(self.eng[k])
        self.cnt[k] += 1
        inst.then_inc(self.sem[k], 1)
        ev = (self.sem[k], self.cnt[k])
        self._commit(ev, reads, writes)
        self.n_inst += 1
        return ev

    def _dma_event(self, prim):
        if prim.dslot is None:
            prim.dslot = self.dfree.pop(0)
        prim.dslot[1] += 1
        return (prim.dslot[0], 16 * prim.dslot[1])

    def _dma_deps(self, q, reads, writes):
        for r in reads:
            self._wait(q, r.w)
        for w in writes:
            if w.w is not None and not (w.dslot is not None and w.w[0] is w.dslot[0]):
                self._wait(q, w.w)
            for ev in w.r:
                self._wait(q, ev)

    def dma(self, q, out_ap, in_ap, reads=(), writes=(), out=False, prim=None, **kw):
        self._dma_deps(q, reads, writes)
        if prim is None:
            prim = (list(writes) + list(reads))[0]
        ev = self._dma_event(prim)
        kw.setdefault("allow_slow_non_contiguous", True)
        self.eng[q].dma_start(out=out_ap, in_=in_ap, **kw).then_inc(ev[0], 16)
        for r in reads:
            r.r.append(ev)
        for w in writes:
            w.w = ev
            w.r = []
        if out:
            self.out_events.append(ev)
        self.n_inst += 1
        return ev

    def gather(self, out_ap, table_ap, idx_ap, reads=(), writes=(), prim=None):
        q = "pool"
        self._dma_deps(q, reads, writes)
        if prim is None:
            prim = list(writes)[0]
        ev = self._dma_event(prim)
        self.nc.gpsimd.indirect_dma_start(
            out=out_ap, out_offset=None, in_=table_ap,
            in_offset=bass.IndirectOffsetOnAxis(ap=idx_ap, axis=0),
        ).then_inc(ev[0], 16)
        for r in reads:
            r.r.append(ev)
        for w in writes:
            w.w = ev
            w.r = []
        self.n_inst += 1
        return ev

    def collective(self, fn, reads, writes):
        q = "pool"
        self._dma_deps(q, reads, writes)
        ev = self._dma_event(list(writes)[0])
        fn(self.nc.gpsimd).then_inc(ev[0], 16)
        for r in reads:
            r.r.append(ev)
        for w in writes:
            w.w = ev
            w.r = []
        return ev

    def finish(self):
        for ev in self._compact(self.out_events):
            self._wait("sync", ev)
        self.barrier()


def _lam_init(l):
    return 0.8 - 0.6 * math.exp(-0.3 * l)


def build_program(stages=("A", "KV", "B"), n_tiles_peer=NALL, cache_rows=2560 * 128, dbg_skip_sample=False, dbg_qtiles=NPT, dbg_att=9):
    nc = bass.Bass("TRN2", target_bir_lowering=False)
    P = Prog(nc)
    stages = list(stages)

    def DI(name, shape, dt=F32):
        return nc.dram_tensor(name, list(shape), dt, kind="ExternalInput").ap()

    def DO(name, shape, dt=F32):
        return nc.dram_tensor(name, list(shape), dt, kind="ExternalOutput").ap()

    xin = DI("xin", [NALL * 128, 1024])
    cache_k = DI("cache_k", [cache_rows, 1024])
    cache_v = DI("cache_v", [cache_rows, 1024])
    pt = DI("pt", [1, 256], I32)
    a_norm_g = DI("a_norm_g", [2, 1024])
    a_w_in = DI("a_w_in", [2, 1024, 4096])
    a_vnorm_g = DI("a_vnorm_g", [2, 2048])
    a_w_s = DI("a_w_s", [2, 8, 128, 128])
    a_b_s = DI("a_b_s", [2, 8, 128])
    a_w_out = DI("a_w_out", [2, 2048, 1024])
    kv_norm_g = DI("kv_norm_g", [1, 1024])
    w_k = DI("w_k", [1024, 1024])
    w_v = DI("w_v", [1024, 1024])
    k_norm_g = DI("k_norm_g", [1, 64])
    b_norm_g = DI("b_norm_g", [2, 1024])
    w_q = DI("w_q", [2, 1024, 1024])
    q_norm_g = DI("q_norm_g", [2, 64])
    lam_in = DI("lam_in", [2, 256])
    subln_g = DI("subln_g", [2, 128])
    w_o = DI("w_o", [2, 1024, 1024])
    f_norm_g = DI("f_norm_g", [4, 1024])
    peer_w_q = DI("peer_w_q", [4, 1024, 2048])
    peer_keys = DI("peer_keys", [4, 16, 128, 128])
    peer_u = DI("peer_u", [4 * N_EXP, 1024])
    peer_v = DI("peer_v", [4 * N_EXP, 1024])
    maskadd = DI("maskadd", [128, 2, 128])
    hfcol_in = DI("hfcol", [128, 1])

    y = DO("y", [NTILE * 128, 1024])
    nkp = DO("nkp", [NPT * 128, 1024])
    nvp = DO("nvp", [NPT * 128, 1024])
    nks = DO("nks", [16, 1024])
    nvs = DO("nvs", [16, 1024])
    gvo = DO("gv", [2, 16, 2048])

    kvx_all = P.dram("kvx_all", [2 * NPT, 128, 2056], BF16)
    PUV = [P.dram(f"puv_b{l}", [N_EXP, 2048], BF16) for l in range(DEPTH)]
    xp = P.dram("xp", [NPT * 128, 1024], F32)
    XPR = [P.view(f"xp{k}", xp.t[k * 128:(k + 1) * 128, :]) for k in range(NPT)]
    dq = P.dram("dq", [16, 1024], F32)
    dD = P.dram("dD", [16, 16, 129], F32)

    X = [P.sb(f"x{j}", [128, 1024], F32) for j in range(NTILE)]
    ident_f = P.sb("ident_f", [128, 128], F32)
    ident_b = P.sb("ident_b", [128, 128], BF16)
    ss = P.sb("ss", [128, 64], F32)
    iota16 = P.sb("iota16", [128, 16], F32)
    pcol = P.sb("pcol", [128, 1], F32)
    BK = [P.ps(f"bank{b}", [128, 512], F32) for b in range(8)]
    bank_rr = [0]

    def nextbank(lo=0, hi=6):
        b = lo + bank_rr[0] % (hi - lo)
        bank_rr[0] += 1
        return BK[b]

    XS = [P.sb("xs0", [128, 1024], F32)]
    XSB = [XS]
    for j in range(NTILE):
        P.dma("sync", X[j].t[:], xin[j * 128:(j + 1) * 128, :], writes=[X[j]])
    for k in range(NPT):
        P.dma("sync", XPR[k].t, xin[(NTILE + k) * 128:(NTILE + k + 1) * 128, :], writes=[XPR[k]])

    for l in range(DEPTH):
        for (src, half) in ((peer_u, 0), (peer_v, 1)):
            for c in range(8):
                P.dma("pool", PUV[l].t[c * 2048:(c + 1) * 2048, half * 1024:(half + 1) * 1024],
                      src[l * N_EXP + c * 2048:l * N_EXP + (c + 1) * 2048, :], writes=[PUV[l]])

    def x_begin(j, buf=0):
        if j < NTILE:
            return X[j]
        r = XSB[0][buf]
        P.dma("sync", r.t[:], XPR[j - NTILE].t, reads=[XPR[j - NTILE]], writes=[r])
        return r

    def x_end(j, r):
        if j >= NTILE:
            P.dma("sync", XPR[j - NTILE].t, r.t[:], reads=[r], writes=[XPR[j - NTILE]])
    P.op("pool", lambda e: e.memset(ident_f.t[:], 1.0), writes=[ident_f])
    P.op("pool", lambda e: e.affine_select(out=ident_f.t[:], in_=ident_f.t[:], pattern=[[-1, 128]],
                                            compare_op=ALU.is_equal, fill=0.0, base=0, channel_multiplier=1),
         reads=[ident_f], writes=[ident_f])
    P.op("dve", lambda e: e.tensor_copy(ident_b.t[:], ident_f.t[:]), reads=[ident_f], writes=[ident_b])
    P.op("pool", lambda e: e.iota(iota16.t[:], [[1, 16]], base=0, channel_multiplier=0,
                                   allow_small_or_imprecise_dtypes=True), writes=[iota16])
    P.op("pool", lambda e: e.iota(pcol.t[:], [[0, 1]], base=0, channel_multiplier=1,
                                   allow_small_or_imprecise_dtypes=True), writes=[pcol])

    def rows(j):
        return 16 if j == 16 else 128

    def rstd_of(src_res, src_ap, junk_res, junk_ap, D, col):
        P.op("dve", lambda e: e.scalar_tensor_tensor(out=junk_ap, in0=src_ap, scalar=1.0, in1=src_ap,
                                                     op0=ALU.mult, op1=ALU.mult, accum_out=ss.t[:, col:col + 1]),
             reads=[src_res], writes=[junk_res, ss])
        P.op("dve", lambda e: e.tensor_scalar(out=ss.t[:, col + 1:col + 2], in0=ss.t[:, col:col + 1],
                                              scalar1=1.0 / D, scalar2=EPS, op0=ALU.mult, op1=ALU.add),
             reads=[ss], writes=[ss])
        P.op("act", lambda e: e.activation(ss.t[:, col + 2:col + 3], ss.t[:, col + 1:col + 2], AF.Sqrt),
             reads=[ss], writes=[ss])
        P.op("dve", lambda e: e.reciprocal(ss.t[:, col + 3:col + 4], ss.t[:, col + 2:col + 3]),
             reads=[ss], writes=[ss])
        return ss.t[:, col + 3:col + 4]

    def group_rstd(src_res, src3, sq_res, sq3, n, gd, st_res, base):
        P.op("dve", lambda e: e.tensor_tensor(out=sq3, in0=src3, in1=src3, op=ALU.mult),
             reads=[src_res], writes=[sq_res])
        a = st_res.t[:, base:base + n]
        b = st_res.t[:, base + n:base + 2 * n]
        c = st_res.t[:, base + 2 * n:base + 3 * n]
        P.op("dve", lambda e: e.tensor_reduce(out=a, in_=sq3, axis=AX.X, op=ALU.add), reads=[sq_res], writes=[st_res])
        P.op("dve", lambda e: e.tensor_scalar(out=a, in0=a, scalar1=1.0 / gd, scalar2=EPS, op0=ALU.mult, op1=ALU.add),
             reads=[st_res], writes=[st_res])
        P.op("act", lambda e: e.activation(b, a, AF.Sqrt), reads=[st_res], writes=[st_res])
        P.op("dve", lambda e: e.reciprocal(c, b), reads=[st_res], writes=[st_res])
        return c

    def transposes(dst_res, dst3, src_res, src_fn, n, ident=None, evac="act"):
        ident = ident or ident_b
        for g0 in range(0, n, 8):
            bk = nextbank()
            bv = bk.t[:].bitcast(BF16)
            cnt = min(8, n - g0)
            for i in range(cnt):
                P.op("pe", lambda e, i=i: e.transpose(bv[:, i * 128:(i + 1) * 128], src_fn(g0 + i), ident.t[:]),
                     reads=[src_res, ident], writes=[bk])
            P.op(evac, (lambda e: e.activation(dst3[:, g0:g0 + cnt, :], bv[:, 0:cnt * 128].rearrange("p (a b) -> p a b", b=128), AF.Copy))
                 if evac == "act" else
                 (lambda e: e.tensor_copy(dst3[:, g0:g0 + cnt, :], bv[:, 0:cnt * 128].rearrange("p (a b) -> p a b", b=128))),
                 reads=[bk], writes=[dst_res])

    def load_w_bf16(dst_res, src2d, kc_n):
        v = src2d.rearrange("(kc p) n -> p kc n", p=128)
        for kc in range(kc_n):
            P.dma("pool", dst_res.t[:, kc, :], v[:, kc, :], writes=[dst_res])

    def bcast_load(dst_res, row_ap):
        P.dma("sync", dst_res.t[:], row_ap.partition_broadcast(128), writes=[dst_res])

    def proj(lhsT_res, lhsT3, w_res, w3, kc_n, n_out, evac_fn, bank_lo=0, bank_hi=6):
        for n in range(n_out // 512):
            bk = nextbank(bank_lo, bank_hi)
            for kc in range(kc_n):
                P.op("pe", lambda e, kc=kc: e.matmul(bk.t[:, 0:512], lhsT3[:, kc, :], w3[:, kc, n * 512:(n + 1) * 512],
                                                     start=(kc == 0), stop=(kc == kc_n - 1)),
                     reads=[lhsT_res, w_res], writes=[bk])
            evac_fn(n, bk)

    def x_add(xr, j, n, bk):
        r = rows(j)
        P.op("dve", lambda e: e.tensor_tensor(out=xr.t[0:r, n * 512:(n + 1) * 512], in0=xr.t[0:r, n * 512:(n + 1) * 512],
                                              in1=bk.t[0:r, 0:512], op=ALU.add),
             reads=[xr, bk], writes=[xr])

    def gmlp_layer(l):
        with P.scope():
            w_in = P.sb("w_in", [128, 8, 4096], BF16)
            w_out = P.sb("w_out", [128, 16, 1024], BF16)
            g_v = P.sb("g_v", [128, 2048], F32)
            ga = P.sb("ga", [128, 8], F32)
            wsT = P.sb("wsT", [128, 8, 128], BF16)
            wsS = P.sb("wsS", [128, 8, 128], BF16)
            w00 = P.sb("w00", [128, 8, 1], F32)
            bsP = P.sb("bsP", [128, 8], F32)
            bsS = P.sb("bsS", [128, 8, 1], F32)
            u = P.sb("u", [128, 2048], BF16)
            v = P.sb("v", [128, 2048], F32)
            vg = P.sb("vg", [128, 2048], BF16)
            hnT = P.view("hnT", vg.t[:, 0:1024].rearrange("p (a b) -> p a b", b=128))
            gT = P.sb("gT", [128, 16, 128], BF16)

            load_w_bf16(w_in, a_w_in[l], 8)
            load_w_bf16(w_out, a_w_out[l], 16)
            bcast_load(g_v, a_vnorm_g[l])
            P.dma("sync", ga.t[:], a_norm_g[l].rearrange("(kc p) -> p kc", p=128), writes=[ga], allow_slow_non_contiguous=True)
            for kc in range(8):
                P.op("dve", lambda e, kc=kc: e.tensor_scalar(out=w_in.t[:, kc, :], in0=w_in.t[:, kc, :], scalar1=ga.t[:, kc:kc + 1],
                                                             scalar2=None, op0=ALU.mult), reads=[w_in, ga], writes=[w_in])
            wsf3 = v.t[:, 0:1024].rearrange("p (g s) -> p g s", s=128)
            wsb3 = vg.t[:, 0:1024].rearrange("p (g s) -> p g s", s=128)
            P.dma("sync", wsf3, a_w_s[l].rearrange("g t s -> t g s"), writes=[v])
            P.op("pool", lambda e: e.affine_select(out=wsf3, in_=wsf3, pattern=[[0, 8], [-1, 128]], compare_op=ALU.is_ge,
                                                    fill=0.0, base=0, channel_multiplier=1), reads=[v], writes=[v])
            P.op("dve", lambda e: e.tensor_copy(wsb3, wsf3), reads=[v], writes=[vg])
            transposes(wsT, wsT.t, vg, lambda i: wsb3[:, i, :], 8)
            P.dma("sync", w00.t[:], a_w_s[l, :, 0, 0:1].partition_broadcast(128), writes=[w00])
            for g in range(8):
                P.op("dve", lambda e, g=g: e.tensor_scalar(out=wsS.t[:, g, :], in0=ident_f.t[:], scalar1=w00.t[:, g, 0:1], scalar2=None,
                                                           op0=ALU.mult), reads=[ident_f, w00], writes=[wsS])
            P.dma("sync", bsP.t[:], a_b_s[l].rearrange("g t -> t g"), writes=[bsP], allow_slow_non_contiguous=True)
            P.dma("sync", bsS.t[:], a_b_s[l, :, 0:1].partition_broadcast(128), writes=[bsS])

            for j in range(NALL):
                xt = x_begin(j)
                hn_ap = gT.t[:, 0:8, :].rearrange("p a b -> p (a b)")
                rs = rstd_of(xt, xt.t[:], v, v.t[:, 0:1024], 1024, 0)
                P.op("dve", lambda e: e.tensor_scalar(out=hn_ap, in0=xt.t[:], scalar1=rs, scalar2=None, op0=ALU.mult),
                     reads=[xt, ss], writes=[gT])
                transposes(vg, hnT.t, gT, lambda i: hn_ap[:, i * 128:(i + 1) * 128], 8)

                def evac_z(n, bk):
                    if n < 4:
                        P.op("act", lambda e: e.activation(u.t[:, n * 512:(n + 1) * 512], bk.t[:, 0:512], AF.Gelu), reads=[bk], writes=[u])
                    else:
                        P.op("act", lambda e: e.activation(v.t[:, (n - 4) * 512:(n - 3) * 512], bk.t[:, 0:512], AF.Gelu), reads=[bk], writes=[v])
                proj(vg, hnT.t, w_in, w_in.t, 8, 4096, evac_z)
                rs2 = rstd_of(v, v.t[:], gT, gT.t[:].rearrange("p a b -> p (a b)"), 2048, 4)
                P.op("dve", lambda e: e.scalar_tensor_tensor(out=v.t[:], in0=v.t[:], scalar=rs2, in1=g_v.t[:], op0=ALU.mult, op1=ALU.mult),
                     reads=[v, ss, g_v], writes=[v])
                if j == 16:
                    P.dma("sync", gvo[l], v.t[0:16, :], reads=[v], out=True)
                P.op("act", lambda e: e.activation(vg.t[:], v.t[:], AF.Copy), reads=[v], writes=[vg])
                wmix = wsS if j == 16 else wsT
                bmix = (lambda g: bsS.t[:, g, 0:1]) if j == 16 else (lambda g: bsP.t[:, g:g + 1])
                for g in range(8):
                    bk = nextbank()
                    P.op("pe", lambda e, g=g: e.matmul(bk.t[:, 0:256], wmix.t[:, g, :], vg.t[:, g * 256:(g + 1) * 256], start=True, stop=True),
                         reads=[wmix, vg], writes=[bk])
                    P.op("dve", lambda e, g=g: e.scalar_tensor_tensor(out=vg.t[:, g * 256:(g + 1) * 256], in0=bk.t[:, 0:256], scalar=bmix(g),
                                                                      in1=u.t[:, g * 256:(g + 1) * 256], op0=ALU.add, op1=ALU.mult),
                         reads=[bk, u, bsP, bsS], writes=[vg])
                transposes(gT, gT.t, vg, lambda i: vg.t[:, i * 128:(i + 1) * 128], 16)
                proj(gT, gT.t, w_out, w_out.t, 16, 1024, lambda n, bk: x_add(xt, j, n, bk))
                x_end(j, xt)

    NB = 8
    GRP = 2

    def peer_layer(l):
        with P.scope():
            pwq = P.sb("pwq", [128, 8, 2048], BF16)
            keysT = P.sb("keysT", [128, 16, 128], BF16)
            g_f = P.sb("g_f", [128, 1024], F32)
            HN = [P.sb(f"hn{i}", [128, 1024], F32) for i in range(2)]
            EIDX = [P.sb(f"eidx{i}", [128, 128], I32) for i in range(2)]
            GW = [P.sb(f"gw{i}", [128, 128], F32) for i in range(2)]
            hnb = P.sb("hnb", [128, 1024], BF16)
            hnT = P.sb("hnT", [128, 8, 128], BF16)
            qb = P.sb("qb", [128, 2048], BF16)
            qT = P.sb("qT", [128, 16, 128], BF16)
            sc = P.sb("sc", [128, 16, 128], F32)
            wk = P.sb("wk", [128, 256], F32)
            sv = P.sb("sv", [128, 16, 16], F32)
            si = P.sb("si", [128, 16, 16], U32)
            sif = P.sb("sif", [128, 16, 16], F32)
            cand = P.sb("cand", [128, 8, 256], F32)
            tops = P.sb("tops", [128, 8, 16], F32)
            pos = P.sb("pos", [128, 8, 16], U32)
            posf = P.sb("posf", [128, 8, 16], F32)
            af = P.sb("af", [128, 8, 16], F32)
            bf = P.sb("bf", [128, 8, 16], F32)
            i0 = P.sb("i0", [128, 8, 16], F32)
            i1 = P.sb("i1", [128, 8, 16], F32)
            ex = P.sb("ex", [128, 8, 16], F32)
            zz = P.sb("zz", [128, 16], F32)
            actp = P.sb("actp", [128, 128], F32)
            wgt = P.sb("wgt", [128, 128], F32)
            junk = P.sb("junk", [128, 1024], F32)
            gel = P.sb("gel", [128, 128], F32)
            GB = [P.sb(f"gb{i}", [128, 2048], BF16) for i in range(NB)]
            DG = [P.sb(f"dg{i}", [128, 128], BF16) for i in range(3)]

            load_w_bf16(pwq, peer_w_q[l], 8)
            bcast_load(g_f, f_norm_g[l])
            P.dma("sync", sc.t[:], peer_keys[l].rearrange("c n d -> n c d"), writes=[sc])
            P.op("dve", lambda e: e.tensor_copy(qT.t[:], sc.t[:]), reads=[sc], writes=[qT])
            transposes(keysT, keysT.t, qT, lambda i: qT.t[:, i, :], 16)

            if l < 2:
                XSB[0] = [XS[0], P.sb("xs1", [128, 1024], F32)]
            XT = {}

            def routing(j):
                xt = XT[j] = x_begin(j, j % 2) if j >= NTILE else X[j]
                hn, eidx, gw = HN[j % 2], EIDX[j % 2], GW[j % 2]
                rs = rstd_of(xt, xt.t[:], junk, junk.t[:], 1024, 8)
                P.op("dve", lambda e: e.scalar_tensor_tensor(out=hn.t[:], in0=xt.t[:], scalar=rs, in1=g_f.t[:], op0=ALU.mult, op1=ALU.mult),
                     reads=[xt, ss, g_f], writes=[hn])
                P.op("act", lambda e: e.activation(hnb.t[:], hn.t[:], AF.Copy), reads=[hn], writes=[hnb])
                transposes(hnT, hnT.t, hnb, lambda i: hnb.t[:, i * 128:(i + 1) * 128], 8)
                yield
                proj(hnT, hnT.t, pwq, pwq.t, 8, 2048,
                     lambda n, bk: P.op("act", lambda e: e.activation(qb.t[:, n * 512:(n + 1) * 512], bk.t[:, 0:512], AF.Copy),
                                        reads=[bk], writes=[qb]))
                yield
                transposes(qT, qT.t, qb, lambda i: qb.t[:, i * 128:(i + 1) * 128], 16)
                yield
                for g4 in range(4):
                    bk = nextbank()
                    for i in range(4):
                        hc = g4 * 4 + i
                        P.op("pe", lambda e, hc=hc, i=i: e.matmul(bk.t[:, i * 128:(i + 1) * 128], qT.t[:, hc, :], keysT.t[:, hc, :], start=True, stop=True),
                             reads=[qT, keysT], writes=[bk])
                    P.op("act", lambda e, g4=g4: e.activation(sc.t[:, g4 * 4:(g4 + 1) * 4, :], bk.t[:, 0:512].rearrange("p (a b) -> p a b", b=128), AF.Copy),
                         reads=[bk], writes=[sc])
                    yield
                for hc in range(16):
                    P.op("dve", lambda e, hc=hc: e.max(sv.t[:, hc, 0:8], sc.t[:, hc, :]), reads=[sc], writes=[sv])
                    P.op("dve", lambda e, hc=hc: e.max_index(si.t[:, hc, 0:8], sv.t[:, hc, 0:8], sc.t[:, hc, :]), reads=[sc, sv], writes=[si])
                    P.op("dve", lambda e, hc=hc: e.match_replace(wk.t[:, 0:128], sv.t[:, hc, 0:8], sc.t[:, hc, :], -1e30), reads=[sc, sv], writes=[wk])
                    P.op("dve", lambda e, hc=hc: e.max(sv.t[:, hc, 8:16], wk.t[:, 0:128]), reads=[wk], writes=[sv])
                    P.op("dve", lambda e, hc=hc: e.max_index(si.t[:, hc, 8:16], sv.t[:, hc, 8:16], wk.t[:, 0:128]), reads=[wk, sv], writes=[si])
                    yield
                P.op("dve", lambda e: e.tensor_copy(sif.t[:], si.t[:]), reads=[si], writes=[sif])
                sv4 = sv.t[:].rearrange("p (h c) k -> p h c k", c=2)
                sif4 = sif.t[:].rearrange("p (h c) k -> p h c k", c=2)
                cand4 = cand.t[:].rearrange("p h (a b) -> p h a b", b=16)
                P.op("dve", lambda e: e.tensor_tensor(out=cand4, in0=sv4[:, :, 0, :].unsqueeze(3).to_broadcast([128, 8, 16, 16]),
                                                      in1=sv4[:, :, 1, :].unsqueeze(2).to_broadcast([128, 8, 16, 16]), op=ALU.add),
                     reads=[sv], writes=[cand])
                for h in range(8):
                    P.op("dve", lambda e, h=h: e.max(tops.t[:, h, 0:8], cand.t[:, h, :]), reads=[cand], writes=[tops])
                    P.op("dve", lambda e, h=h: e.max_index(pos.t[:, h, 0:8], tops.t[:, h, 0:8], cand.t[:, h, :]), reads=[cand, tops], writes=[pos])
                    P.op("dve", lambda e, h=h: e.match_replace(wk.t[:], tops.t[:, h, 0:8], cand.t[:, h, :], -1e30), reads=[cand, tops], writes=[wk])
                    P.op("dve", lambda e, h=h: e.max(tops.t[:, h, 8:16], wk.t[:]), reads=[wk], writes=[tops])
                    P.op("dve", lambda e, h=h: e.max_index(pos.t[:, h, 8:16], tops.t[:, h, 8:16], wk.t[:]), reads=[wk, tops], writes=[pos])
                    yield
                P.op("dve", lambda e: e.tensor_tensor(out=ex.t[:], in0=tops.t[:], in1=tops.t[:, :, 0:1].to_broadcast([128, 8, 16]), op=ALU.subtract),
                     reads=[tops], writes=[ex])
                P.op("act", lambda e: e.activation(ex.t[:], ex.t[:], AF.Exp), reads=[ex], writes=[ex])
                P.op("dve", lambda e: e.tensor_reduce(out=zz.t[:, 0:8], in_=ex.t[:], axis=AX.X, op=ALU.add), reads=[ex], writes=[zz])
                P.op("dve", lambda e: e.reciprocal(zz.t[:, 8:16], zz.t[:, 0:8]), reads=[zz], writes=[zz])
                P.op("dve", lambda e: e.tensor_tensor(out=gw.t[:].rearrange("p (h k) -> p h k", k=16), in0=ex.t[:],
                                                      in1=zz.t[:, 8:16].unsqueeze(2).to_broadcast([128, 8, 16]), op=ALU.mult),
                     reads=[ex, zz], writes=[gw])
                yield
                P.op("dve", lambda e: e.tensor_copy(posf.t[:], pos.t[:]), reads=[pos], writes=[posf])
                P.op("dve", lambda e: e.tensor_scalar(out=af.t[:], in0=posf.t[:], scalar1=16.0, scalar2=None, op0=ALU.is_ge), reads=[posf], writes=[af])
                for k in range(2, 16):
                    P.op("dve", lambda e, k=k: e.scalar_tensor_tensor(out=af.t[:], in0=posf.t[:], scalar=16.0 * k, in1=af.t[:], op0=ALU.is_ge, op1=ALU.add),
                         reads=[posf, af], writes=[af])
                P.op("dve", lambda e: e.scalar_tensor_tensor(out=bf.t[:], in0=af.t[:], scalar=-16.0, in1=posf.t[:], op0=ALU.mult, op1=ALU.add),
                     reads=[posf, af], writes=[bf])
                yield
                oh = P.view("oh", cand.t[:].rearrange("p h (a b) -> p h a b", b=16))
                io4 = iota16.t[:].unsqueeze(1).unsqueeze(1).to_broadcast([128, 8, 16, 16])
                for (sel, c, dst) in ((af, 0, i0), (bf, 1, i1)):
                    P.op("dve", lambda e, sel=sel: e.tensor_tensor(out=oh.t[:], in0=sel.t[:].unsqueeze(3).to_broadcast([128, 8, 16, 16]), in1=io4, op=ALU.is_equal),
                         reads=[sel, iota16], writes=[cand])
                    P.op("dve", lambda e, c=c: e.tensor_tensor(out=oh.t[:], in0=oh.t[:], in1=sif4[:, :, c, :].unsqueeze(2).to_broadcast([128, 8, 16, 16]), op=ALU.mult),
                         reads=[cand, sif], writes=[cand])
                    P.op("dve", lambda e, dst=dst: e.tensor_reduce(out=dst.t[:], in_=oh.t[:], axis=AX.X, op=ALU.add), reads=[cand], writes=[dst])
                    yield
                P.op("dve", lambda e: e.tensor_scalar(out=i0.t[:], in0=i0.t[:], scalar1=128.0, scalar2=None, op0=ALU.mult),
                     reads=[i0], writes=[i0])
                P.op("dve", lambda e: e.tensor_tensor(out=eidx.t[:].rearrange("p (h k) -> p h k", k=16), in0=i0.t[:], in1=i1.t[:], op=ALU.add),
                     reads=[i0, i1], writes=[eidx])

            gb_rr = [0]

            def experts(j, gen=None):
                hn, eidx, gw = HN[j % 2], EIDX[j % 2], GW[j % 2]
                for g0 in range(0, 128, GRP):
                    gbs = []
                    for s in range(g0, g0 + GRP):
                        gb = GB[gb_rr[0] % NB]
                        gb_rr[0] += 1
                        gbs.append(gb)
                        P.gather(gb.t[:], PUV[l].t, eidx.t[:, s:s + 1], reads=[eidx, PUV[l]], writes=[gb])
                        P.op("dve", lambda e, s=s, gb=gb: e.scalar_tensor_tensor(out=junk.t[:], in0=gb.t[:, 0:1024], scalar=1.0, in1=hn.t[:], op0=ALU.mult, op1=ALU.mult,
                                                                                 accum_out=actp.t[:, s:s + 1]),
                             reads=[gb, hn], writes=[junk, actp])
                    P.op("act", lambda e, g0=g0: e.activation(gel.t[:, g0:g0 + GRP], actp.t[:, g0:g0 + GRP], AF.Gelu), reads=[actp], writes=[gel])
                    for k, s in enumerate(range(g0, g0 + GRP)):
                        gb = gbs[k]
                        dg = DG[s % 3]
                        P.op("dve", lambda e, s=s, dg=dg: e.tensor_scalar(out=dg.t[:], in0=ident_f.t[:], scalar1=gel.t[:, s:s + 1], scalar2=gw.t[:, s:s + 1],
                                                                          op0=ALU.mult, op1=ALU.mult),
                             reads=[ident_f, gel, gw], writes=[dg])
                        for n in range(2):
                            P.op("pe", lambda e, n=n, s=s, dg=dg, gb=gb: e.matmul(BK[6 + n].t[:, 0:512], dg.t[:], gb.t[:, 1024 + n * 512:1024 + (n + 1) * 512],
                                                                                   start=(s == 0), stop=(s == 127)),
                                 reads=[dg, gb], writes=[BK[6 + n]])
                    if gen is not None and g0 >= 8:
                        for _ in range(2):
                            next(gen, None)
                for n in range(2):
                    x_add(XT[j], j, n, BK[6 + n])
                x_end(j, XT[j])

            nt = min(n_tiles_peer, NALL if l < 2 else NTILE)
            for _ in routing(0):
                pass
            for j in range(nt):
                gen = routing(j + 1) if j + 1 < nt else None
                experts(j, gen)
                if gen is not None:
                    for _ in gen:
                        pass
        XSB[0] = XS

    def kv_phase():
        with P.scope():
            wk_ = P.sb("w_k", [128, 8, 1024], BF16)
            wv_ = P.sb("w_v", [128, 8, 1024], BF16)
            g_kv = P.sb("g_kv", [128, 1024], F32)
            g_kn = P.sb("g_kn", [128, 64], F32)
            hnb = P.sb("hnb", [128, 1024], BF16)
            hnT = P.sb("hnT", [128, 8, 128], BF16)
            KF = [P.sb(f"kf{i}", [128, 1024], F32) for i in range(2)]
            VF = [P.sb(f"vf{i}", [128, 1024], F32) for i in range(2)]
            sq = P.sb("sq", [128, 1024], F32)
            st = P.sb("st", [128, 48], F32)
            knb = P.sb("knb", [128, 8, 2, 64], BF16)
            KT = [P.sb(f"kt{i}", [128, 8, 128], BF16) for i in range(2)]
            VE = [P.sb(f"ve{i}", [128, 8, 129], BF16) for i in range(2)]
            load_w_bf16(wk_, w_k, 8)
            load_w_bf16(wv_, w_v, 8)
            bcast_load(g_kv, kv_norm_g[0])
            bcast_load(g_kn, k_norm_g[0])
            for i in range(2):
                P.op("pool", lambda e, i=i: e.memset(VE[i].t[:], 1.0), writes=[VE[i]])
            for j in range(NALL):
                xt = x_begin(j)
                own = j < NPT
                slot = 2 * j if own else 2 * (j - NTILE) + 1
                kf = kn_s if j == 16 else KF[j % 2]
                vf = v_s if j == 16 else VF[j % 2]
                rs = rstd_of(xt, xt.t[:], sq, sq.t[:], 1024, 12)
                P.op("dve", lambda e: e.scalar_tensor_tensor(out=hnb.t[:], in0=xt.t[:], scalar=rs, in1=g_kv.t[:], op0=ALU.mult, op1=ALU.mult),
                     reads=[xt, ss, g_kv], writes=[hnb])
                transposes(hnT, hnT.t, hnb, lambda i: hnb.t[:, i * 128:(i + 1) * 128], 8)
                proj(hnT, hnT.t, wk_, wk_.t, 8, 1024,
                     lambda n, bk: P.op("act", lambda e: e.activation(kf.t[:, n * 512:(n + 1) * 512], bk.t[:, 0:512], AF.Copy), reads=[bk], writes=[kf]))
                proj(hnT, hnT.t, wv_, wv_.t, 8, 1024,
                     lambda n, bk: P.op("act", lambda e: e.activation(vf.t[:, n * 512:(n + 1) * 512], bk.t[:, 0:512], AF.Copy), reads=[bk], writes=[vf]))
                k3 = kf.t[:].rearrange("p (a b) -> p a b", b=64)
                rg = group_rstd(kf, k3, sq, sq.t[:].rearrange("p (a b) -> p a b", b=64), 16, 64, st, 0)
                P.op("dve", lambda e: e.tensor_tensor(out=k3, in0=k3, in1=rg.unsqueeze(2).to_broadcast([128, 16, 64]), op=ALU.mult),
                     reads=[kf, st], writes=[kf])
                P.op("dve", lambda e: e.tensor_tensor(out=k3, in0=k3, in1=g_kn.t[:].unsqueeze(1).to_broadcast([128, 16, 64]), op=ALU.mult),
                     reads=[kf, g_kn], writes=[kf])
                if j == 16:
                    P.dma("sync", nks, kf.t[0:16, :], reads=[kf], out=True)
                    P.dma("sync", nvs, vf.t[0:16, :], reads=[vf], out=True)
                    continue
                if own:
                    P.dma("sync", nkp[j * 128:(j + 1) * 128, :], kf.t[:], reads=[kf], out=True)
                    P.dma("sync", nvp[j * 128:(j + 1) * 128, :], vf.t[:], reads=[vf], out=True)
                kt, ve = KT[j % 2], VE[j % 2]
                P.op("act", lambda e: e.activation(knb.t[:], kf.t[:].rearrange("p (m h d) -> p h m d", m=2, d=64), AF.Copy), reads=[kf], writes=[knb])
                transposes(kt, kt.t, knb, lambda h: knb.t[:, h, :, :].rearrange("p m d -> p (m d)"), 8)
                P.op("act", lambda e: e.activation(ve.t[:, :, 0:128], vf.t[:].rearrange("p (h d) -> p h d", d=128), AF.Copy), reads=[vf], writes=[ve])
                P.dma("sync", kvx_all.t[slot, :, 0:1024], kt.t[:].rearrange("p a b -> p (a b)"), reads=[kt], writes=[kvx_all], prim=kvx_all)
                P.dma("sync", kvx_all.t[slot, :, 1024:2056], ve.t[:].rearrange("p a b -> p (a b)"), reads=[ve], writes=[kvx_all], prim=kvx_all)

    def attn_layer(li):
        l = 2 + li
        lam0 = _lam_init(l)
        slopes = [2.0 ** (-8.0 * (h + 1) / 8) for h in range(8)]
        with P.scope():
            wq_ = P.sb("w_q", [128, 8, 1024], BF16)
            wo_ = P.sb("w_o", [128, 8, 1024], BF16)
            g_b = P.sb("g_b", [128, 1024], F32)
            g_qn = P.sb("g_qn", [128, 64], F32)
            g_sub = P.sb("g_sub", [128, 128], F32)
            lamv = P.sb("lamv", [128, 256], F32)
            lamc = P.sb("lamc", [128, 8], F32)
            hfc = P.sb("hfc", [128, 1], F32)
            biasT = P.sb("biasT", [128, 16, 2, 8], F32)
            bbase = P.sb("bbase", [128, 16, 2], F32)
            sbias = P.sb("sbias", [128, 16, 16], F32)
            sbase = P.sb("sbase", [128, 16], F32)
            mskf = P.sb("mskf", [128, 2, 128], F32)
            mskb = P.sb("mskb", [128, 2, 128], BF16)
            ptb = P.sb("ptb", [128, 256], I32)
            rowidx = P.sb("rowidx", [128, 256], I32)
            ones1 = P.sb("ones1", [128, 1], F32)
            bmask = P.sb("bmask", [128, 8], F32)
            hnb = P.sb("hnb", [128, 1024], BF16)
            hnT = P.sb("hnT", [128, 8, 128], BF16)
            qf = P.sb("qf", [128, 1024], F32)
            sq = P.sb("sq", [128, 1024], F32)
            st = P.sb("st", [128, 64], F32)
            qnb = P.sb("qnb", [128, 8, 2, 64], BF16)
            QT = P.sb("QT", [128, 8, 128], BF16)
            QTm = [P.sb(f"QTm{m}", [128, 8, 128], BF16) for m in range(2)]
            KV = [P.sb(f"kv{i}", [128, 2056], BF16) for i in range(3)]
            PT = [P.sb(f"pt{i}", [128, 256], BF16) for i in range(4)]
            oev = P.sb("oev", [128, 16, 129], F32)
            rz = P.sb("rz", [128, 16], F32)
            o0 = P.sb("o0", [128, 8, 128], F32)
            o1 = P.sb("o1", [128, 8, 128], F32)
            onb = P.sb("onb", [128, 1024], BF16)
            oT = P.sb("oT", [128, 8, 128], BF16)
            qbc = P.sb("qbc", [128, 1024], F32)
            KP = [P.sb(f"kp{i}", [128, 1024], F32) for i in range(2)]
            VP = [P.sb(f"vp{i}", [128, 1024], F32) for i in range(2)]
            sc16 = P.sb("sc16", [128, 32], F32)
            E16 = [P.sb(f"e16_{i}", [128, 16], F32) for i in range(2)]
            dall = P.sb("dall", [16, 16, 129], F32)
            enew = P.sb("enew", [128, 16], F32)
            ST = [P.view(f"st{i}", BK[6 + i].t[:, 0:256]) for i in range(2)]

            load_w_bf16(wq_, w_q[li], 8)
            load_w_bf16(wo_, w_o[li], 8)
            bcast_load(g_b, b_norm_g[li])
            bcast_load(g_qn, q_norm_g[li])
            bcast_load(g_sub, subln_g[li])
            bcast_load(lamv, lam_in[li])
            P.dma("sync", hfc.t[:], hfcol_in, writes=[hfc])
            P.dma("sync", mskf.t[:], maskadd, writes=[mskf])
            P.op("dve", lambda e: e.tensor_copy(mskb.t[:], mskf.t[:]), reads=[mskf], writes=[mskb])
            P.dma("sync", ptb.t[:], pt[0].partition_broadcast(128), writes=[ptb])
            P.op("dve", lambda e: e.scalar_tensor_tensor(out=rowidx.t[:], in0=ptb.t[:], scalar=128.0, in1=pcol.t[:, 0:1].to_broadcast([128, 256]),
                                                         op0=ALU.mult, op1=ALU.add), reads=[ptb, pcol], writes=[rowidx])
            P.op("pool", lambda e: e.memset(ones1.t[:], 1.0), writes=[ones1])
            P.op("dve", lambda e: e.scalar_tensor_tensor(out=sq.t[:, 0:64], in0=lamv.t[:, 0:64], scalar=1.0, in1=lamv.t[:, 64:128], op0=ALU.mult, op1=ALU.mult,
                                                         accum_out=lamc.t[:, 0:1]), reads=[lamv], writes=[sq, lamc])
            P.op("dve", lambda e: e.scalar_tensor_tensor(out=sq.t[:, 0:64], in0=lamv.t[:, 128:192], scalar=1.0, in1=lamv.t[:, 192:256], op0=ALU.mult, op1=ALU.mult,
                                                         accum_out=lamc.t[:, 1:2]), reads=[lamv], writes=[sq, lamc])
            P.op("act", lambda e: e.activation(lamc.t[:, 2:4], lamc.t[:, 0:2], AF.Exp), reads=[lamc], writes=[lamc])
            P.op("dve", lambda e: e.tensor_tensor(out=lamc.t[:, 4:5], in0=lamc.t[:, 3:4], in1=lamc.t[:, 2:3], op=ALU.subtract), reads=[lamc], writes=[lamc])
            P.op("dve", lambda e: e.tensor_scalar(out=lamc.t[:, 4:5], in0=lamc.t[:, 4:5], scalar1=-lam0, scalar2=None, op0=ALU.add), reads=[lamc], writes=[lamc])
            neg_lam = lamc.t[:, 4:5]
            P.op("pool", lambda e: e.iota(bbase.t[:], [[-256, 16], [0, 2]], base=-64, channel_multiplier=1, allow_small_or_imprecise_dtypes=True), writes=[bbase])
            P.op("dve", lambda e: e.tensor_scalar(out=bbase.t[:, :, 1], in0=bbase.t[:, :, 1], scalar1=hfc.t[:, 0:1], scalar2=None, op0=ALU.add), reads=[bbase, hfc], writes=[bbase])
            for h in range(8):
                P.op("dve", lambda e, h=h: e.tensor_scalar(out=biasT.t[:, :, :, h], in0=bbase.t[:], scalar1=slopes[h], scalar2=None, op0=ALU.mult),
                     reads=[bbase], writes=[biasT])
            P.op("pool", lambda e: e.iota(sbase.t[:], [[128, 16]], base=-2048, channel_multiplier=1, allow_small_or_imprecise_dtypes=True), writes=[sbase])
            for mh in range(16):
                P.op("dve", lambda e, mh=mh: e.tensor_scalar(out=sbias.t[:, :, mh], in0=sbase.t[:], scalar1=slopes[mh % 8], scalar2=None, op0=ALU.mult),
                     reads=[sbase], writes=[sbias])
            P.op("dve", lambda e: e.tensor_tensor(out=bmask.t[0:16, :], in0=ident_f.t[0:16, 0:8], in1=ident_f.t[0:16, 8:16], op=ALU.add),
                 reads=[ident_f], writes=[bmask])

            for m in range(2):
                P.op("pool", lambda e, m=m: e.memset(QTm[m].t[:], 0.0), writes=[QTm[m]])

            def q_side(j):
                xt = X[j]
                rs = rstd_of(xt, xt.t[:], sq, sq.t[:], 1024, 16)
                P.op("dve", lambda e: e.scalar_tensor_tensor(out=hnb.t[:], in0=xt.t[:], scalar=rs, in1=g_b.t[:], op0=ALU.mult, op1=ALU.mult),
                     reads=[xt, ss, g_b], writes=[hnb])
                transposes(hnT, hnT.t, hnb, lambda i: hnb.t[:, i * 128:(i + 1) * 128], 8)
                proj(hnT, hnT.t, wq_, wq_.t, 8, 1024,
                     lambda n, bk: P.op("act", lambda e: e.activation(qf.t[:, n * 512:(n + 1) * 512], bk.t[:, 0:512], AF.Copy), reads=[bk], writes=[qf]))
                q3 = qf.t[:].rearrange("p (a b) -> p a b", b=64)
                rg = group_rstd(qf, q3, sq, sq.t[:].rearrange("p (a b) -> p a b", b=64), 16, 64, st, 0)
                P.op("dve", lambda e: e.tensor_tensor(out=q3, in0=q3, in1=rg.unsqueeze(2).to_broadcast([128, 16, 64]), op=ALU.mult),
                     reads=[qf, st], writes=[qf])
                P.op("dve", lambda e: e.tensor_tensor(out=q3, in0=q3, in1=g_qn.t[:].unsqueeze(1).to_broadcast([128, 16, 64]), op=ALU.mult),
                     reads=[qf, g_qn], writes=[qf])

            def epilogue(j):
                P.op("dve", lambda e: e.reciprocal(rz.t[:], oev.t[:, :, 128]), reads=[oev], writes=[rz])
                P.op("dve", lambda e: e.tensor_tensor(out=o0.t[:], in0=oev.t[:, 0:8, 0:128], in1=rz.t[:, 0:8].unsqueeze(2).to_broadcast([128, 8, 128]), op=ALU.mult),
                     reads=[oev, rz], writes=[o0])
                P.op("dve", lambda e: e.tensor_tensor(out=o1.t[:], in0=oev.t[:, 8:16, 0:128], in1=rz.t[:, 8:16].unsqueeze(2).to_broadcast([128, 8, 128]), op=ALU.mult),
                     reads=[oev, rz], writes=[o1])
                of = o0.t[:].rearrange("p a b -> p (a b)")
                P.op("dve", lambda e: e.scalar_tensor_tensor(out=of, in0=o1.t[:].rearrange("p a b -> p (a b)"), scalar=neg_lam, in1=of, op0=ALU.mult, op1=ALU.add),
                     reads=[o0, o1, lamc], writes=[o0])
                rg = group_rstd(o0, o0.t[:], o1, o1.t[:], 8, 128, st, 48 - 24)
                P.op("dve", lambda e: e.tensor_tensor(out=o0.t[:], in0=o0.t[:], in1=rg.unsqueeze(2).to_broadcast([128, 8, 128]), op=ALU.mult),
                     reads=[o0, st], writes=[o0])
                P.op("dve", lambda e: e.scalar_tensor_tensor(out=onb.t[:].rearrange("p (a b) -> p a b", b=128), in0=o0.t[:], scalar=1.0 - lam0,
                                                             in1=g_sub.t[:].unsqueeze(1).to_broadcast([128, 8, 128]), op0=ALU.mult, op1=ALU.mult),
                     reads=[o0, g_sub], writes=[onb])
                transposes(oT, oT.t, onb, lambda i: onb.t[:, i * 128:(i + 1) * 128], 8)
                proj(oT, oT.t, wo_, wo_.t, 8, 1024, lambda n, bk: x_add(X[j], j, n, bk))

            kv_rr = [0]
            pt_rr = [0]
            for i in range(dbg_qtiles):
                q_side(i)
                P.op("act", lambda e: e.activation(qnb.t[:], qf.t[:].rearrange("p (m h d) -> p h m d", m=2, d=64), AF.Copy), reads=[qf], writes=[qnb])
                transposes(QT, QT.t, qnb, lambda h: qnb.t[:, h, :, :].rearrange("p m d -> p (m d)"), 8)
                for m in range(2):
                    P.op("dve", lambda e, m=m: e.tensor_copy(QTm[m].t[m * 64:(m + 1) * 64], QT.t[m * 64:(m + 1) * 64]), reads=[QT], writes=[QTm[m]])
                nkb = 2 * i + 2
                if dbg_att < 2:
                    continue
                bank_started = set()
                for kb in range(nkb):
                    kvb = KV[kv_rr[0] % 3]
                    kv_rr[0] += 1
                    P.dma("sync", kvb.t[:], kvx_all.t[kb], reads=[kvx_all], writes=[kvb])
                    ip, w = kb // 2, kb % 2
                    di = i - ip
                    special = (ip == i) or FORCE_SPECIAL
                    for h in range(8):
                        stv = ST[pt_rr[0] % 2]
                        ptb_ = PT[pt_rr[0] % 4]
                        pt_rr[0] += 1
                        for m in range(2):
                            P.op("pe", lambda e, m=m, h=h: e.matmul(stv.t[:, m * 128:(m + 1) * 128], kvb.t[:, h * 128:(h + 1) * 128],
                                                                    QTm[m].t[:, h, :], start=True, stop=not special),
                                 reads=[kvb, QTm[m]], writes=[stv])
                            if special:
                                P.op("pe", lambda e, m=m: e.matmul(stv.t[:, m * 128:(m + 1) * 128], ident_b.t[:], mskb.t[:, w, :], start=False, stop=True),
                                     reads=[ident_b, mskb], writes=[stv])
                        P.op("act", lambda e, h=h: e.activation(ptb_.t[:], stv.t[:], AF.Exp, bias=biasT.t[:, di, w, h:h + 1], scale=0.125),
                             reads=[stv, biasT], writes=[ptb_])
                        for m in range(2):
                            if dbg_att < 3:
                                continue
                            gi = m * 8 + h
                            ob = BK[gi // 3]
                            first = (gi // 3) not in bank_started
                            bank_started.add(gi // 3)
                            P.op("pe", lambda e, m=m, h=h, gi=gi, ob=ob, first=first: e.matmul(ob.t[:, (gi % 3) * 129:(gi % 3) * 129 + 129], ptb_.t[:, m * 128:(m + 1) * 128],
                                                                                kvb.t[:, 1024 + h * 129:1024 + (h + 1) * 129], start=first, stop=(kb == nkb - 1),
                                                                                skip_group_check=True),
                                 reads=[ptb_, kvb], writes=[ob])
                if dbg_att < 4:
                    continue
                for b in range(6):
                    n3 = 3 if b < 5 else 1
                    P.op("act" if b % 2 else "dve",
                         (lambda e, b=b, n3=n3: e.activation(oev.t[:, b * 3:b * 3 + n3, :], BK[b].t[:, 0:n3 * 129].rearrange("p (a c) -> p a c", c=129), AF.Copy)) if b % 2 else
                         (lambda e, b=b, n3=n3: e.tensor_copy(oev.t[:, b * 3:b * 3 + n3, :], BK[b].t[:, 0:n3 * 129].rearrange("p (a c) -> p a c", c=129))),
                         reads=[BK[b]], writes=[oev])
                epilogue(i)

            if dbg_skip_sample:
                return
            q_side(16)
            q3 = qf.t[:].rearrange("p (a b) -> p a b", b=64)
            P.dma("sync", dq.t, qf.t[0:16, :], reads=[qf], writes=[dq])
            P.op("dve", lambda e: e.tensor_tensor(out=sq.t[:], in0=qf.t[:], in1=kn_s.t[:], op=ALU.mult), reads=[qf, kn_s], writes=[sq])
            P.op("dve", lambda e: e.tensor_reduce(out=enew.t[:], in_=sq.t[:].rearrange("p (a b) -> p a b", b=64), axis=AX.X, op=ALU.add), reads=[sq], writes=[enew])
            P.op("act", lambda e: e.activation(enew.t[:], enew.t[:], AF.Exp, scale=0.125), reads=[enew], writes=[enew])
            pg_rr = [0]
            rtmp_ap = o1.t[0:16, :, :]
            for s in range(16):
                P.dma("sync", qbc.t[:], dq.t[s].partition_broadcast(128), reads=[dq], writes=[qbc])
                for pg in range(16):
                    kp, vp, e16 = KP[pg_rr[0] % 2], VP[pg_rr[0] % 2], E16[pg_rr[0] % 2]
                    pg_rr[0] += 1
                    P.gather(kp.t[:], cache_k, rowidx.t[:, s * 16 + pg:s * 16 + pg + 1], reads=[rowidx], writes=[kp])
                    P.gather(vp.t[:], cache_v, rowidx.t[:, s * 16 + pg:s * 16 + pg + 1], reads=[rowidx], writes=[vp])
                    P.op("dve", lambda e, kp=kp: e.tensor_tensor(out=sq.t[:], in0=kp.t[:], in1=qbc.t[:], op=ALU.mult), reads=[kp, qbc], writes=[sq])
                    P.op("dve", lambda e: e.tensor_reduce(out=sc16.t[:, 0:16], in_=sq.t[:].rearrange("p (a b) -> p a b", b=64), axis=AX.X, op=ALU.add),
                         reads=[sq], writes=[sc16])
                    P.op("dve", lambda e, pg=pg: e.scalar_tensor_tensor(out=sc16.t[:, 16:32], in0=sc16.t[:, 0:16], scalar=0.125, in1=sbias.t[:, pg, :], op0=ALU.mult, op1=ALU.add),
                         reads=[sc16, sbias], writes=[sc16])
                    P.op("act", lambda e, e16=e16: e.activation(e16.t[:], sc16.t[:, 16:32], AF.Exp), reads=[sc16], writes=[e16])
                    for n in range(2):
                        P.op("pe", lambda e, n=n, e16=e16, vp=vp, pg=pg: e.matmul(BK[n].t[0:16, 0:512], e16.t[:], vp.t[:, n * 512:(n + 1) * 512], start=(pg == 0), stop=(pg == 15)),
                             reads=[e16, vp], writes=[BK[n]])
                    P.op("pe", lambda e, e16=e16, pg=pg: e.matmul(BK[2].t[0:16, 0:1], e16.t[:], ones1.t[:], start=(pg == 0), stop=(pg == 15)),
                         reads=[e16, ones1], writes=[BK[2]])
                for n in range(2):
                    P.op("dve", lambda e, n=n: e.tensor_tensor(out=rtmp_ap[:, n * 4:(n + 1) * 4, :], in0=BK[n].t[0:16, 0:512].rearrange("p (a b) -> p a b", b=128),
                                                               in1=bmask.t[0:16, n * 4:(n + 1) * 4].unsqueeze(2).to_broadcast([16, 4, 128]), op=ALU.mult),
                         reads=[BK[n], bmask], writes=[o1])
                P.op("dve", lambda e, s=s: e.tensor_reduce(out=dall.t[:, s, 0:128], in_=rtmp_ap.rearrange("p h d -> p d h"), axis=AX.X, op=ALU.add),
                     reads=[o1], writes=[dall])
                P.op("dve", lambda e, s=s: e.tensor_copy(dall.t[:, s, 128:129], BK[2].t[0:16, 0:1]), reads=[BK[2]], writes=[dall])
            P.op("pool", lambda e: e.memset(oev.t[:], 1.0), writes=[oev])
            P.dma("sync", dD.t, dall.t[:], reads=[dall], writes=[dD])
            P.dma("sync", oev.t[0:16, :, :], dD.t.rearrange("m s d -> s m d"), reads=[dD], writes=[oev])
            v3 = v_s.t[:].rearrange("p (h d) -> p h d", d=128)
            for m in range(2):
                P.op("dve", lambda e, m=m: e.tensor_tensor(out=o1.t[:], in0=v3, in1=enew.t[:, m * 8:(m + 1) * 8].unsqueeze(2).to_broadcast([128, 8, 128]), op=ALU.mult),
                     reads=[v_s, enew], writes=[o1])
                P.op("dve", lambda e, m=m: e.tensor_tensor(out=oev.t[:, m * 8:(m + 1) * 8, 0:128], in0=oev.t[:, m * 8:(m + 1) * 8, 0:128], in1=o1.t[:], op=ALU.add),
                     reads=[oev, o1], writes=[oev])
            P.op("dve", lambda e: e.tensor_tensor(out=oev.t[:, :, 128], in0=oev.t[:, :, 128], in1=enew.t[:], op=ALU.add), reads=[oev, enew], writes=[oev])
            epilogue(16)

    steps = {"g0": lambda: gmlp_layer(0), "p0": lambda: peer_layer(0), "g1": lambda: gmlp_layer(1), "p1": lambda: peer_layer(1),
             "kv": kv_phase, "a0": lambda: attn_layer(0), "p2": lambda: peer_layer(2), "a1": lambda: attn_layer(1), "p3": lambda: peer_layer(3)}
    if "A" in stages:
        stages = ["g0", "p0", "g1", "p1"] + [x for x in stages if x != "A"]
    if "KV" in stages:
        stages = [("kv" if x == "KV" else x) for x in stages]
    if "B" in stages:
        stages = [x for x in stages if x != "B"] + ["a0", "p2", "a1", "p3"]
    kn_alloc = [False]
    for st_ in stages:
        if st_ in ("kv", "a0", "a1") and not kn_alloc[0]:
            kn_alloc[0] = True
            kn_s = P.sb("kn_s", [128, 1024], F32)
            v_s = P.sb("v_s", [128, 1024], F32)
        steps[st_]()
    for j in range(NTILE):
        P.dma("sync", y[j * 128:(j + 1) * 128, :], X[j].t[:], reads=[X[j]], out=True)
    P.finish()
    return nc, P


_CACHE = {}


def kernel(x_prompt, x_sample, cache_k, cache_v, page_table,
           a_norm_g, a_w_in, a_vnorm_g, a_w_s, a_b_s, a_w_out,
           kv_norm_g, w_k, w_v, k_norm_g,
           b_norm_g, w_q, q_norm_g, lambda_q1, lambda_k1, lambda_q2, lambda_k2, subln_g, w_o,
           f_norm_g, peer_w_q, peer_keys, peer_u, peer_v):
    f = lambda a: np.ascontiguousarray(np.asarray(a, dtype=np.float32))
    xp = f(x_prompt)
    xs = f(x_sample).reshape(128, 1024)
    if "nc" not in _CACHE:
        _CACHE["nc"] = build_program()[0]
    nc = _CACHE.get("nc_override", _CACHE["nc"])
    lam_in = np.ascontiguousarray(np.stack([f(lambda_q1), f(lambda_k1), f(lambda_q2), f(lambda_k2)], axis=1).reshape(2, 256))
    shared = {
        "cache_k": f(cache_k).reshape(2560 * 128, 1024), "cache_v": f(cache_v).reshape(2560 * 128, 1024),
        "a_norm_g": f(a_norm_g), "a_w_in": f(a_w_in), "a_vnorm_g": f(a_vnorm_g), "a_w_s": f(a_w_s), "a_b_s": f(a_b_s),
        "a_w_out": f(a_w_out), "kv_norm_g": f(kv_norm_g).reshape(1, 1024), "w_k": f(w_k), "w_v": f(w_v),
        "k_norm_g": f(k_norm_g).reshape(1, 64), "b_norm_g": f(b_norm_g), "w_q": f(w_q), "q_norm_g": f(q_norm_g),
        "lam_in": lam_in, "subln_g": f(subln_g), "w_o": f(w_o), "f_norm_g": f(f_norm_g), "peer_w_q": f(peer_w_q),
        "peer_keys": f(peer_keys).reshape(4, 16, 128, 128), "peer_u": f(peer_u).reshape(4 * N_EXP, 1024),
        "peer_v": f(peer_v).reshape(4 * N_EXP, 1024),
    }
    ptab = np.ascontiguousarray(np.asarray(page_table, dtype=np.int32))
    kk = np.arange(128)[:, None]
    qq = np.arange(128)[None, :]
    causal = np.where(kk <= qq, 0.0, -30000.0).astype(np.float32)
    in_maps = []
    for c in range(8):
        b, hf = c // 2, c % 2
        xin = np.zeros((NALL * 128, 1024), np.float32)
        xin[:NPT * 128] = xp[b].reshape(16, 2, 128, 1024)[:, hf].reshape(NPT * 128, 1024)
        xin[NPT * 128:NPT * 128 + 16] = xs[c * 16:(c + 1) * 16]
        xin[NTILE * 128:] = xp[b].reshape(16, 2, 128, 1024)[:, 1 - hf].reshape(NPT * 128, 1024)
        msk = np.zeros((128, 2, 128), np.float32)
        msk[:, 0, :] = causal
        msk[:, 1, :] = -30000.0 if hf == 0 else 0.0
        m = dict(shared)
        m["xin"] = xin
        m["pt"] = ptab[c * 16:(c + 1) * 16].reshape(1, 256)
        m["maskadd"] = msk
        m["hfcol"] = np.full((128, 1), 128.0 * (1 - 2 * hf), np.float32)
        in_maps.append(m)
    res = run_bass_kernel_spmd(nc, in_maps, core_ids=list(range(8)))
    R = res.results
    y_prompt = np.zeros((4, 4096, 1024), np.float32)
    nk_p = np.zeros((4, 4096, 1024), np.float32)
    nv_p = np.zeros((4, 4096, 1024), np.float32)
    y_sample = np.zeros((128, 1024), np.float32)
    nk_s = np.zeros((128, 1024), np.float32)
    nv_s = np.zeros((128, 1024), np.float32)
    gv = np.zeros((2, 128, 2048), np.float32)
    for c in range(8):
        b, hf = c // 2, c % 2
        r = R[c]
        y_prompt[b].reshape(16, 2, 128, 1024)[:, hf] = r["y"][:NPT * 128].reshape(16, 128, 1024)
        nk_p[b].reshape(16, 2, 128, 1024)[:, hf] = r["nkp"].reshape(16, 128, 1024)
        nv_p[b].reshape(16, 2, 128, 1024)[:, hf] = r["nvp"].reshape(16, 128, 1024)
        y_sample[c * 16:(c + 1) * 16] = r["y"][NPT * 128:NPT * 128 + 16]
        nk_s[c * 16:(c + 1) * 16] = r["nks"]
        nv_s[c * 16:(c + 1) * 16] = r["nvs"]
        gv[:, c * 16:(c + 1) * 16] = r["gv"]
    return (y_prompt, y_sample.reshape(128, 1, 1024), nk_p.reshape(4, 4096, 16, 64), nv_p.reshape(4, 4096, 8, 128),
            nk_s.reshape(128, 1, 16, 64), nv_s.reshape(128, 1, 8, 128), gv.reshape(2, 128, 1, 2048))
```
